# Optimizing a Trainium2 kernel written in Bass

```python
import jax, jax.numpy as jnp
from jax import lax
import numpy as np

D_MODEL = 1024
BATCH = 32
SEQ = 2048
DEPTH = 2

GRID_W = 64
CTX_LEN = 256
N_MIXERS = 2
N_HGRN = (DEPTH + N_MIXERS - 1) // N_MIXERS
N_CMLP = DEPTH // N_MIXERS
HG_HEADS = 8
HG_DK = D_MODEL // HG_HEADS
HG_DV = D_MODEL // HG_HEADS
HG_CHUNK = 64
CM_CHUNK = 128
CM_GROUPS = 8
CM_INNER = 3 * D_MODEL
FFN_HIDDEN = ((8 * D_MODEL // 3 + 127) // 128) * 128
DN_ALPHA = (2 * DEPTH) ** 0.25
DN_BETA = (8 * DEPTH) ** -0.25
LN_EPS = 1e-5
RMS_EPS = 1e-6
N_MOD = 9

kernel_name = "hybrid_hgrn2_chunkmlp_flow_block"


def layer_norm(x, g, b):
    xf = x.astype(jnp.float32)
    mu = jnp.mean(xf, axis=-1, keepdims=True)
    var = jnp.mean(jnp.square(xf - mu), axis=-1, keepdims=True)
    return ((xf - mu) * lax.rsqrt(var + LN_EPS) * g + b).astype(x.dtype)


def modulation(cond, w_mod, b_mod):
    m = jax.nn.silu(cond) @ w_mod + b_mod
    return m.reshape(m.shape[:-1] + (N_MOD, D_MODEL))


def modulate(x, m, k):
    return x * (1 + m[..., 3 * k + 1, :]) + m[..., 3 * k, :]


def post_norm_update(x, y, m, k, g, b):
    return layer_norm(DN_ALPHA * x + m[..., 3 * k + 2, :] * y, g, b)


def swiglu(h, w_in, w_out):
    gate, up = jnp.split(h @ w_in, 2, axis=-1)
    return (jax.nn.silu(gate) * up) @ w_out


def macaron_half_ffn(x, m, k, w_in, w_out, g, b):
    return post_norm_update(x, 0.5 * swiglu(modulate(x, m, k), w_in, w_out), m, k, g, b)


def gla_chunkwise(q, k, v, logf, s0):
    bsz, nh, length, dk = q.shape
    dv = v.shape[-1]
    n = length // HG_CHUNK

    def blocks(t):
        return t.reshape(bsz, nh, n, HG_CHUNK, t.shape[-1]).transpose(2, 0, 1, 3, 4)

    q, k, v, logf = blocks(q), blocks(k), blocks(v), blocks(logf)
    b = jnp.cumsum(logf, axis=-2)
    b_last = b[..., -1:, :]
    q_in = q * jnp.exp(b)
    k_in = k * jnp.exp(-b)
    k_out = k * jnp.exp(b_last - b)
    lower = jnp.tril(jnp.ones((HG_CHUNK, HG_CHUNK), dtype=bool))
    scores = jnp.where(lower, jnp.einsum('nbhtd,nbhsd->nbhts', q_in, k_in), 0.0)
    o_intra = jnp.einsum('nbhts,nbhse->nbhte', scores, v)
    kv_chunk = jnp.einsum('nbhsd,nbhse->nbhde', k_out, v)
    decay_chunk = jnp.exp(b_last[..., 0, :])

    def step(state, xs):
        q_c, dec_c, kv_c = xs
        o_c = jnp.einsum('bhtd,bhde->bhte', q_c, state)
        return dec_c[..., None] * state + kv_c, o_c

    s_final, o_inter = lax.scan(step, s0, (q_in, decay_chunk, kv_chunk))
    o = (o_intra + o_inter).transpose(1, 2, 0, 3, 4).reshape(bsz, nh, length, dv)
    return o, s_final


def hgrn2_project(h, w_in, lb):
    bsz, length, _ = h.shape
    p = (h @ w_in).astype(jnp.float32)
    p = p.reshape(bsz, length, 5, HG_HEADS, HG_DK).transpose(2, 0, 3, 1, 4)
    q = jax.nn.silu(p[0])
    i = p[1]
    lbb = lb[:, None]
    f = lbb + (1 - lbb) * jax.nn.sigmoid(p[2:4])
    return q, i, 1 - f, jnp.log(f), p[4]


def hgrn2_readout(o, g, norm_w, w_out):
    bsz, nh, length, dv = o.shape
    o = o * lax.rsqrt(jnp.mean(o * o, axis=-1, keepdims=True) + RMS_EPS) * norm_w
    o = o * jax.nn.silu(g)
    return o.transpose(0, 2, 1, 3).reshape(bsz, length, nh * dv).astype(w_out.dtype) @ w_out


def hgrn2_mixer(hx, hc, w_in, lb, norm_w, w_out, ctx_out):
    qx, ix, kx, lfx, gx = hgrn2_project(hx, w_in, lb)
    qc, ic, kc, lfc, gc = hgrn2_project(hc, w_in, lb)
    s0 = jnp.zeros((hc.shape[0], HG_HEADS, HG_DK, HG_DV), jnp.float32)
    rev = lambda t: jnp.flip(t, axis=2)
    oc_f, sc_f = gla_chunkwise(qc, kc[0], ic, lfc[0], s0)
    oc_b, sc_b = gla_chunkwise(rev(qc), rev(kc[1]), rev(ic), rev(lfc[1]), s0)
    ox_f, _ = gla_chunkwise(qx, kx[0], ix, lfx[0], sc_f)
    ox_b, _ = gla_chunkwise(rev(qx), rev(kx[1]), rev(ix), rev(lfx[1]), sc_b)
    yx = hgrn2_readout(ox_f + rev(ox_b), gx, norm_w, w_out)
    yc = hgrn2_readout(oc_f + rev(oc_b), gc, norm_w, w_out) if ctx_out else None
    return yx, yc


def chunk_mlp(h, n_chunks, w_in, v_g, v_b, w_s, b_s, w_out):
    bsz, length, _ = h.shape
    u, v = jnp.split(jax.nn.gelu(h @ w_in), 2, axis=-1)
    v = layer_norm(v, v_g, v_b)
    v = v.reshape(bsz, n_chunks, CM_CHUNK, CM_GROUPS, CM_INNER // CM_GROUPS)
    v = jnp.einsum('gts,bnsgc->bntgc', w_s, v) + b_s.T[None, None, :, :, None]
    return (u * v.reshape(bsz, length, CM_INNER)) @ w_out


def setup_inputs(seed: int = 0) -> dict:
    key = jax.random.key(seed)
    ks = jax.random.split(key, 24)
    nrm = lambda k, shape, s: jax.random.normal(k, shape, jnp.float32) * s
    d = D_MODEL
    return {
        "x": nrm(ks[0], (BATCH, SEQ, d), 1.0),
        "c": nrm(ks[1], (BATCH, d), 1.0),
        "ctx": nrm(ks[2], (BATCH, CTX_LEN, d), 1.0),
        "c_ctx": nrm(ks[3], (d,), 1.0),
        "mod_w": nrm(ks[4], (DEPTH, d, N_MOD * d), 0.5 * d ** -0.5),
        "mod_b": nrm(ks[5], (DEPTH, N_MOD * d), 0.02),
        "ln_g": 1.0 + nrm(ks[6], (DEPTH, 3, d), 0.02),
        "ln_b": nrm(ks[7], (DEPTH, 3, d), 0.02),
        "ffn_w_in": nrm(ks[8], (DEPTH, 2, d, 2 * FFN_HIDDEN), d ** -0.5),
        "ffn_w_out": nrm(ks[9], (DEPTH, 2, FFN_HIDDEN, d), DN_BETA * FFN_HIDDEN ** -0.5),
        "hg_w_in": nrm(ks[10], (N_HGRN, d, 5 * d), d ** -0.5),
        "hg_lower_bounds": nrm(ks[11], (2, DEPTH + 1, d), 0.1),
        "hg_norm_w": 1.0 + nrm(ks[12], (N_HGRN, HG_DV), 0.02),
        "hg_w_out": nrm(ks[13], (N_HGRN, d, d), DN_BETA * d ** -0.5),
        "cm_w_in": nrm(ks[14], (N_CMLP, d, 2 * CM_INNER), d ** -0.5),
        "cm_v_g": 1.0 + nrm(ks[15], (N_CMLP, CM_INNER), 0.02),
        "cm_v_b": nrm(ks[16], (N_CMLP, CM_INNER), 0.02),
        "cm_w_s": nrm(ks[17], (N_CMLP, CM_GROUPS, CM_CHUNK, CM_CHUNK), CM_CHUNK ** -0.5),
        "cm_b_s": 1.0 + nrm(ks[18], (N_CMLP, CM_GROUPS, CM_CHUNK), 0.02),
        "cm_w_out": nrm(ks[19], (N_CMLP, CM_INNER, d), DN_BETA * CM_INNER ** -0.5),
    }


def reference(x, c, ctx, c_ctx, mod_w, mod_b, ln_g, ln_b, ffn_w_in, ffn_w_out,
              hg_w_in, hg_lower_bounds, hg_norm_w, hg_w_out,
              cm_w_in, cm_v_g, cm_v_b, cm_w_s, cm_b_s, cm_w_out):
    rows = x.shape[1] // GRID_W
    n_lat_chunks = rows // (CM_CHUNK // GRID_W)
    n_ctx_chunks = ctx.shape[1] // CM_CHUNK
    lb_all = jnp.cumsum(jax.nn.softmax(hg_lower_bounds.astype(jnp.float32), axis=1), axis=1)
    for i in range(DEPTH):
        last = i == DEPTH - 1
        kind = i % N_MIXERS
        j = i // N_MIXERS
        ctx_needed = (not last) or kind == 0
        mx = modulation(c, mod_w[i], mod_b[i])[:, None]
        x = macaron_half_ffn(x, mx, 0, ffn_w_in[i, 0], ffn_w_out[i, 0], ln_g[i, 0], ln_b[i, 0])
        if ctx_needed:
            mc = modulation(c_ctx, mod_w[i], mod_b[i])
            ctx = macaron_half_ffn(ctx, mc, 0, ffn_w_in[i, 0], ffn_w_out[i, 0], ln_g[i, 0], ln_b[i, 0])
        hx = modulate(x, mx, 1)
        if kind == 0:
            lb = lb_all[:, i].reshape(2, HG_HEADS, 1, HG_DK)
            yx, yc = hgrn2_mixer(hx, modulate(ctx, mc, 1), hg_w_in[j], lb, hg_norm_w[j], hg_w_out[j],
                                 ctx_out=not last)
        else:
            yx = chunk_mlp(hx, n_lat_chunks, cm_w_in[j], cm_v_g[j], cm_v_b[j], cm_w_s[j], cm_b_s[j], cm_w_out[j])
            yc = None if last else chunk_mlp(modulate(ctx, mc, 1), n_ctx_chunks, cm_w_in[j], cm_v_g[j],
                                             cm_v_b[j], cm_w_s[j], cm_b_s[j], cm_w_out[j])
        x = post_norm_update(x, yx, mx, 1, ln_g[i, 1], ln_b[i, 1])
        x = macaron_half_ffn(x, mx, 2, ffn_w_in[i, 1], ffn_w_out[i, 1], ln_g[i, 2], ln_b[i, 2])
        if not last:
            ctx = post_norm_update(ctx, yc, mc, 1, ln_g[i, 1], ln_b[i, 1])
            ctx = macaron_half_ffn(ctx, mc, 2, ffn_w_in[i, 1], ffn_w_out[i, 1], ln_g[i, 2], ln_b[i, 2])
    return x
```

```python
import numpy as np
from contextlib import ExitStack
import concourse.bass as bass
import concourse.mybir as mybir
from concourse.bass_utils import run_bass_kernel_spmd

F32 = mybir.dt.float32
BF16 = mybir.dt.bfloat16
AF = mybir.ActivationFunctionType
ALU = mybir.AluOpType

D = 1024
NCH = 8
FH = 2816
NJ = 22
NHEAD = 8
CMI = 3072
NFC = 24
DEPTH = 2
DN_ALPHA = (2 * DEPTH) ** 0.25
LN_EPS = 1e-5
RMS_EPS = 1e-6
EPOCH = 30000
import os
DBG_SKIP_BWD_SUB = bool(int(os.environ.get('DBG_SKIP_BWD_SUB', '0')))


class Tok:
    __slots__ = ("grp", "ep", "val", "sem", "own", "inc")

    def __init__(self, grp):
        self.grp = grp
        self.ep = 0
        self.val = None
        self.sem = None
        self.own = False
        self.inc = 1


class Rec:
    __slots__ = ("fn", "deps", "tok", "scope")


class Prog:
    ENG = ("pe", "act", "dve", "pool", "sp")

    def __init__(self, nc, es):
        self.nc = nc
        self.es = es
        self.streams = {e: [] for e in self.ENG}
        self.cnt = {}
        self.cur = {}
        self.pending = {e: [] for e in self.ENG}
        self.lastw = {}
        self.readers = {}
        self.fence_deps = []
        self.latest = {}
        self.nsem = 0
        self.scope = None
        self.use_scopes = False
        self.nofence = set()

    def _newsem(self):
        self.nsem += 1
        return self.es.enter_context(self.nc.semaphore(f"s{self.nsem}"))

    def _signal(self, grp, inc):
        if grp not in self.cur or self.cnt[grp] + inc > EPOCH:
            ep = self.cur[grp][0] + 1 if grp in self.cur else 0
            self.cur[grp] = (ep, self._newsem())
            self.cnt[grp] = 0
        self.cnt[grp] += inc
        t = Tok(grp)
        t.ep, t.sem = self.cur[grp]
        t.val = self.cnt[grp]
        t.own = True
        t.inc = inc
        return t

    def _record(self, eng, fn, reads, writes, tok_grp, inc, signal):
        if getattr(self, "dead", False):
            return None
        deps = list(self.fence_deps)
        for k in reads:
            t = self.lastw.get(k)
            if t is not None:
                deps.append(t)
        for k in writes:
            t = self.lastw.get(k)
            if t is not None:
                deps.append(t)
            deps.extend(self.readers.get(k, {}).values())
        for t in deps:
            if t.sem is None and not (eng == "pe" and t.grp == "pe"):
                raise RuntimeError(f"dependency on unsignaled op ({t.grp}) from {eng}")
        r = Rec()
        r.fn = fn
        r.deps = deps
        r.scope = self.scope
        if signal:
            tok = self._signal(tok_grp, inc)
            if tok_grp == eng:
                for p in self.pending[eng]:
                    p.ep, p.sem, p.val = tok.ep, tok.sem, tok.val
                self.pending[eng] = []
        else:
            tok = Tok(tok_grp)
            self.pending[eng].append(tok)
        r.tok = tok
        for k in reads:
            self.readers.setdefault(k, {})[tok.grp] = tok
        for k in writes:
            self.lastw[k] = tok
            self.readers[k] = {}
        self.latest[tok.grp] = tok
        self.streams[eng].append(r)
        return tok

    def op(self, eng, fn, reads=(), writes=(), signal=True):
        return self._record(eng, fn, reads, writes, eng, 1, signal)

    def dma(self, fn, slot, reads=(), writes=(), queue="sp"):
        return self._record(queue, fn, reads, writes, ("dma", slot), 16, True)

    def fence(self, all_groups=False):
        for e in self.ENG:
            if self.pending[e]:
                raise RuntimeError(f"fence with pending unsignaled ops on {e}")
        self.fence_deps = [t for g, t in self.latest.items() if all_groups or g not in self.nofence]

    def finish(self):
        self.fence(all_groups=True)
        r = Rec()
        r.fn = None
        r.deps = list(self.fence_deps)
        r.tok = None
        r.scope = None
        self.streams["sp"].append(r)

    def emit(self):
        nc = self.nc
        with nc.Block() as block:
            decos = {"pe": block.tensor, "act": block.scalar, "dve": block.vector,
                     "pool": block.gpsimd, "sp": block.sync}
            for name in self.ENG:
                def body(eng, name=name):
                    waited = {}
                    for r in self.streams[name]:
                        best = {}
                        for t in r.deps:
                            if name == "pe" and t.grp == "pe":
                                continue
                            b = best.get(t.grp)
                            if b is None or (t.ep, t.val) > (b.ep, b.val):
                                best[t.grp] = t
                        for t in best.values():
                            cur = waited.get(t.grp, (-1, -1))
                            if (t.ep, t.val) <= cur:
                                continue
                            eng.wait_ge(t.sem, t.val)
                            waited[t.grp] = (t.ep, t.val)
                        if r.fn is not None:
                            if self.use_scopes and r.scope is not None:
                                with nc.named_scope(r.scope):
                                    inst = r.fn(eng)
                            else:
                                inst = r.fn(eng)
                            if r.tok.own:
                                inst.then_inc(r.tok.sem, r.tok.inc)
                decos[name](body)


def build_nc(NSEQ, T, CTX, stages, dbg=False, scopes=False):
    nc = bass.Bass("TRN2", target_bir_lowering=False)
    NB = NSEQ + 1
    dt = nc.dram_tensor

    def din(name, shape, dtype=F32):
        return dt(name, list(shape), dtype, kind="ExternalInput").ap()

    x_d = din("x", (NSEQ, T, D))
    ctx_d = din("ctx", (NSEQ, CTX, D))
    cc_d = din("cc", (NB * NCH, 128))
    mod_w_d = din("mod_w", (DEPTH, D, 9 * D))
    mod_b_d = din("mod_b", (DEPTH * 72, 128))
    ln_g_d = din("ln_g", (48, 128))
    ln_b_d = din("ln_b", (48, 128))
    ffn_w_in_d = din("ffn_w_in", (DEPTH, 2, D, 2 * FH))
    ffn_w_out_d = din("ffn_w_out", (DEPTH, 2, FH, D))
    ident_d = din("ident", (128, 128))
    TT = T + CTX
    hg_w_in_d = din("hg_w_in", (D, 5 * D))
    hg_w_out_d = din("hg_w_out", (D, D))
    hg_lb_d = din("hg_lb", (48, 128))
    hg_nw_d = din("hg_nw", (128, 1))
    cm_w_in_d = din("cm_w_in", (D, 2 * CMI))
    cm_w_out_d = din("cm_w_out", (CMI, D))
    cm_vg_d = din("cm_vg", (NFC, 128))
    cm_vb_d = din("cm_vb", (NFC, 128))
    cm_ws_d = din("cm_ws", (8, 128, 128))
    cm_bs_d = din("cm_bs", (8, 128))
    maskf_d = din("maskf", (128, 64))
    maskb_d = din("maskb", (128, 64))
    rmask_d = din("rmask", (128, TT))
    whb_d = dt("whb", [NHEAD, 128, NCH, 640], BF16, kind="Internal").ap()
    whob_d = dt("whob", [NHEAD, 128, D], BF16, kind="Internal").ap()
    wub_d = dt("wub", [NFC, 128, NCH, 128], BF16, kind="Internal").ap()
    wvb_d = dt("wvb", [6, 128, NCH, 512], BF16, kind="Internal").ap()
    wcob_d = dt("wcob", [NCH, 128, NFC, 128], BF16, kind="Internal").ap()
    r_d = dt("r_d", [128, NFC * 128], F32, kind="Internal").ap()
    out_d = dt("out", [NSEQ, T, D], F32, kind="ExternalOutput").ap()
    dbg_d = dt("dbgout", [3, 128, TT], F32, kind="ExternalOutput").ap() if dbg == 77 else None
    dbgb_d = dt("dbgb", [3, 128, TT], BF16, kind="ExternalOutput").ap() if dbg == 78 else None

    winb_d = [[dt(f"winb{l}{s}", [NJ, 128, NCH, 256], BF16, kind="Internal").ap() for s in range(2)] for l in range(DEPTH)]
    woutb_d = [[dt(f"woutb{l}{s}", [NCH, 128, NJ, 128], BF16, kind="Internal").ap() for s in range(2)] for l in range(DEPTH)]

    es = ExitStack()
    with es:
        P = Prog(nc, es)
        P.use_scopes = scopes
        P.scope = "setup"
        sb = lambda name, shape, dtype: es.enter_context(nc.sbuf_tensor(name, list(shape), dtype))
        ps = lambda name: es.enter_context(nc.psum_tensor(name, [128, 512], F32))

        x_sb = sb("x_sb", (128, NCH, T), F32)
        ctx_sb = sb("ctx_sb", (128, NCH, CTX), F32)
        ARENA = 88064
        arena = sb("arena", (128, ARENA // 2), BF16)
        stat = sb("stat", (128, 5, 512), F32)
        sgb = sb("sgb", (128, 2, 512), F32)
        tb = sb("tb", (128, 2, 512), F32)
        ident_f = sb("ident_f", (128, 128), F32)
        ident_b = sb("ident_b", (128, 128), BF16)
        ones_b = sb("ones_b", (128, 128), BF16)
        small_in = sb("small_in", (128, 128), F32)
        scT = sb("scT", (128, NCH, 8), F32)
        modT = [sb(f"modT{l}", (128, 72, 8), F32) for l in range(DEPTH)]
        modbT = sb("modbT", (128, DEPTH * 72), F32)
        lngT = sb("lngT", (128, 48), F32)
        lnbT = sb("lnbT", (128, 48), F32)
        epsc = sb("epsc", (128, 4), F32)
        ones_h = sb("ones_h", (128, 128), BF16)
        maskf = sb("maskf_s", (128, 64), F32)
        maskb = sb("maskb_s", (128, 64), F32)
        rmask = sb("rmask_s", (128, TT), BF16)
        lbT = sb("lbT", (128, 48), F32)
        lbt = sb("lbt", (128, 2, 8), F32)
        omlb = sb("omlb", (128, 2, 8), F32)
        nwc = sb("nwc", (128, 1), F32)
        vgT = sb("vgT", (128, NFC), F32)
        vbT = sb("vbT", (128, NFC), F32)
        wsT = sb("wsT", (128, 8, 128), BF16)
        cmst = sb("cmst", (128, 160), F32)

        def aview(off, shape, dtype):
            nbytes = int(np.prod(shape)) * (4 if dtype == F32 else 2)
            assert off % 4 == 0 and off + nbytes <= ARENA, (off, nbytes)
            v = arena[:, off // 2:(off + nbytes) // 2]
            if dtype == F32:
                v = v.bitcast(F32)
            if len(shape) == 1:
                return v
            if len(shape) == 2:
                return v.rearrange("p (a b) -> p a b", a=shape[0])
            if len(shape) == 3:
                return v.rearrange("p (a b c) -> p a b c", a=shape[0], b=shape[1])
            return v

        hmod = aview(0, (NCH, 1024), BF16)
        hid = aview(16384, (NJ, 1024), BF16)
        win = aview(61440, (3, NCH, 256), BF16)
        wout = aview(73728, (2, NJ, 128), BF16)
        st32 = aview(16384, (2, 5632), F32)
        st16 = aview(61440, (2, 5632), BF16)
        iost = aview(16384, (4, D), F32)

        pg = [ps("pg0"), ps("pg1")]
        pu = [ps("pu0"), ps("pu1")]
        py = [ps("py0"), ps("py1")]
        pst = [ps("pst0"), ps("pst1")]

        P.dma(lambda e: e.dma_start(out=ident_f[:, :], in_=ident_d[:, :]), "ident", writes=["ident_f"])
        P.op("dve", lambda e: e.tensor_copy(ident_b[:, :], ident_f[:, :]), reads=["ident_f"], writes=["ident_b"])
        P.op("pool", lambda e: e.memset(ones_b[:, :], 1.0 / D), writes=["ones_b"])
        P.op("pool", lambda e: e.memset(epsc[:, 0:1], LN_EPS / (DN_ALPHA ** 2)), writes=["epsc"])
        P.op("pool", lambda e: e.memset(epsc[:, 1:2], LN_EPS), writes=["epsc"])
        P.op("pool", lambda e: e.memset(epsc[:, 2:3], RMS_EPS), writes=["epsc"])
        P.op("pool", lambda e: e.memset(scT[:, :, :], 0.0), writes=["scT"])

        wkeys = {}

        def cast_dma(out_ap, in_ap, slot, key):
            P.nofence.add(("dma", slot))
            P.dma(lambda e: e.dma_start(out=out_ap, in_=in_ap), slot, writes=[key], queue="pool")

        def conv_ffn(l, s):
            ki, ko = [], []
            for kc in range(NCH):
                for half in range(2):
                    key = ("winb", l, s, kc, half)
                    ki.append(key)
                    cast_dma(winb_d[l][s][:, :, kc, half * 128:(half + 1) * 128].rearrange("j p n -> p j n"),
                             ffn_w_in_d[l, s, kc * 128:(kc + 1) * 128, half * FH:(half + 1) * FH].rearrange("p (j n) -> p j n", n=128),
                             ("cw_in", l, s), key)
            for j in range(NJ):
                key = ("woutb", l, s, j)
                ko.append(key)
                cast_dma(woutb_d[l][s][:, :, j, :].rearrange("c p n -> p c n"),
                         ffn_w_out_d[l, s, j * 128:(j + 1) * 128, :].rearrange("p (c n) -> p c n", n=128),
                         ("cw_out", l, s), key)
            wkeys[("winb", l, s)] = ki
            wkeys[("woutb", l, s)] = ko

        def conv_hg():
            ki = []
            for kc in range(NCH):
                for sel in range(5):
                    key = ("whb", kc, sel)
                    ki.append(key)
                    cast_dma(whb_d[:, :, kc, sel * 128:(sel + 1) * 128].rearrange("h p n -> p h n"),
                             hg_w_in_d[kc * 128:(kc + 1) * 128, sel * D:(sel + 1) * D].rearrange("p (h n) -> p h n", n=128),
                             "cw_hgi", key)
            wkeys["whb"] = ki
            cast_dma(whob_d.rearrange("h p n -> p h n"), hg_w_out_d.rearrange("(h p) n -> p h n", p=128), "cw_hgo", "whob")
            wkeys["whob"] = ["whob"]

        def conv_cm():
            ku, kv, ko = [], [], []
            for kc in range(NCH):
                key = ("wub", kc)
                ku.append(key)
                cast_dma(wub_d[:, :, kc, :].rearrange("f p n -> p f n"),
                         cm_w_in_d[kc * 128:(kc + 1) * 128, 0:CMI].rearrange("p (f n) -> p f n", n=128), "cw_cmu", key)
                key = ("wvb", kc)
                kv.append(key)
                cast_dma(wvb_d[:, :, kc, :].rearrange("b p n -> p b n"),
                         cm_w_in_d[kc * 128:(kc + 1) * 128, CMI:2 * CMI].rearrange("p (b n) -> p b n", n=512), "cw_cmv", key)
            for fc in range(NFC):
                key = ("wcob", fc)
                ko.append(key)
                cast_dma(wcob_d[:, :, fc, :].rearrange("c p n -> p c n"),
                         cm_w_out_d[fc * 128:(fc + 1) * 128, :].rearrange("p (c n) -> p c n", n=128), "cw_cmo", key)
            wkeys["wub"], wkeys["wvb"], wkeys["wcob"] = ku, kv, ko

        conv_ffn(0, 0)

        bg_jobs = []

        def conv_rest():
            if stages >= 2:
                conv_hg()
            if stages >= 3:
                bg_jobs.append(lambda: conv_ffn(0, 1))
            if stages >= 4:
                bg_jobs.append(lambda: conv_ffn(1, 0))
            if stages >= 5:
                bg_jobs.append(conv_cm)
            if stages >= 6:
                bg_jobs.append(lambda: conv_ffn(1, 1))

        def run_bg_job():
            if bg_jobs:
                bg_jobs.pop(0)()

        def load_T(dst, src_rows, nrows, func=None, key=None, srcview=None):
            P.dma(lambda e: e.dma_start(out=small_in[0:nrows, :], in_=src_rows), "small_in", writes=["small_in"])
            if func is not None:
                P.op("act", lambda e: e.activation(out=small_in[0:nrows, :], in_=small_in[0:nrows, :], func=func),
                     reads=["small_in"], writes=["small_in"])
            P.op("pe", lambda e: e.transpose(pst[0][:, 0:nrows], small_in[0:nrows, :], ident_f[0:nrows, 0:nrows]),
                 reads=["small_in", "ident_f"], writes=["pst0"])
            src = pst[0][:, 0:nrows] if srcview is None else srcview(pst[0][:, 0:nrows])
            P.op("dve", lambda e: e.tensor_copy(dst, src), reads=["pst0"], writes=[key])

        load_T(scT[:, :, 0:NB].rearrange("p k n -> p n k"), cc_d[:, :], NB * NCH, func=AF.Silu, key="scT",
               srcview=lambda a: a.rearrange("p (n k) -> p n k", k=NCH))
        load_T(modbT[:, 0:72], mod_b_d[0:72, :], 72, key="modbT")
        load_T(modbT[:, 72:144], mod_b_d[72:144, :], 72, key="modbT")
        load_T(lngT[:, :], ln_g_d[:, :], 48, key="lngT")
        load_T(lnbT[:, :], ln_b_d[:, :], 48, key="lnbT")

        P.op("pool", lambda e: e.memset(ones_h[:, :], 1.0 / 128.0), writes=["ones_h"])
        P.dma(lambda e: e.dma_start(out=maskf[:, :], in_=maskf_d[:, :]), "maskf", writes=["maskf"])
        P.dma(lambda e: e.dma_start(out=maskb[:, :], in_=maskb_d[:, :]), "maskb", writes=["maskb"])
        P.dma(lambda e: e.dma_start(out=nwc[:, :], in_=hg_nw_d[:, :]), "nwc", writes=["nwc"])
        rm32 = aview(0, (TT,), F32)
        P.dma(lambda e: e.dma_start(out=rm32, in_=rmask_d[:, :]), "rm32", writes=["rm32"])
        P.op("dve", lambda e: e.tensor_copy(rmask[:, :], rm32), reads=["rm32"], writes=["rmask"])
        if dbg == 10:
            P.finish(); P.emit(); return nc
        load_T(lbT[:, :], hg_lb_d[:, :], 48, key="lbT")
        load_T(vgT[:, :], cm_vg_d[:, :], NFC, key="vgT")
        load_T(vbT[:, :], cm_vb_d[:, :], NFC, key="vbT")
        P.op("act", lambda e: e.activation(out=lbT[:, :], in_=lbT[:, :], func=AF.Exp), reads=["lbT"], writes=["lbT"])
        for dr in range(2):
            e0, e1, e2 = (lbT[:, (dr * 3 + i) * 8:(dr * 3 + i + 1) * 8] for i in range(3))
            P.op("dve", lambda e, dr=dr, e0=e0, e1=e1: e.tensor_tensor(out=omlb[:, dr, :], in0=e0, in1=e1, op=ALU.add),
                 reads=["lbT"], writes=["omlb"])
            P.op("dve", lambda e, dr=dr, e2=e2: e.tensor_tensor(out=omlb[:, dr, :], in0=omlb[:, dr, :], in1=e2, op=ALU.add),
                 reads=["lbT", "omlb"], writes=["omlb"])
            P.op("dve", lambda e, dr=dr: e.reciprocal(omlb[:, dr, :], omlb[:, dr, :]), reads=["omlb"], writes=["omlb"])
            P.op("dve", lambda e, dr=dr, e0=e0: e.tensor_tensor(out=lbt[:, dr, :], in0=omlb[:, dr, :], in1=e0, op=ALU.mult),
                 reads=["lbT", "omlb"], writes=["lbt"])
            P.op("dve", lambda e, dr=dr: e.tensor_scalar(out=omlb[:, dr, :], in0=lbt[:, dr, :], scalar1=-1.0, scalar2=1.0,
                                                        op0=ALU.mult, op1=ALU.add), reads=["lbt"], writes=["omlb"])
        if dbg == 11:
            P.finish(); P.emit(); return nc
        bsrep = aview(16384, (8, 128), F32)
        rsrep = aview(20480, (8, 128), F32)
        rtab = aview(24576, (NFC, 128), F32)
        onesf = aview(40960, (128,), F32)
        P.op("pool", lambda e: e.memset(onesf, 1.0), writes=["onesf"])
        P.dma(lambda e: e.dma_start(out=bsrep, in_=cm_bs_d.partition_broadcast(128)), "bsrep", writes=["bsrep"])
        if dbg == 12:
            P.finish(); P.emit(); return nc
        for g in range(8):
            P.dma(lambda e, g=g: e.dma_start(out=small_in[:, :], in_=cm_ws_d[g]), "small_in", writes=["small_in"])
            P.op("pe", lambda e: e.transpose(pst[0][:, 0:128], small_in[:, :], ident_f[:, :]),
                 reads=["small_in", "ident_f"], writes=["pst0"])
            P.op("dve", lambda e, g=g: e.tensor_copy(wsT[:, g, :], pst[0][:, 0:128]), reads=["pst0"], writes=["wsT"])
            P.op("pe", lambda e, g=g: e.matmul(pst[1][:, 0:128], lhsT=ones_h[:, :], rhs=wsT[:, g, :], start=True, stop=True),
                 reads=["wsT", "ones_h"], writes=["pst1"])
            P.op("dve", lambda e, g=g: e.tensor_scalar_mul(rsrep[:, g, :], pst[1][:, 0:128], 128.0), reads=["pst1"], writes=["rsrep"])
        if dbg == 13:
            P.finish(); P.emit(); return nc
        for fc in range(NFC):
            P.op("dve", lambda e, fc=fc: e.scalar_tensor_tensor(
                out=rtab[:, fc, :], in0=rsrep[:, fc // 3, :], scalar=vbT[:, fc:fc + 1], in1=bsrep[:, fc // 3, :],
                op0=ALU.mult, op1=ALU.add), reads=["rsrep", "bsrep", "vbT"], writes=["rtab"])
        P.dma(lambda e: e.dma_start(out=r_d[:, :], in_=rtab.rearrange("p a b -> p (a b)")), "rtab", reads=["rtab"], writes=["r_d"])
        P.fence()
        if dbg == 1:
            P.finish(); P.emit(); return nc
        mw32 = aview(16384, (2, 4608), F32)
        mps = [pg[0], pg[1]]
        cnt = 0
        for l in range(DEPTH):
            for kc in range(NCH):
                for hf in range(2):
                    slot = cnt % 2
                    cnt += 1
                    P.dma(lambda e, l=l, kc=kc, hf=hf, slot=slot: e.dma_start(
                        out=mw32[:, slot, :], in_=mod_w_d[l, kc * 128:(kc + 1) * 128, hf * 4608:(hf + 1) * 4608]),
                        ("mw", slot), writes=[("mw", slot)])
                    for f in range(36):
                        P.op("pe", lambda e, kc=kc, hf=hf, slot=slot, f=f: e.matmul(
                            mps[hf][:, f * 8:(f + 1) * 8], lhsT=mw32[:, slot, f * 128:(f + 1) * 128],
                            rhs=scT[:, kc, :], start=(kc == 0 and f == 0), stop=(kc == NCH - 1),
                            skip_group_check=True),
                            reads=[("mw", slot), "scT"], writes=[("mps", hf)], signal=(f == 35))
            for hf in range(2):
                P.op("dve", lambda e, l=l, hf=hf: e.tensor_tensor(
                    out=modT[l][:, hf * 36:(hf + 1) * 36, :],
                    in0=mps[hf][:, 0:288].rearrange("p (f n) -> p f n", n=8),
                    in1=modbT[:, l * 72 + hf * 36: l * 72 + (hf + 1) * 36].unsqueeze(2).to_broadcast([128, 36, 8]),
                    op=ALU.add), reads=[("mps", hf), "modbT"], writes=[("modT", l)])
            for k in range(3):
                gf = (0.5 if k != 1 else 1.0) / DN_ALPHA
                P.op("dve", lambda e, l=l, k=k: e.tensor_scalar_add(
                    modT[l][:, (3 * k + 1) * 8:(3 * k + 2) * 8, :], modT[l][:, (3 * k + 1) * 8:(3 * k + 2) * 8, :], 1.0),
                    reads=[("modT", l)], writes=[("modT", l)])
                P.op("dve", lambda e, l=l, k=k, gf=gf: e.tensor_scalar_mul(
                    modT[l][:, (3 * k + 2) * 8:(3 * k + 3) * 8, :], modT[l][:, (3 * k + 2) * 8:(3 * k + 3) * 8, :], gf),
                    reads=[("modT", l)], writes=[("modT", l)])
        P.fence()

        if dbg == 2:
            P.finish(); P.emit(); return nc
        cast_rr = [0]

        def cast(out, in_, reads, writes):
            i = cast_rr[0] % 3
            cast_rr[0] += 1
            if i == 0:
                P.op("act", lambda e: e.copy(out, in_), reads=reads, writes=writes)
            elif i == 1:
                P.op("dve", lambda e: e.tensor_copy(out, in_), reads=reads, writes=writes)
            else:
                P.op("pool", lambda e: e.tensor_copy(out, in_), reads=reads, writes=writes)

        pp = [0]

        def pp_slot():
            s = pp[0] % 2
            pp[0] += 1
            return s

        def prep_ffn(l, s):
            for kc in range(NCH):
                sl = pp_slot()
                P.dma(lambda e, kc=kc, sl=sl: e.dma_start(out=st32[:, sl, :], in_=ffn_w_in_d[l, s, kc * 128:(kc + 1) * 128, :]),
                      ("st32", sl), writes=[("st32", sl)])
                o16 = st16[:, sl, :].rearrange("p (j n) -> p j n", n=256)
                for half in range(2):
                    cast(o16[:, :, half * 128:(half + 1) * 128],
                         st32[:, sl, half * FH:(half + 1) * FH].rearrange("p (j n) -> p j n", n=128),
                         reads=[("st32", sl)], writes=[("st16", sl, half)])
                P.dma(lambda e, kc=kc, sl=sl, o16=o16: e.dma_start(
                    out=winb_d[l][s][:, :, kc, :].rearrange("j p n -> p j n"), in_=o16),
                    ("st16", sl), reads=[("st16", sl, 0), ("st16", sl, 1)], writes=[("winb", l, s)])
            for j0 in range(0, NJ, 5):
                nj = min(5, NJ - j0)
                sl = pp_slot()
                i32 = st32[:, sl, 0:nj * D].rearrange("p (j n) -> p j n", n=D)
                i16 = st16[:, sl, 0:nj * D].rearrange("p (j n) -> p j n", n=D)
                P.dma(lambda e, j0=j0, nj=nj, i32=i32: e.dma_start(
                    out=i32, in_=ffn_w_out_d[l, s, j0 * 128:(j0 + nj) * 128, :].rearrange("(j p) n -> p j n", p=128)),
                    ("st32", sl), writes=[("st32", sl)])
                cast(i16, i32, reads=[("st32", sl)], writes=[("st16", sl, 0), ("st16", sl, 1)])
                for jj in range(nj):
                    P.dma(lambda e, j0=j0, jj=jj, i16=i16: e.dma_start(
                        out=woutb_d[l][s][:, :, j0 + jj, :].rearrange("c p n -> p c n"),
                        in_=i16[:, jj, :].rearrange("p (c n) -> p c n", n=128)),
                        ("st16", sl, jj), reads=[("st16", sl, 0), ("st16", sl, 1)], writes=[("woutb", l, s)])

        def prep_rows(src_rows, ncols, cast_fn, stores):
            sl = pp_slot()
            P.dma(lambda e: e.dma_start(out=st32[:, sl, 0:ncols], in_=src_rows), ("st32", sl), writes=[("st32", sl)])
            cast_fn(st32[:, sl, 0:ncols], st16[:, sl, 0:ncols], sl)
            for i, (dst, srcv, wkey) in enumerate(stores):
                P.dma(lambda e, dst=dst, srcv=srcv: e.dma_start(out=dst, in_=srcv(st16[:, sl, 0:ncols])),
                      ("st16", sl, i), reads=[("st16", sl, 0), ("st16", sl, 1)], writes=[wkey])

        def plain_cast(i32, i16, sl):
            cast(i16, i32, reads=[("st32", sl)], writes=[("st16", sl, 0), ("st16", sl, 1)])

        def prep_rows3(src_rows3, a, b, cast_fn, stores):
            sl = pp_slot()
            P.dma(lambda e: e.dma_start(out=st32[:, sl, 0:a * b].rearrange("p (a b) -> p a b", a=a), in_=src_rows3),
                  ("st32", sl), writes=[("st32", sl)])
            cast_fn(st32[:, sl, 0:a * b], st16[:, sl, 0:a * b], sl)
            for i, (dst, srcv, wkey) in enumerate(stores):
                P.dma(lambda e, dst=dst, srcv=srcv: e.dma_start(out=dst, in_=srcv(st16[:, sl, 0:a * b])),
                      ("st16", sl, i), reads=[("st16", sl, 0), ("st16", sl, 1)], writes=[wkey])

        def prep_hg():
            for kc in range(NCH):
                def cst(i32, i16, sl):
                    cast(i16.rearrange("p (h s n) -> p h s n", h=NHEAD, s=5),
                         i32.rearrange("p (s h n) -> p h s n", s=5, h=NHEAD),
                         reads=[("st32", sl)], writes=[("st16", sl, 0), ("st16", sl, 1)])
                prep_rows(hg_w_in_d[kc * 128:(kc + 1) * 128, :], 5 * D, cst,
                          [(whb_d[:, :, kc, :].rearrange("h p n -> p h n"),
                            lambda v: v.rearrange("p (h n) -> p h n", h=NHEAD), "whb")])
            for h0 in range(0, NHEAD, 4):
                prep_rows3(hg_w_out_d[h0 * 128:(h0 + 4) * 128, :].rearrange("(h p) n -> p h n", p=128), 4, D, plain_cast,
                           [(whob_d[h0:h0 + 4].rearrange("h p n -> p h n"),
                             lambda v: v.rearrange("p (h n) -> p h n", h=4), "whob")])

        def prep_cm():
            for kc in range(NCH):
                prep_rows(cm_w_in_d[kc * 128:(kc + 1) * 128, 0:CMI], CMI, plain_cast,
                          [(wub_d[:, :, kc, :].rearrange("f p n -> p f n"),
                            lambda v: v.rearrange("p (f n) -> p f n", n=128), "wub")])
                prep_rows(cm_w_in_d[kc * 128:(kc + 1) * 128, CMI:2 * CMI], CMI, plain_cast,
                          [(wvb_d[:, :, kc, :].rearrange("b p n -> p b n"),
                            lambda v: v.rearrange("p (b n) -> p b n", n=512), "wvb")])
            for j0 in range(0, NFC, 4):
                prep_rows3(cm_w_out_d[j0 * 128:(j0 + 4) * 128, :].rearrange("(j p) n -> p j n", p=128), 4, D, plain_cast,
                           [(wcob_d[:, :, j0 + jj, :].rearrange("c p n -> p c n"),
                             (lambda v, jj=jj: v[:, jj * D:(jj + 1) * D].rearrange("p (c n) -> p c n", n=128)), "wcob")
                            for jj in range(4)])

        zsm = sb("zsm", (128, 2048), BF16)

        def layer_norm_tile(xv, xkeys, n, gcol, bcol, eps_col, zbuf=None):
            for c in range(NCH):
                b = c % 2
                zb = zsm[:, b * 512:b * 512 + n]
                zq = zsm[:, 1024 + b * 512:1024 + b * 512 + n]
                P.op("act", lambda e, c=c, zb=zb: e.activation(out=zb, in_=xv(c), func=AF.Copy),
                     reads=[xkeys[c]], writes=[("zb", b)])
                P.op("pool", lambda e, c=c, zq=zq: e.tensor_tensor(out=zq, in0=xv(c), in1=xv(c), op=ALU.mult),
                     reads=[xkeys[c]], writes=[("zq", b)])
                P.op("pe", lambda e, c=c, zb=zb: e.matmul(pst[0][:, 0:n], lhsT=ones_b[:, :], rhs=zb, start=(c == 0), stop=(c == NCH - 1)),
                     reads=[("zb", b), "ones_b"], writes=["pst0"])
                P.op("pe", lambda e, c=c, zq=zq: e.matmul(pst[1][:, 0:n], lhsT=ones_b[:, :], rhs=zq, start=(c == 0), stop=(c == NCH - 1)),
                     reads=[("zq", b), "ones_b"], writes=["pst1"])
            var, rstd = (stat[:, i, 0:n] for i in range(2))
            m2 = tb[:, 0, 0:n]
            P.op("act", lambda e: e.activation(out=m2, in_=pst[0][:, 0:n], func=AF.Square), reads=["pst0"], writes=[("tb", 0)])
            P.op("dve", lambda e: e.tensor_tensor(out=var, in0=pst[1][:, 0:n], in1=m2, op=ALU.subtract),
                 reads=["pst1", ("tb", 0)], writes=["var"])
            P.op("act", lambda e: e.activation(out=var, in_=var, func=AF.Ln, bias=epsc[:, eps_col:eps_col + 1], scale=1.0),
                 reads=["var", "epsc"], writes=["var"])
            P.op("act", lambda e: e.activation(out=rstd, in_=var, func=AF.Exp, scale=-0.5), reads=["var"], writes=["rstd"])
            for c in range(NCH):
                b = c % 2
                t = tb[:, b, 0:n]
                P.op("dve", lambda e, c=c, t=t: e.tensor_tensor(out=t, in0=xv(c), in1=pst[0][:, 0:n], op=ALU.subtract),
                     reads=[xkeys[c], "pst0"], writes=[("tb", b)])
                P.op("pool", lambda e, t=t: e.tensor_tensor(out=t, in0=t, in1=rstd, op=ALU.mult),
                     reads=[("tb", b), "rstd"], writes=[("tb", b)])
                P.op("act", lambda e, c=c, t=t: e.activation(out=xv(c), in_=t, func=AF.Identity,
                                                            bias=lnbT[:, bcol + c:bcol + c + 1], scale=lngT[:, gcol + c:gcol + c + 1]),
                     reads=[("tb", b), "lngT", "lnbT"], writes=[xkeys[c]])

        wctr = {"win": 0, "wout": 0, "pgu": 0, "py": 0}

        hmod_ready = [False]
        pending_ln = []

        def flush_ln():
            while pending_ln:
                sts, lk = pending_ln.pop(0)
                for st in sts:
                    layer_norm_tile(st["xv"], st["keys"], st["n"], lk, lk, 0)

        def ffn_hmod(l, s, subtiles):
            want = {kk for st in subtiles for kk in st["keys"]}
            if any(kk in want for sts, _ in pending_ln for st in sts for kk in st["keys"]):
                flush_ln()
            k = 0 if s == 0 else 2
            o = 0
            for si, st in enumerate(subtiles):
                n, col = st["n"], st["col"]
                for c in range(NCH):
                    P.op("act", lambda e, st=st, c=c, o=o, n=n, col=col: e.activation(
                        out=hmod[:, c, o:o + n], in_=st["xv"](c), func=AF.Identity,
                        bias=modT[l][:, (3 * k) * 8 + c, col:col + 1], scale=modT[l][:, (3 * k + 1) * 8 + c, col:col + 1]),
                        reads=[st["keys"][c], ("modT", l)], writes=[("hmod", si)])
                o += n

        def ffn_group(l, s, subtiles, prefetch=None):
            k = 0 if s == 0 else 2
            lnk = (l * 3 + k) * 8
            offs = []
            o = 0
            for st in subtiles:
                offs.append(o)
                o += st["n"]
            assert o <= 1024
            if not hmod_ready[0]:
                ffn_hmod(l, s, subtiles)
            hmod_ready[0] = False
            for j in range(NJ):
                if j == 2:
                    flush_ln()
                    run_bg_job()
                slot = wctr["win"] % 3
                wctr["win"] += 1
                P.dma(lambda e, j=j, slot=slot: e.dma_start(out=win[:, slot, :, :], in_=winb_d[l][s][j]),
                      ("win", slot), reads=wkeys[("winb", l, s)], writes=[("win", slot)])
                for si, st in enumerate(subtiles):
                    n, o = st["n"], offs[si]
                    b = wctr["pgu"] % 2
                    wctr["pgu"] += 1
                    for c in range(NCH):
                        P.op("pe", lambda e, c=c, slot=slot, b=b, o=o, n=n: e.matmul(
                            pg[b][:, 0:n], lhsT=win[:, slot, c, 0:128], rhs=hmod[:, c, o:o + n],
                            start=(c == 0), stop=(c == NCH - 1)),
                            reads=[("win", slot), ("hmod", si)], writes=[("pg", b)], signal=(c == NCH - 1))
                    for c in range(NCH):
                        P.op("pe", lambda e, c=c, slot=slot, b=b, o=o, n=n: e.matmul(
                            pu[b][:, 0:n], lhsT=win[:, slot, c, 128:256], rhs=hmod[:, c, o:o + n],
                            start=(c == 0), stop=(c == NCH - 1)),
                            reads=[("win", slot), ("hmod", si)], writes=[("pu", b)], signal=(c == NCH - 1))
                    P.op("act", lambda e, b=b, n=n: e.activation(out=sgb[:, b, 0:n], in_=pg[b][:, 0:n], func=AF.Silu),
                         reads=[("pg", b)], writes=[("sg", b)])
                    P.op("dve", lambda e, b=b, n=n, o=o, j=j: e.tensor_tensor(
                        out=hid[:, j, o:o + n], in0=sgb[:, b, 0:n], in1=pu[b][:, 0:n], op=ALU.mult),
                        reads=[("sg", b), ("pu", b)], writes=[("hid", si)])
            if prefetch is not None:
                ffn_hmod(*prefetch)
                hmod_ready[0] = True
            for c in range(NCH):
                slot = wctr["wout"] % 2
                wctr["wout"] += 1
                P.dma(lambda e, c=c, slot=slot: e.dma_start(out=wout[:, slot, :, :], in_=woutb_d[l][s][c]),
                      ("wout", slot), reads=wkeys[("woutb", l, s)], writes=[("wout", slot)])
                for si, st in enumerate(subtiles):
                    n, o, col = st["n"], offs[si], st["col"]
                    b = wctr["py"] % 2
                    wctr["py"] += 1
                    for j in range(NJ):
                        P.op("pe", lambda e, j=j, slot=slot, b=b, o=o, n=n: e.matmul(
                            py[b][:, 0:n], lhsT=wout[:, slot, j, :], rhs=hid[:, j, o:o + n],
                            start=(j == 0), stop=(j == NJ - 1)),
                            reads=[("wout", slot), ("hid", si)], writes=[("py", b)], signal=(j == NJ - 1))
                    P.op("dve", lambda e, st=st, c=c, b=b, n=n, col=col: e.scalar_tensor_tensor(
                        out=st["xv"](c), in0=py[b][:, 0:n], scalar=modT[l][:, (3 * k + 2) * 8 + c, col:col + 1],
                        in1=st["xv"](c), op0=ALU.mult, op1=ALU.add),
                        reads=[("py", b), st["keys"][c], ("modT", l)], writes=[st["keys"][c]])
            pending_ln.append((subtiles, lnk))

        def lat_subtiles(seq, t0, t1):
            res = []
            for a in range(t0, t1, 512):
                n = min(512, t1 - a)
                res.append(dict(xv=(lambda c, a=a, n=n: x_sb[:, c, a:a + n]),
                                keys=[("x", c, a // 512) for c in range(NCH)], n=n, col=seq))
            return res

        def ctx_subtiles():
            return [dict(xv=(lambda c: ctx_sb[:, c, 0:CTX]), keys=[("ctx", c) for c in range(NCH)], n=CTX, col=NSEQ)]

        def ffn_sublayer(l, s, seq, with_ctx, nxt=None):
            groups = [lat_subtiles(seq, t0, min(T, t0 + 1024)) for t0 in range(0, T, 1024)]
            if with_ctx:
                groups.append(ctx_subtiles())
            for gi, g in enumerate(groups):
                if gi + 1 < len(groups):
                    pf = (l, s, groups[gi + 1])
                elif nxt is not None and len(groups) > 1:
                    pf = (nxt[0], nxt[1], groups[0])
                else:
                    pf = None
                ffn_group(l, s, g, prefetch=pf)

        def hgrn_sublayer(seq):
            l, k = 0, 1
            NT = TT // 128
            NCK = TT // 64
            CTXC = CTX // 64
            off = [0]

            def carve(shape, dtype):
                nb = int(np.prod(shape)) * (4 if dtype == F32 else 2)
                v = aview(off[0], shape, dtype)
                off[0] += (nb + 63) // 64 * 64
                return v
            hx = carve((NCH, TT), BF16)
            wh = carve((NCH, 640), BF16)
            who = carve((D,), BF16)
            vtok = carve((NT, 128), BF16)
            qf = carve((T,), F32)
            gb = carve((T,), BF16)
            ob = carve((T,), BF16)
            A = carve((TT,), F32)
            kin = carve((TT,), BF16)
            Sb = carve((2, 128), BF16)
            PT = carve((2, 64), BF16)
            osq = carve((512,), BF16)
            onb = carve((512,), BF16)
            tot = carve((NCK,), F32)
            B2 = stat[:, :, :].rearrange("p a b -> p (a b)")[:, 0:TT]
            Qin = sgb[:, :, :].rearrange("p a b -> p (a b)").bitcast(BF16)[:, 0:T]
            kin_tok = ctx_sb[:, :, :].rearrange("p a b -> p (a b)").bitcast(BF16)[:, 0:NT * 128].rearrange("p (t n) -> p t n", n=128)
            pstb = pst[1][:, :].bitcast(BF16)
            xkeys_all = [("x", c, st) for c in range(NCH) for st in range((T + 511) // 512)]
            ctxkeys = [("ctx", c) for c in range(NCH)]
            rngs = [(0, CTX)] + [(CTX + a, min(512, T - a)) for a in range(0, T, 512)]
            for c in range(NCH):
                P.op("act", lambda e, c=c: e.activation(
                    out=hx[:, c, 0:CTX], in_=ctx_sb[:, c, :], func=AF.Identity,
                    bias=modT[l][:, 3 * 8 + c, NSEQ:NSEQ + 1], scale=modT[l][:, 4 * 8 + c, NSEQ:NSEQ + 1]),
                    reads=[("ctx", c), ("modT", l)], writes=["hx"])
                for a in range(0, T, 512):
                    n = min(512, T - a)
                    P.op("act", lambda e, c=c, a=a, n=n: e.activation(
                        out=hx[:, c, CTX + a:CTX + a + n], in_=x_sb[:, c, a:a + n], func=AF.Identity,
                        bias=modT[l][:, 3 * 8 + c, seq:seq + 1], scale=modT[l][:, 4 * 8 + c, seq:seq + 1]),
                        reads=[("x", c, a // 512), ("modT", l)], writes=["hx"])
            pb_ctr = [0]

            def proj(sel, a, n, bank):
                for kc in range(NCH):
                    P.op("pe", lambda e, kc=kc: e.matmul(bank[:, 0:n], lhsT=wh[:, kc, sel * 128:(sel + 1) * 128],
                                                         rhs=hx[:, kc, a:a + n], start=(kc == 0), stop=(kc == NCH - 1)),
                         reads=["wh", "hx"], writes=[("bank", id(bank))], signal=(kc == NCH - 1))

            def finalize_gen(po, s0, nst, st):
                o32 = tb[:, 0, 0:nst]
                r32 = tb[:, 1, 0:nst]
                P.op("dve", lambda e: e.tensor_tensor(out=o32, in0=po[:, 0:nst], in1=ob[:, s0:s0 + nst], op=ALU.add),
                     reads=[("bank", id(po)), "ob"], writes=[("tb", 0)])
                P.op("pool", lambda e: e.tensor_tensor(out=osq[:, 0:nst], in0=o32, in1=o32, op=ALU.mult),
                     reads=[("tb", 0)], writes=["osq"])
                yield
                P.op("pe", lambda e: e.matmul(pst[1][:, 0:nst], lhsT=ones_h[:, :], rhs=osq[:, 0:nst], start=True, stop=True),
                     reads=["osq", "ones_h"], writes=["pst1"])
                P.op("act", lambda e: e.activation(out=r32, in_=pst[1][:, 0:nst], func=AF.Ln, bias=epsc[:, 2:3], scale=1.0),
                     reads=["pst1", "epsc"], writes=[("tb", 1)])
                P.op("act", lambda e: e.activation(out=r32, in_=r32, func=AF.Exp, scale=-0.5), reads=[("tb", 1)], writes=[("tb", 1)])
                yield
                P.op("dve", lambda e: e.tensor_tensor(out=o32, in0=o32, in1=r32, op=ALU.mult),
                     reads=[("tb", 0), ("tb", 1)], writes=[("tb", 0)])
                P.op("dve", lambda e: e.scalar_tensor_tensor(
                    out=onb[:, 0:nst], in0=o32, scalar=nwc[:, 0:1], in1=gb[:, s0:s0 + nst], op0=ALU.mult, op1=ALU.mult),
                    reads=[("tb", 0), "nwc", "gb"], writes=["onb"])
                yield
                for c in range(NCH):
                    bank = py[c % 2]
                    P.op("pe", lambda e, c=c, bank=bank: e.matmul(bank[:, 0:nst], lhsT=who[:, c * 128:(c + 1) * 128], rhs=onb[:, 0:nst], start=True, stop=True),
                         reads=["who", "onb"], writes=[("bank", id(bank))])
                    P.op("dve", lambda e, c=c, bank=bank: e.scalar_tensor_tensor(
                        out=x_sb[:, c, s0:s0 + nst], in0=bank[:, 0:nst], scalar=modT[l][:, 5 * 8 + c, seq:seq + 1],
                        in1=x_sb[:, c, s0:s0 + nst], op0=ALU.mult, op1=ALU.add),
                        reads=[("bank", id(bank)), ("x", c, st), ("modT", l)], writes=[("x", c, st)])
                    if c % 2 == 1:
                        yield

            deferred = []
            for h in range(NHEAD):
                P.dma(lambda e, h=h: e.dma_start(out=wh, in_=whb_d[h].rearrange("p k n -> p k n")), "wh",
                      reads=wkeys["whb"], writes=["wh"])
                P.dma(lambda e, h=h: e.dma_start(out=who, in_=whob_d[h]), "who", reads=wkeys["whob"], writes=["who"])
                for t0 in range(0, NT, 4):
                    nt = min(4, NT - t0)
                    bank = py[(t0 // 4) % 2]
                    for ti in range(nt):
                        tl = t0 + ti
                        for kc in range(NCH):
                            P.op("pe", lambda e, kc=kc, ti=ti, tl=tl, bank=bank: e.matmul(
                                bank[:, ti * 128:(ti + 1) * 128], lhsT=hx[:, kc, tl * 128:(tl + 1) * 128],
                                rhs=wh[:, kc, 128:256], start=(kc == 0), stop=(kc == NCH - 1)),
                                reads=["wh", "hx"], writes=[("bank", id(bank))], signal=(kc == NCH - 1))
                    P.op("dve", lambda e, t0=t0, nt=nt, bank=bank: e.tensor_copy(
                        vtok[:, t0:t0 + nt, :], bank[:, 0:nt * 128].rearrange("p (t n) -> p t n", n=128)),
                        reads=[("bank", id(bank))], writes=["vtok"])
                for (a, n) in rngs[1:]:
                    bank = pg[pb_ctr[0] % 2]
                    pb_ctr[0] += 1
                    proj(0, a, n, bank)
                    P.op("act", lambda e, a=a, n=n, bank=bank: e.activation(out=qf[:, a - CTX:a - CTX + n], in_=bank[:, 0:n], func=AF.Silu),
                         reads=[("bank", id(bank))], writes=["qf"])
                    bank = pg[pb_ctr[0] % 2]
                    pb_ctr[0] += 1
                    proj(4, a, n, bank)
                    P.op("act", lambda e, a=a, n=n, bank=bank: e.activation(out=gb[:, a - CTX:a - CTX + n], in_=bank[:, 0:n], func=AF.Silu),
                         reads=[("bank", id(bank))], writes=["gb"])
                for dr in (1, 0):
                    def piece_of(a):
                        return 0 if a < CTX else 1 + (a - CTX) // 512
                    cp_ctr = [0]
                    for pi, (a, n) in enumerate(rngs):
                        bank = pu[pb_ctr[0] % 2]
                        pb_ctr[0] += 1
                        proj(2 + dr, a, n, bank)
                        kA, kB, kK = ("A", pi), ("B2", pi), ("kin", pi)
                        c0, nck = a // 64, n // 64
                        Ap, Bp, Kp = A[:, a:a + n], B2[:, a:a + n], kin[:, a:a + n]
                        P.op("act", lambda e, Ap=Ap, n=n, bank=bank: e.activation(out=Ap, in_=bank[:, 0:n], func=AF.Sigmoid),
                             reads=[("bank", id(bank))], writes=[kA])
                        P.op("dve", lambda e, Ap=Ap, dr=dr, h=h: e.tensor_scalar(out=Ap, in0=Ap, scalar1=omlb[:, dr, h:h + 1], scalar2=lbt[:, dr, h:h + 1],
                                                                               op0=ALU.mult, op1=ALU.add), reads=[kA, "omlb", "lbt"], writes=[kA])
                        P.op("pool", lambda e, Ap=Ap, Kp=Kp: e.tensor_scalar(out=Kp, in0=Ap, scalar1=-1.0, scalar2=1.0, op0=ALU.mult, op1=ALU.add),
                             reads=[kA], writes=[kK])
                    for pi, (a, n) in enumerate(rngs):
                        kA, kB, kK = ("A", pi), ("B2", pi), ("kin", pi)
                        c0, nck = a // 64, n // 64
                        Ap, Bp, Kp = A[:, a:a + n], B2[:, a:a + n], kin[:, a:a + n]
                        P.op("act", lambda e, Ap=Ap: e.activation(out=Ap, in_=Ap, func=AF.Ln), reads=[kA], writes=[kA])
                        P.op("dve", lambda e, Ap=Ap, Bp=Bp, a=a, n=n: e.tensor_tensor_scan(out=Bp, data0=rmask[:, a:a + n], data1=Ap, initial=0.0,
                                                                                         op0=ALU.mult, op1=ALU.add),
                             reads=[kA, "rmask"], writes=[kB])
                        P.op("act", lambda e, Bp=Bp, c0=c0, nck=nck: e.activation(
                            out=tot[:, c0:c0 + nck].unsqueeze(2), in_=Bp.rearrange("p (c t) -> p c t", t=64)[:, :, 63:64], func=AF.Exp),
                            reads=[kB], writes=[("dt", pi)])
                        if dr == 1 and not DBG_SKIP_BWD_SUB:
                            P.op("dve", lambda e, Ap=Ap, Bp=Bp: e.tensor_tensor(out=Bp, in0=Ap, in1=Bp, op=ALU.subtract), reads=[kA, kB], writes=[kB])
                        P.op("act", lambda e, Ap=Ap, Bp=Bp: e.activation(out=Ap, in_=Bp, func=AF.Exp, scale=-1.0), reads=[kB, kA], writes=[kA])
                        P.op("pool", lambda e, Ap=Ap, Kp=Kp: e.tensor_tensor(out=Kp, in0=Kp, in1=Ap, op=ALU.mult), reads=[kK, kA], writes=[kK])
                        P.op("act", lambda e, Bp=Bp: e.activation(out=Bp, in_=Bp, func=AF.Exp), reads=[kB], writes=[kB])
                        if a >= CTX:
                            P.op("dve", lambda e, Bp=Bp, a=a, n=n: e.tensor_tensor(out=Qin[:, a - CTX:a - CTX + n], in0=qf[:, a - CTX:a - CTX + n], in1=Bp, op=ALU.mult),
                                 reads=["qf", kB], writes=[("Qin", pi)])
                        t_lo, t_hi = a // 128, (a + n) // 128
                        for t0 in range(t_lo, t_hi, 4):
                            nt = min(4, t_hi - t0)
                            half = cp_ctr[0] % 2
                            cp_ctr[0] += 1
                            pk = ("pstb", half)
                            for ti in range(nt):
                                tl = t0 + ti
                                P.op("pe", lambda e, ti=ti, tl=tl, half=half: e.transpose(
                                    pstb[:, half * 512 + ti * 128:half * 512 + (ti + 1) * 128], kin[:, tl * 128:(tl + 1) * 128], ident_b[:, :]),
                                    reads=[kK, "ident_b"], writes=[pk], signal=(ti == nt - 1))
                            src = pstb[:, half * 512:half * 512 + nt * 128].rearrange("p (t n) -> p t n", n=128)
                            if half == 0:
                                P.op("dve", lambda e, t0=t0, nt=nt, src=src: e.tensor_copy(kin_tok[:, t0:t0 + nt, :], src),
                                     reads=[pk], writes=ctxkeys + [("ktok", t0 + q_) for q_ in range(nt)])
                            else:
                                P.op("act", lambda e, t0=t0, nt=nt, src=src: e.copy(kin_tok[:, t0:t0 + nt, :], src),
                                     reads=[pk], writes=ctxkeys + [("ktok", t0 + q_) for q_ in range(nt)])
                    if dbg == 77 and h == 0 and dr == 1:
                        P.fence()
                        P.dma(lambda e: e.dma_start(out=dbg_d[0], in_=B2), "dbg0", reads=[], writes=["dbg0"])
                        P.dma(lambda e: e.dma_start(out=dbg_d[1], in_=A), "dbg1", reads=[], writes=["dbg1"])
                        P.dma(lambda e: e.dma_start(out=dbg_d[2][:, 0:NCK], in_=tot), "dbg2", reads=[], writes=["dbg2"])
                        P.fence()
                        P.dead = True
                    order = list(range(NCK)) if dr == 0 else (list(range(CTXC - 1, -1, -1)) + list(range(NCK - 1, CTXC - 1, -1)))
                    mask = maskf if dr == 0 else maskb
                    P.op("pool", lambda e: e.memset(Sb[:, 0, :], 0.0), writes=[("S", 0)])
                    done_in_st = {}

                    def issue_sT(idx2):
                        ck2 = order[idx2]
                        if ck2 < CTXC:
                            return
                        a2 = ck2 * 64
                        tl2 = a2 - CTX
                        pb2 = (ck2 % 2) * 64
                        r = idx2 % 8
                        r2 = idx2 % 2
                        pi2 = piece_of(a2)
                        P.op("pe", lambda e: e.matmul(
                            pst[0][pb2:pb2 + 64, r * 64:(r + 1) * 64], lhsT=kin[:, a2:a2 + 64], rhs=Qin[:, tl2:tl2 + 64], start=True, stop=True),
                            reads=[("kin", pi2), ("Qin", pi2)], writes=[("sT", r)])
                        P.op("dve", lambda e, mask=mask: e.tensor_tensor(
                            out=PT[pb2:pb2 + 64, r2, :], in0=pst[0][pb2:pb2 + 64, r * 64:(r + 1) * 64], in1=mask[pb2:pb2 + 64, :], op=ALU.mult),
                            reads=[("sT", r), "maskf", "maskb"], writes=[("PT", r2)])
                    issue_sT(0)
                    for idx, ck in enumerate(order):
                        if deferred:
                            if next(deferred[0], "END") == "END":
                                deferred.pop(0)
                        if idx + 1 < len(order):
                            issue_sT(idx + 1)
                        a = ck * 64
                        tile = ck // 2
                        pb = (ck % 2) * 64
                        is_lat = ck >= CTXC
                        tl = a - CTX
                        sc, sn = idx % 2, (idx + 1) % 2
                        pS = pu[idx % 2]
                        if is_lat:
                            st = tl // 512
                            po = pg[st % 2]
                            r2 = idx % 2
                            o0 = tl % 512
                            P.op("pe", lambda e, pb=pb, r2=r2, tile=tile, po=po, o0=o0: e.matmul(
                                po[:, o0:o0 + 64], lhsT=vtok[pb:pb + 64, tile, :], rhs=PT[pb:pb + 64, r2, :], start=True, stop=False),
                                reads=["vtok", ("PT", r2)], writes=[("bank", id(po))], signal=False)
                        P.op("pe", lambda e, pb=pb, tile=tile, pS=pS: e.matmul(
                            pS[:, 0:128], lhsT=kin_tok[pb:pb + 64, tile, :], rhs=vtok[pb:pb + 64, tile, :], start=True, stop=False),
                            reads=[("ktok", tile), "vtok"], writes=[("bank", id(pS))], signal=False)
                        if is_lat:
                            P.op("pe", lambda e, sc=sc, tl=tl, po=po, o0=o0: e.matmul(
                                po[:, o0:o0 + 64], lhsT=Sb[:, sc, :], rhs=Qin[:, tl:tl + 64], start=False, stop=True),
                                reads=[("S", sc), ("Qin", piece_of(a))], writes=[("bank", id(po))])
                        P.op("pe", lambda e, sc=sc, pS=pS: e.matmul(pS[:, 0:128], lhsT=ident_b[:, :], rhs=Sb[:, sc, :], start=False, stop=True),
                             reads=[("S", sc), "ident_b"], writes=[("bank", id(pS))])
                        if dr == 0:
                            dck = ck
                        else:
                            dck = order[idx + 1] if idx + 1 < len(order) else ck
                        P.op("dve", lambda e, sn=sn, pS=pS, dck=dck: e.tensor_scalar_mul(Sb[:, sn, :], pS[:, 0:128], tot[:, dck:dck + 1]),
                             reads=[("bank", id(pS)), ("dt", piece_of(dck * 64))], writes=[("S", sn)])
                        if dbg == 78 and h == 0 and dr == 1 and idx == len(order) - 1:
                            P.op("act", lambda e, po=po: e.copy(ob[:, 0:512], po[:, 0:512]), reads=[("bank", id(po))], writes=["ob"])
                            P.fence()
                            P.dma(lambda e: e.dma_start(out=dbgb_d[0][:, 0:T], in_=ob), "dbg0", reads=[], writes=["dbg0"])
                            P.dma(lambda e: e.dma_start(out=dbgb_d[1][:, 0:T], in_=Qin), "dbg1", reads=[], writes=["dbg1"])
                            P.dma(lambda e: e.dma_start(out=dbgb_d[2], in_=kin), "dbg2", reads=[], writes=["dbg2"])
                            P.fence()
                            P.dead = True
                        if is_lat:
                            done_in_st[st] = done_in_st.get(st, 0) + 1
                            nst = min(512, T - st * 512)
                            if done_in_st[st] == nst // 64:
                                s0 = st * 512
                                if dr == 1:
                                    P.op("act", lambda e, po=po, s0=s0, nst=nst: e.copy(ob[:, s0:s0 + nst], po[:, 0:nst]),
                                         reads=[("bank", id(po))], writes=["ob"])
                                else:
                                    while deferred:
                                        if next(deferred[0], "END") == "END":
                                            deferred.pop(0)
                                    deferred.append(finalize_gen(po, s0, nst, st))
                    while deferred:
                        if next(deferred[0], "END") == "END":
                            deferred.pop(0)
            P.fence()
            zbuf = aview(0, (NCH, 1024), BF16)
            lnk = (l * 3 + k) * 8
            for stl in lat_subtiles(seq, 0, T):
                layer_norm_tile(stl["xv"], stl["keys"], stl["n"], lnk, lnk, 0, zbuf=zbuf)

        def cm_sublayer(seq):
            l, k = 1, 1
            hmc = aview(0, (NCH, 512), BF16)
            wvs2 = [aview(8192, (NCH, 512), BF16),
                    ctx_sb[:, :, :].rearrange("p a b -> p (a b)").bitcast(BF16)[:, 0:NCH * 512].rearrange("p (k n) -> p k n", k=NCH)]
            vt = aview(16384, (4, CMI), BF16)
            ub = aview(40960, (NFC, 512), BF16)
            wus = aview(65536, (2, NCH, 128), BF16)
            wos2 = [aview(69632, (NFC, 128), BF16),
                    stat[:, 2:5, :].rearrange("p a b -> p (a b)").bitcast(BF16)[:, 0:NFC * 128].rearrange("p (f n) -> p f n", f=NFC)]
            dctr = [0, 0]
            rt = aview(75776, (NFC, 128), F32)
            zbuf = aview(16384, (NCH, 1024), BF16)
            bst = cmst[:, 0:144].rearrange("p (c s) -> p c s", c=4)
            mv = cmst[:, 144:152].rearrange("p (c s) -> p c s", c=4)
            sd = cmst[:, 152:156]
            lnk = (l * 3 + k) * 8
            P.dma(lambda e: e.dma_start(out=rt.rearrange("p a b -> p (a b)"), in_=r_d[:, :]), "rt", reads=["r_d"], writes=["rt"])
            bctr = [0]

            def gelu_from(bank, out, okeys):
                bk = ("bank", id(bank))
                P.op("act", lambda e: e.activation(out=out, in_=bank[:, :], func=AF.Gelu_apprx_tanh), reads=[bk], writes=okeys)

            def cm_hmc(st):
                s0 = st * 512
                for c in range(NCH):
                    P.op("act", lambda e, c=c, s0=s0: e.activation(
                        out=hmc[:, c, :], in_=x_sb[:, c, s0:s0 + 512], func=AF.Identity,
                        bias=modT[l][:, 3 * 8 + c, seq:seq + 1], scale=modT[l][:, 4 * 8 + c, seq:seq + 1]),
                        reads=[("x", c, st), ("modT", l)], writes=["hmc"])

            nsub = (T + 511) // 512
            cm_hmc(0)
            for st in range(nsub):
                s0 = st * 512
                for cb in range(6):
                    wsl = dctr[0] % 2
                    dctr[0] += 1
                    wvs = wvs2[wsl]
                    P.dma(lambda e, cb=cb, wvs=wvs: e.dma_start(out=wvs, in_=wvb_d[cb]), ("wvs", wsl), reads=wkeys["wvb"],
                          writes=[("wvs", wsl)] + ([("ctx", c_) for c_ in range(NCH)] if wsl == 1 else []))
                    for ch in range(4):
                        bank = pg[bctr[0] % 2]
                        bctr[0] += 1
                        for kc in range(NCH):
                            P.op("pe", lambda e, kc=kc, ch=ch, bank=bank, wvs=wvs: e.matmul(
                                bank[:, :], lhsT=hmc[:, kc, ch * 128:(ch + 1) * 128], rhs=wvs[:, kc, :], start=(kc == 0), stop=(kc == NCH - 1)),
                                reads=["hmc", ("wvs", wsl)], writes=[("bank", id(bank))], signal=(kc == NCH - 1))
                        gelu_from(bank, vt[:, ch, cb * 512:(cb + 1) * 512], [("vt", ch)])
                        P.op("dve", lambda e, ch=ch, cb=cb: e.bn_stats(bst[:, ch, cb * 6:(cb + 1) * 6], vt[:, ch, cb * 512:(cb + 1) * 512]),
                             reads=[("vt", ch)], writes=[("bst", ch)])
                for ch in range(4):
                    P.op("dve", lambda e, ch=ch: e.bn_aggr(mv[:, ch, :], bst[:, ch, :]), reads=[("bst", ch)], writes=[("mv", ch)])
                    P.op("act", lambda e, ch=ch: e.activation(out=sd[:, ch:ch + 1], in_=mv[:, ch, 1:2], func=AF.Sqrt, bias=epsc[:, 1:2], scale=1.0),
                         reads=[("mv", ch), "epsc"], writes=[("sd", ch)])
                    P.op("dve", lambda e, ch=ch: e.reciprocal(sd[:, ch:ch + 1], sd[:, ch:ch + 1]), reads=[("sd", ch)], writes=[("sd", ch)])
                    P.op("dve", lambda e, ch=ch: e.tensor_scalar(out=vt[:, ch, :], in0=vt[:, ch, :], scalar1=mv[:, ch, 0:1], scalar2=sd[:, ch:ch + 1],
                                                                op0=ALU.subtract, op1=ALU.mult),
                         reads=[("vt", ch), ("mv", ch), ("sd", ch)], writes=[("vt", ch)])
                for fc in range(NFC):
                    slot = fc % 2
                    P.dma(lambda e, fc=fc, slot=slot: e.dma_start(out=wus[:, slot, :, :], in_=wub_d[fc]), ("wus", slot),
                          reads=wkeys["wub"], writes=[("wus", slot)])
                    bank = pu[fc % 2]
                    for kc in range(NCH):
                        P.op("pe", lambda e, kc=kc, slot=slot, bank=bank: e.matmul(
                            bank[:, :], lhsT=wus[:, slot, kc, :], rhs=hmc[:, kc, :], start=(kc == 0), stop=(kc == NCH - 1)),
                            reads=["hmc", ("wus", slot)], writes=[("bank", id(bank))], signal=(kc == NCH - 1))
                    gelu_from(bank, ub[:, fc, :], [("ub", fc)])
                if st + 1 < nsub:
                    cm_hmc(st + 1)
                for fc in range(NFC):
                    bank = py[fc % 2]
                    for ch in range(4):
                        P.op("pe", lambda e, fc=fc, ch=ch, bank=bank: e.matmul(
                            bank[:, ch * 128:(ch + 1) * 128], lhsT=vt[:, ch, fc * 128:(fc + 1) * 128], rhs=wsT[:, fc // 3, :], start=True, stop=True),
                            reads=[("vt", ch), "wsT"], writes=[("bank", id(bank))], signal=(ch == 3))
                    tmp = tb[:, fc % 2, :]
                    P.op("dve", lambda e, fc=fc, bank=bank, tmp=tmp: e.scalar_tensor_tensor(
                        out=tmp.rearrange("p (c t) -> p c t", c=4), in0=bank[:, :].rearrange("p (c t) -> p c t", c=4), scalar=vgT[:, fc:fc + 1],
                        in1=rt[:, fc, :].unsqueeze(1).to_broadcast([128, 4, 128]), op0=ALU.mult, op1=ALU.add),
                        reads=[("bank", id(bank)), "vgT", "rt"], writes=[("tb", fc % 2)])
                    P.op("pool" if fc % 2 == 0 else "dve", lambda e, fc=fc, tmp=tmp: e.tensor_tensor(out=ub[:, fc, :], in0=tmp, in1=ub[:, fc, :], op=ALU.mult),
                         reads=[("tb", fc % 2), ("ub", fc)], writes=[("ub", fc)])
                for c in range(NCH):
                    osl = dctr[1] % 2
                    dctr[1] += 1
                    wos = wos2[osl]
                    P.dma(lambda e, c=c, wos=wos: e.dma_start(out=wos, in_=wcob_d[c]), ("wos", osl), reads=wkeys["wcob"],
                          writes=[("wos", osl)])
                    bank = pg[c % 2]
                    for fc in range(NFC):
                        P.op("pe", lambda e, fc=fc, bank=bank, wos=wos: e.matmul(bank[:, :], lhsT=wos[:, fc, :], rhs=ub[:, fc, :], start=(fc == 0), stop=(fc == NFC - 1)),
                             reads=[("wos", osl), ("ub", fc)], writes=[("bank", id(bank))], signal=(fc == NFC - 1))
                    P.op("dve", lambda e, c=c, bank=bank, s0=s0: e.scalar_tensor_tensor(
                        out=x_sb[:, c, s0:s0 + 512], in0=bank[:, :], scalar=modT[l][:, 5 * 8 + c, seq:seq + 1],
                        in1=x_sb[:, c, s0:s0 + 512], op0=ALU.mult, op1=ALU.add),
                        reads=[("bank", id(bank)), ("x", c, st), ("modT", l)], writes=[("x", c, st)])
                stl = lat_subtiles(seq, s0, s0 + 512)[0]
                layer_norm_tile(stl["xv"], stl["keys"], 512, lnk, lnk, 0)

        def load_tokens(src_rows_ap, ntok, dstv, keys_fn):
            for i, t0 in enumerate(range(0, ntok, 128)):
                sl = i % 4
                P.dma(lambda e, t0=t0, sl=sl: e.dma_start(out=iost[:, sl, :], in_=src_rows_ap[t0:t0 + 128, :]),
                      ("iost", sl), writes=[("iost", sl)])
                for h in range(2):
                    b = (2 * i + h) % 2
                    for cc in range(4):
                        c = h * 4 + cc
                        P.op("pe", lambda e, c=c, cc=cc, sl=sl, b=b: e.transpose(
                            pg[b][:, cc * 128:(cc + 1) * 128], iost[:, sl, c * 128:(c + 1) * 128], ident_f[:, :]),
                            reads=[("iost", sl), "ident_f"], writes=[("pg", b)], signal=(cc == 3))
                    P.op("dve" if h == 0 else "act",
                         (lambda e, h=h, b=b, t0=t0: e.tensor_copy(dstv(h, t0), pg[b][:, :].rearrange("p (c n) -> p c n", n=128)))
                         if h == 0 else
                         (lambda e, h=h, b=b, t0=t0: e.copy(dstv(h, t0), pg[b][:, :].rearrange("p (c n) -> p c n", n=128))),
                         reads=[("pg", b)], writes=keys_fn(h, t0))

        def store_tokens(dst_rows_ap, ntok):
            for i, t0 in enumerate(range(0, ntok, 128)):
                sl = i % 4
                for h in range(2):
                    b = (2 * i + h) % 2
                    for cc in range(4):
                        c = h * 4 + cc
                        P.op("pe", lambda e, c=c, cc=cc, b=b, t0=t0: e.transpose(
                            pu[b][:, cc * 128:(cc + 1) * 128], x_sb[:, c, t0:t0 + 128], ident_f[:, :]),
                            reads=[("x", c, t0 // 512), "ident_f"], writes=[("pu", b)], signal=(cc == 3))
                    if h == 0:
                        P.op("dve", lambda e, sl=sl, b=b: e.tensor_copy(iost[:, sl, 0:512], pu[b][:, :]),
                             reads=[("pu", b)], writes=[("iost", sl)])
                    else:
                        P.op("act", lambda e, sl=sl, b=b: e.copy(iost[:, sl, 512:1024], pu[b][:, :]),
                             reads=[("pu", b)], writes=[("iost", sl)])
                P.dma(lambda e, t0=t0, sl=sl: e.dma_start(out=dst_rows_ap[t0:t0 + 128, :], in_=iost[:, sl, :]),
                      ("iost", sl), reads=[("iost", sl)], writes=[("outd", t0)])


        if dbg == 3:
            P.finish(); P.emit(); return nc
        conv_rest()
        for seq in range(NSEQ):
            P.scope = f"load{seq}"
            load_tokens(x_d[seq], T,
                        lambda h, t0: x_sb[:, h * 4:(h + 1) * 4, t0:t0 + 128],
                        lambda h, t0: [("x", h * 4 + cc, t0 // 512) for cc in range(4)])
            load_tokens(ctx_d[seq], CTX,
                        lambda h, t0: ctx_sb[:, h * 4:(h + 1) * 4, t0:t0 + 128],
                        lambda h, t0: [("ctx", h * 4 + cc) for cc in range(4)])
            P.fence()
            P.scope = f"ffn00_{seq}"
            if stages >= 1 and dbg != 4:
                ffn_sublayer(0, 0, seq, True)
            flush_ln()
            while bg_jobs:
                run_bg_job()
            if stages >= 2:
                P.fence()
                P.scope = f"hgrn_{seq}"
                hgrn_sublayer(seq)
                P.fence()
            P.scope = f"ffn01_{seq}"
            if stages >= 3:
                ffn_sublayer(0, 1, seq, False, nxt=(1, 0) if stages >= 4 else None)
            P.scope = f"ffn10_{seq}"
            if stages >= 4:
                ffn_sublayer(1, 0, seq, False)
            if stages >= 5:
                flush_ln()
                P.fence()
                P.scope = f"cm_{seq}"
                cm_sublayer(seq)
                P.fence()
            P.scope = f"ffn11_{seq}"
            if stages >= 6:
                ffn_sublayer(1, 1, seq, False)
            flush_ln()
            P.fence()
            P.scope = f"store{seq}"
            store_tokens(out_d[seq], T)
            P.fence()
        P.finish()
        P.emit()
    return nc


def make_in_maps(inputs, n_cores, NSEQ):
    ident = np.eye(128, dtype=np.float32)
    TT = inputs["x"].shape[1] + inputs["ctx"].shape[1]
    si = np.arange(128)[:, None] % 64
    ti = np.arange(64)[None, :]
    maskf = (si <= ti).astype(np.float32)
    maskb = (si >= ti).astype(np.float32)
    rmask = np.broadcast_to((np.arange(TT) % 64 != 0).astype(np.float32)[None, :], (128, TT)).copy()
    maps = []
    for i in range(n_cores):
        b0 = i * NSEQ
        cc = np.concatenate([inputs["c"][b0:b0 + NSEQ], inputs["c_ctx"][None, :]], axis=0)
        maps.append({
            "x": np.ascontiguousarray(inputs["x"][b0:b0 + NSEQ]),
            "ctx": np.ascontiguousarray(inputs["ctx"][b0:b0 + NSEQ]),
            "cc": np.ascontiguousarray(cc.reshape((NSEQ + 1) * NCH, 128)),
            "mod_w": inputs["mod_w"],
            "mod_b": np.ascontiguousarray(inputs["mod_b"].reshape(DEPTH * 72, 128)),
            "ln_g": np.ascontiguousarray(inputs["ln_g"].reshape(48, 128)),
            "ln_b": np.ascontiguousarray(inputs["ln_b"].reshape(48, 128)),
            "ffn_w_in": inputs["ffn_w_in"],
            "ffn_w_out": inputs["ffn_w_out"],
            "ident": ident,
            "hg_w_in": inputs["hg_w_in"][0],
            "hg_w_out": inputs["hg_w_out"][0],
            "hg_lb": np.ascontiguousarray(inputs["hg_lower_bounds"].reshape(48, 128)),
            "hg_nw": np.ascontiguousarray(inputs["hg_norm_w"][0].reshape(128, 1)),
            "cm_w_in": inputs["cm_w_in"][0],
            "cm_w_out": inputs["cm_w_out"][0],
            "cm_vg": np.ascontiguousarray(inputs["cm_v_g"][0].reshape(NFC, 128)),
            "cm_vb": np.ascontiguousarray(inputs["cm_v_b"][0].reshape(NFC, 128)),
            "cm_ws": inputs["cm_w_s"][0],
            "cm_bs": inputs["cm_b_s"][0],
            "maskf": maskf, "maskb": maskb, "rmask": rmask,
        })
    return maps


def kernel(**inputs):
    inputs = {k: np.asarray(v) for k, v in inputs.items()}
    n_cores = 8
    B, T, _ = inputs["x"].shape
    CTX = inputs["ctx"].shape[1]
    NSEQ = B // n_cores
    nc = build_nc(NSEQ, T, CTX, stages=6)
    in_maps = make_in_maps(inputs, n_cores, NSEQ)
    res = run_bass_kernel_spmd(nc, in_maps, core_ids=list(range(n_cores)))
    return np.concatenate([r["out"] for r in res.results], axis=0)
```

```python
import numpy as np
from contextlib import ExitStack
import concourse.bass as bass
import concourse.mybir as mybir
from concourse.bass_utils import run_bass_kernel_spmd

F32 = mybir.dt.float32
BF16 = mybir.dt.bfloat16
AF = mybir.ActivationFunctionType
ALU = mybir.AluOpType

D = 1024
NCH = 8
FH = 2816
NJ = 22
NHEAD = 8
CMI = 3072
NFC = 24
DEPTH = 2
DN_ALPHA = (2 * DEPTH) ** 0.25
LN_EPS = 1e-5
RMS_EPS = 1e-6
EPOCH = 30000
import os
DBG_SKIP_BWD_SUB = bool(int(os.environ.get('DBG_SKIP_BWD_SUB', '0')))


class Tok:
    __slots__ = ("grp", "ep", "val", "sem", "own", "inc")

    def __init__(self, grp):
        self.grp = grp
        self.ep = 0
        self.val = None
        self.sem = None
        self.own = False
        self.inc = 1


class Rec:
    __slots__ = ("fn", "deps", "tok", "scope")


class Prog:
    ENG = ("pe", "act", "dve", "pool", "sp")

    def __init__(self, nc, es):
        self.nc = nc
        self.es = es
        self.streams = {e: [] for e in self.ENG}
        self.cnt = {}
        self.cur = {}
        self.pending = {e: [] for e in self.ENG}
        self.lastw = {}
        self.readers = {}
        self.fence_deps = []
        self.latest = {}
        self.nsem = 0
        self.scope = None
        self.use_scopes = False
        self.nofence = set()

    def _newsem(self):
        self.nsem += 1
        return self.es.enter_context(self.nc.semaphore(f"s{self.nsem}"))

    def _signal(self, grp, inc):
        if grp not in self.cur or self.cnt[grp] + inc > EPOCH:
            ep = self.cur[grp][0] + 1 if grp in self.cur else 0
            self.cur[grp] = (ep, self._newsem())
            self.cnt[grp] = 0
        self.cnt[grp] += inc
        t = Tok(grp)
        t.ep, t.sem = self.cur[grp]
        t.val = self.cnt[grp]
        t.own = True
        t.inc = inc
        return t

    def _record(self, eng, fn, reads, writes, tok_grp, inc, signal):
        if getattr(self, "dead", False):
            return None
        deps = list(self.fence_deps)
        for k in reads:
            t = self.lastw.get(k)
            if t is not None:
                deps.append(t)
        for k in writes:
            t = self.lastw.get(k)
            if t is not None:
                deps.append(t)
            deps.extend(self.readers.get(k, {}).values())
        for t in deps:
            if t.sem is None and not (eng == "pe" and t.grp == "pe"):
                raise RuntimeError(f"dependency on unsignaled op ({t.grp}) from {eng}")
        r = Rec()
        r.fn = fn
        r.deps = deps
        r.scope = self.scope
        if signal:
            tok = self._signal(tok_grp, inc)
            if tok_grp == eng:
                for p in self.pending[eng]:
                    p.ep, p.sem, p.val = tok.ep, tok.sem, tok.val
                self.pending[eng] = []
        else:
            tok = Tok(tok_grp)
            self.pending[eng].append(tok)
        r.tok = tok
        for k in reads:
            self.readers.setdefault(k, {})[tok.grp] = tok
        for k in writes:
            self.lastw[k] = tok
            self.readers[k] = {}
        self.latest[tok.grp] = tok
        self.streams[eng].append(r)
        return tok

    def op(self, eng, fn, reads=(), writes=(), signal=True):
        return self._record(eng, fn, reads, writes, eng, 1, signal)

    def dma(self, fn, slot, reads=(), writes=(), queue="sp"):
        return self._record(queue, fn, reads, writes, ("dma", slot), 16, True)

    def fence(self, all_groups=False):
        for e in self.ENG:
            if self.pending[e]:
                raise RuntimeError(f"fence with pending unsignaled ops on {e}")
        self.fence_deps = [t for g, t in self.latest.items() if all_groups or g not in self.nofence]

    def finish(self):
        self.fence(all_groups=True)
        r = Rec()
        r.fn = None
        r.deps = list(self.fence_deps)
        r.tok = None
        r.scope = None
        self.streams["sp"].append(r)

    def emit(self):
        nc = self.nc
        with nc.Block() as block:
            decos = {"pe": block.tensor, "act": block.scalar, "dve": block.vector,
                     "pool": block.gpsimd, "sp": block.sync}
            for name in self.ENG:
                def body(eng, name=name):
                    waited = {}
                    for r in self.streams[name]:
                        best = {}
                        for t in r.deps:
                            if name == "pe" and t.grp == "pe":
                                continue
                            b = best.get(t.grp)
                            if b is None or (t.ep, t.val) > (b.ep, b.val):
                                best[t.grp] = t
                        for t in best.values():
                            cur = waited.get(t.grp, (-1, -1))
                            if (t.ep, t.val) <= cur:
                                continue
                            eng.wait_ge(t.sem, t.val)
                            waited[t.grp] = (t.ep, t.val)
                        if r.fn is not None:
                            if self.use_scopes and r.scope is not None:
                                with nc.named_scope(r.scope):
                                    inst = r.fn(eng)
                            else:
                                inst = r.fn(eng)
                            if r.tok.own:
                                inst.then_inc(r.tok.sem, r.tok.inc)
                decos[name](body)


def build_nc(NSEQ, T, CTX, stages, dbg=False, scopes=False):
    nc = bass.Bass("TRN2", target_bir_lowering=False)
    NB = NSEQ + 1
    dt = nc.dram_tensor

    def din(name, shape, dtype=F32):
        return dt(name, list(shape), dtype, kind="ExternalInput").ap()

    x_d = din("x", (NSEQ, T, D))
    ctx_d = din("ctx", (NSEQ, CTX, D))
    cc_d = din("cc", (NB * NCH, 128))
    mod_w_d = din("mod_w", (DEPTH, D, 9 * D))
    mod_b_d = din("mod_b", (DEPTH * 72, 128))
    ln_g_d = din("ln_g", (48, 128))
    ln_b_d = din("ln_b", (48, 128))
    ffn_w_in_d = din("ffn_w_in", (DEPTH, 2, D, 2 * FH))
    ffn_w_out_d = din("ffn_w_out", (DEPTH, 2, FH, D))
    ident_d = din("ident", (128, 128))
    TT = T + CTX
    hg_w_in_d = din("hg_w_in", (D, 5 * D))
    hg_w_out_d = din("hg_w_out", (D, D))
    hg_lb_d = din("hg_lb", (48, 128))
    hg_nw_d = din("hg_nw", (128, 1))
    cm_w_in_d = din("cm_w_in", (D, 2 * CMI))
    cm_w_out_d = din("cm_w_out", (CMI, D))
    cm_vg_d = din("cm_vg", (NFC, 128))
    cm_vb_d = din("cm_vb", (NFC, 128))
    cm_ws_d = din("cm_ws", (8, 128, 128))
    cm_bs_d = din("cm_bs", (8, 128))
    maskf_d = din("maskf", (128, 64))
    maskb_d = din("maskb", (128, 64))
    rmask_d = din("rmask", (128, TT))
    whb_d = dt("whb", [NHEAD, 128, NCH, 640], BF16, kind="Internal").ap()
    whob_d = dt("whob", [NHEAD, 128, D], BF16, kind="Internal").ap()
    wub_d = dt("wub", [NFC, 128, NCH, 128], BF16, kind="Internal").ap()
    wvb_d = dt("wvb", [6, 128, NCH, 512], BF16, kind="Internal").ap()
    wcob_d = dt("wcob", [NCH, 128, NFC, 128], BF16, kind="Internal").ap()
    r_d = dt("r_d", [128, NFC * 128], F32, kind="Internal").ap()
    out_d = dt("out", [NSEQ, T, D], F32, kind="ExternalOutput").ap()
    dbg_d = dt("dbgout", [3, 128, TT], F32, kind="ExternalOutput").ap() if dbg == 77 else None
    dbgb_d = dt("dbgb", [3, 128, TT], BF16, kind="ExternalOutput").ap() if dbg == 78 else None

    winb_d = [[dt(f"winb{l}{s}", [NJ, 128, NCH, 256], BF16, kind="Internal").ap() for s in range(2)] for l in range(DEPTH)]
    woutb_d = [[dt(f"woutb{l}{s}", [NCH, 128, NJ, 128], BF16, kind="Internal").ap() for s in range(2)] for l in range(DEPTH)]

    es = ExitStack()
    with es:
        P = Prog(nc, es)
        P.use_scopes = scopes
        P.scope = "setup"
        sb = lambda name, shape, dtype: es.enter_context(nc.sbuf_tensor(name, list(shape), dtype))
        ps = lambda name: es.enter_context(nc.psum_tensor(name, [128, 512], F32))

        x_sb = sb("x_sb", (128, NCH, T), F32)
        ctx_sb = sb("ctx_sb", (128, NCH, CTX), F32)
        ARENA = 88064
        arena = sb("arena", (128, ARENA // 2), BF16)
        stat = sb("stat", (128, 5, 512), F32)
        sgb = sb("sgb", (128, 2, 512), F32)
        tb = sb("tb", (128, 2, 512), F32)
        ident_f = sb("ident_f", (128, 128), F32)
        ident_b = sb("ident_b", (128, 128), BF16)
        ones_b = sb("ones_b", (128, 128), BF16)
        small_in = sb("small_in", (128, 128), F32)
        scT = sb("scT", (128, NCH, 8), F32)
        modT = [sb(f"modT{l}", (128, 72, 8), F32) for l in range(DEPTH)]
        modbT = sb("modbT", (128, DEPTH * 72), F32)
        lngT = sb("lngT", (128, 48), F32)
        lnbT = sb("lnbT", (128, 48), F32)
        epsc = sb("epsc", (128, 4), F32)
        ones_h = sb("ones_h", (128, 128), BF16)
        maskf = sb("maskf_s", (128, 64), F32)
        maskb = sb("maskb_s", (128, 64), F32)
        rmask = sb("rmask_s", (128, TT), BF16)
        lbT = sb("lbT", (128, 48), F32)
        lbt = sb("lbt", (128, 2, 8), F32)
        omlb = sb("omlb", (128, 2, 8), F32)
        nwc = sb("nwc", (128, 1), F32)
        vgT = sb("vgT", (128, NFC), F32)
        vbT = sb("vbT", (128, NFC), F32)
        wsT = sb("wsT", (128, 8, 128), BF16)
        cmst = sb("cmst", (128, 160), F32)
        wus4 = sb("wus4", (128, 4, NCH, 128), BF16)

        def aview(off, shape, dtype):
            nbytes = int(np.prod(shape)) * (4 if dtype == F32 else 2)
            assert off % 4 == 0 and off + nbytes <= ARENA, (off, nbytes)
            v = arena[:, off // 2:(off + nbytes) // 2]
            if dtype == F32:
                v = v.bitcast(F32)
            if len(shape) == 1:
                return v
            if len(shape) == 2:
                return v.rearrange("p (a b) -> p a b", a=shape[0])
            if len(shape) == 3:
                return v.rearrange("p (a b c) -> p a b c", a=shape[0], b=shape[1])
            return v

        hmod = aview(0, (NCH, 1024), BF16)
        hid = aview(16384, (NJ, 1024), BF16)
        win = aview(61440, (3, NCH, 256), BF16)
        wout = aview(73728, (2, NJ, 128), BF16)
        st32 = aview(16384, (2, 5632), F32)
        st16 = aview(61440, (2, 5632), BF16)
        iost = aview(16384, (4, D), F32)

        pg = [ps("pg0"), ps("pg1")]
        pu = [ps("pu0"), ps("pu1")]
        py = [ps("py0"), ps("py1")]
        pst = [ps("pst0"), ps("pst1")]

        P.dma(lambda e: e.dma_start(out=ident_f[:, :], in_=ident_d[:, :]), "ident", writes=["ident_f"])
        P.op("dve", lambda e: e.tensor_copy(ident_b[:, :], ident_f[:, :]), reads=["ident_f"], writes=["ident_b"])
        P.op("pool", lambda e: e.memset(ones_b[:, :], 1.0 / D), writes=["ones_b"])
        P.op("pool", lambda e: e.memset(epsc[:, 0:1], LN_EPS / (DN_ALPHA ** 2)), writes=["epsc"])
        P.op("pool", lambda e: e.memset(epsc[:, 1:2], LN_EPS), writes=["epsc"])
        P.op("pool", lambda e: e.memset(epsc[:, 2:3], RMS_EPS), writes=["epsc"])
        P.op("pool", lambda e: e.memset(scT[:, :, :], 0.0), writes=["scT"])

        wkeys = {}

        def cast_dma(out_ap, in_ap, slot, key):
            P.nofence.add(("dma", slot))
            P.dma(lambda e: e.dma_start(out=out_ap, in_=in_ap), slot, writes=[key], queue="pool")

        def conv_ffn(l, s):
            ki, ko = [], []
            for kc in range(NCH):
                for half in range(2):
                    key = ("winb", l, s, kc, half)
                    ki.append(key)
                    cast_dma(winb_d[l][s][:, :, kc, half * 128:(half + 1) * 128].rearrange("j p n -> p j n"),
                             ffn_w_in_d[l, s, kc * 128:(kc + 1) * 128, half * FH:(half + 1) * FH].rearrange("p (j n) -> p j n", n=128),
                             ("cw_in", l, s), key)
            for j in range(NJ):
                key = ("woutb", l, s, j)
                ko.append(key)
                cast_dma(woutb_d[l][s][:, :, j, :].rearrange("c p n -> p c n"),
                         ffn_w_out_d[l, s, j * 128:(j + 1) * 128, :].rearrange("p (c n) -> p c n", n=128),
                         ("cw_out", l, s), key)
            wkeys[("winb", l, s)] = ki
            wkeys[("woutb", l, s)] = ko

        def conv_hg():
            ki = []
            for kc in range(NCH):
                for sel in range(5):
                    key = ("whb", kc, sel)
                    ki.append(key)
                    cast_dma(whb_d[:, :, kc, sel * 128:(sel + 1) * 128].rearrange("h p n -> p h n"),
                             hg_w_in_d[kc * 128:(kc + 1) * 128, sel * D:(sel + 1) * D].rearrange("p (h n) -> p h n", n=128),
                             "cw_hgi", key)
            wkeys["whb"] = ki
            cast_dma(whob_d.rearrange("h p n -> p h n"), hg_w_out_d.rearrange("(h p) n -> p h n", p=128), "cw_hgo", "whob")
            wkeys["whob"] = ["whob"]

        def conv_cm():
            ku, kv, ko = [], [], []
            for kc in range(NCH):
                key = ("wub", kc)
                ku.append(key)
                cast_dma(wub_d[:, :, kc, :].rearrange("f p n -> p f n"),
                         cm_w_in_d[kc * 128:(kc + 1) * 128, 0:CMI].rearrange("p (f n) -> p f n", n=128), "cw_cmu", key)
                key = ("wvb", kc)
                kv.append(key)
                cast_dma(wvb_d[:, :, kc, :].rearrange("b p n -> p b n"),
                         cm_w_in_d[kc * 128:(kc + 1) * 128, CMI:2 * CMI].rearrange("p (b n) -> p b n", n=512), "cw_cmv", key)
            for fc in range(NFC):
                key = ("wcob", fc)
                ko.append(key)
                cast_dma(wcob_d[:, :, fc, :].rearrange("c p n -> p c n"),
                         cm_w_out_d[fc * 128:(fc + 1) * 128, :].rearrange("p (c n) -> p c n", n=128), "cw_cmo", key)
            wkeys["wub"], wkeys["wvb"], wkeys["wcob"] = ku, kv, ko

        conv_ffn(0, 0)

        bg_jobs = []

        def conv_rest():
            if stages >= 2:
                conv_hg()
            if stages >= 3:
                bg_jobs.append(lambda: conv_ffn(0, 1))
            if stages >= 4:
                bg_jobs.append(lambda: conv_ffn(1, 0))
            if stages >= 5:
                bg_jobs.append(conv_cm)
            if stages >= 6:
                bg_jobs.append(lambda: conv_ffn(1, 1))

        def run_bg_job():
            if bg_jobs:
                bg_jobs.pop(0)()

        def load_T(dst, src_rows, nrows, func=None, key=None, srcview=None):
            P.dma(lambda e: e.dma_start(out=small_in[0:nrows, :], in_=src_rows), "small_in", writes=["small_in"])
            if func is not None:
                P.op("act", lambda e: e.activation(out=small_in[0:nrows, :], in_=small_in[0:nrows, :], func=func),
                     reads=["small_in"], writes=["small_in"])
            P.op("pe", lambda e: e.transpose(pst[0][:, 0:nrows], small_in[0:nrows, :], ident_f[0:nrows, 0:nrows]),
                 reads=["small_in", "ident_f"], writes=["pst0"])
            src = pst[0][:, 0:nrows] if srcview is None else srcview(pst[0][:, 0:nrows])
            P.op("dve", lambda e: e.tensor_copy(dst, src), reads=["pst0"], writes=[key])

        load_T(scT[:, :, 0:NB].rearrange("p k n -> p n k"), cc_d[:, :], NB * NCH, func=AF.Silu, key="scT",
               srcview=lambda a: a.rearrange("p (n k) -> p n k", k=NCH))
        load_T(modbT[:, 0:72], mod_b_d[0:72, :], 72, key="modbT")
        load_T(modbT[:, 72:144], mod_b_d[72:144, :], 72, key="modbT")
        load_T(lngT[:, :], ln_g_d[:, :], 48, key="lngT")
        load_T(lnbT[:, :], ln_b_d[:, :], 48, key="lnbT")

        P.op("pool", lambda e: e.memset(ones_h[:, :], 1.0 / 128.0), writes=["ones_h"])
        P.dma(lambda e: e.dma_start(out=maskf[:, :], in_=maskf_d[:, :]), "maskf", writes=["maskf"])
        P.dma(lambda e: e.dma_start(out=maskb[:, :], in_=maskb_d[:, :]), "maskb", writes=["maskb"])
        P.dma(lambda e: e.dma_start(out=nwc[:, :], in_=hg_nw_d[:, :]), "nwc", writes=["nwc"])
        rm32 = aview(0, (TT,), F32)
        P.dma(lambda e: e.dma_start(out=rm32, in_=rmask_d[:, :]), "rm32", writes=["rm32"])
        P.op("dve", lambda e: e.tensor_copy(rmask[:, :], rm32), reads=["rm32"], writes=["rmask"])
        if dbg == 10:
            P.finish(); P.emit(); return nc
        load_T(lbT[:, :], hg_lb_d[:, :], 48, key="lbT")
        load_T(vgT[:, :], cm_vg_d[:, :], NFC, key="vgT")
        load_T(vbT[:, :], cm_vb_d[:, :], NFC, key="vbT")
        P.op("act", lambda e: e.activation(out=lbT[:, :], in_=lbT[:, :], func=AF.Exp), reads=["lbT"], writes=["lbT"])
        for dr in range(2):
            e0, e1, e2 = (lbT[:, (dr * 3 + i) * 8:(dr * 3 + i + 1) * 8] for i in range(3))
            P.op("dve", lambda e, dr=dr, e0=e0, e1=e1: e.tensor_tensor(out=omlb[:, dr, :], in0=e0, in1=e1, op=ALU.add),
                 reads=["lbT"], writes=["omlb"])
            P.op("dve", lambda e, dr=dr, e2=e2: e.tensor_tensor(out=omlb[:, dr, :], in0=omlb[:, dr, :], in1=e2, op=ALU.add),
                 reads=["lbT", "omlb"], writes=["omlb"])
            P.op("dve", lambda e, dr=dr: e.reciprocal(omlb[:, dr, :], omlb[:, dr, :]), reads=["omlb"], writes=["omlb"])
            P.op("dve", lambda e, dr=dr, e0=e0: e.tensor_tensor(out=lbt[:, dr, :], in0=omlb[:, dr, :], in1=e0, op=ALU.mult),
                 reads=["lbT", "omlb"], writes=["lbt"])
            P.op("dve", lambda e, dr=dr: e.tensor_scalar(out=omlb[:, dr, :], in0=lbt[:, dr, :], scalar1=-1.0, scalar2=1.0,
                                                        op0=ALU.mult, op1=ALU.add), reads=["lbt"], writes=["omlb"])
        if dbg == 11:
            P.finish(); P.emit(); return nc
        bsrep = aview(16384, (8, 128), F32)
        rsrep = aview(20480, (8, 128), F32)
        rtab = aview(24576, (NFC, 128), F32)
        onesf = aview(40960, (128,), F32)
        P.op("pool", lambda e: e.memset(onesf, 1.0), writes=["onesf"])
        P.dma(lambda e: e.dma_start(out=bsrep, in_=cm_bs_d.partition_broadcast(128)), "bsrep", writes=["bsrep"])
        if dbg == 12:
            P.finish(); P.emit(); return nc
        for g in range(8):
            P.dma(lambda e, g=g: e.dma_start(out=small_in[:, :], in_=cm_ws_d[g]), "small_in", writes=["small_in"])
            P.op("pe", lambda e: e.transpose(pst[0][:, 0:128], small_in[:, :], ident_f[:, :]),
                 reads=["small_in", "ident_f"], writes=["pst0"])
            P.op("dve", lambda e, g=g: e.tensor_copy(wsT[:, g, :], pst[0][:, 0:128]), reads=["pst0"], writes=["wsT"])
            P.op("pe", lambda e, g=g: e.matmul(pst[1][:, 0:128], lhsT=ones_h[:, :], rhs=wsT[:, g, :], start=True, stop=True),
                 reads=["wsT", "ones_h"], writes=["pst1"])
            P.op("dve", lambda e, g=g: e.tensor_scalar_mul(rsrep[:, g, :], pst[1][:, 0:128], 128.0), reads=["pst1"], writes=["rsrep"])
        if dbg == 13:
            P.finish(); P.emit(); return nc
        for fc in range(NFC):
            P.op("dve", lambda e, fc=fc: e.scalar_tensor_tensor(
                out=rtab[:, fc, :], in0=rsrep[:, fc // 3, :], scalar=vbT[:, fc:fc + 1], in1=bsrep[:, fc // 3, :],
                op0=ALU.mult, op1=ALU.add), reads=["rsrep", "bsrep", "vbT"], writes=["rtab"])
        P.dma(lambda e: e.dma_start(out=r_d[:, :], in_=rtab.rearrange("p a b -> p (a b)")), "rtab", reads=["rtab"], writes=["r_d"])
        P.fence()
        if dbg == 1:
            P.finish(); P.emit(); return nc
        mw32 = aview(16384, (2, 4608), F32)
        mps = [pg[0], pg[1]]
        cnt = 0
        for l in range(DEPTH):
            for kc in range(NCH):
                for hf in range(2):
                    slot = cnt % 2
                    cnt += 1
                    P.dma(lambda e, l=l, kc=kc, hf=hf, slot=slot: e.dma_start(
                        out=mw32[:, slot, :], in_=mod_w_d[l, kc * 128:(kc + 1) * 128, hf * 4608:(hf + 1) * 4608]),
                        ("mw", slot), writes=[("mw", slot)])
                    for f in range(36):
                        P.op("pe", lambda e, kc=kc, hf=hf, slot=slot, f=f: e.matmul(
                            mps[hf][:, f * 8:(f + 1) * 8], lhsT=mw32[:, slot, f * 128:(f + 1) * 128],
                            rhs=scT[:, kc, :], start=(kc == 0 and f == 0), stop=(kc == NCH - 1),
                            skip_group_check=True),
                            reads=[("mw", slot), "scT"], writes=[("mps", hf)], signal=(f == 35))
            for hf in range(2):
                P.op("dve", lambda e, l=l, hf=hf: e.tensor_tensor(
                    out=modT[l][:, hf * 36:(hf + 1) * 36, :],
                    in0=mps[hf][:, 0:288].rearrange("p (f n) -> p f n", n=8),
                    in1=modbT[:, l * 72 + hf * 36: l * 72 + (hf + 1) * 36].unsqueeze(2).to_broadcast([128, 36, 8]),
                    op=ALU.add), reads=[("mps", hf), "modbT"], writes=[("modT", l)])
            for k in range(3):
                gf = (0.5 if k != 1 else 1.0) / DN_ALPHA
                P.op("dve", lambda e, l=l, k=k: e.tensor_scalar_add(
                    modT[l][:, (3 * k + 1) * 8:(3 * k + 2) * 8, :], modT[l][:, (3 * k + 1) * 8:(3 * k + 2) * 8, :], 1.0),
                    reads=[("modT", l)], writes=[("modT", l)])
                P.op("dve", lambda e, l=l, k=k, gf=gf: e.tensor_scalar_mul(
                    modT[l][:, (3 * k + 2) * 8:(3 * k + 3) * 8, :], modT[l][:, (3 * k + 2) * 8:(3 * k + 3) * 8, :], gf),
                    reads=[("modT", l)], writes=[("modT", l)])
        P.fence()

        if dbg == 2:
            P.finish(); P.emit(); return nc
        cast_rr = [0]

        def cast(out, in_, reads, writes):
            i = cast_rr[0] % 3
            cast_rr[0] += 1
            if i == 0:
                P.op("act", lambda e: e.copy(out, in_), reads=reads, writes=writes)
            elif i == 1:
                P.op("dve", lambda e: e.tensor_copy(out, in_), reads=reads, writes=writes)
            else:
                P.op("pool", lambda e: e.tensor_copy(out, in_), reads=reads, writes=writes)

        pp = [0]

        def pp_slot():
            s = pp[0] % 2
            pp[0] += 1
            return s

        def prep_ffn(l, s):
            for kc in range(NCH):
                sl = pp_slot()
                P.dma(lambda e, kc=kc, sl=sl: e.dma_start(out=st32[:, sl, :], in_=ffn_w_in_d[l, s, kc * 128:(kc + 1) * 128, :]),
                      ("st32", sl), writes=[("st32", sl)])
                o16 = st16[:, sl, :].rearrange("p (j n) -> p j n", n=256)
                for half in range(2):
                    cast(o16[:, :, half * 128:(half + 1) * 128],
                         st32[:, sl, half * FH:(half + 1) * FH].rearrange("p (j n) -> p j n", n=128),
                         reads=[("st32", sl)], writes=[("st16", sl, half)])
                P.dma(lambda e, kc=kc, sl=sl, o16=o16: e.dma_start(
                    out=winb_d[l][s][:, :, kc, :].rearrange("j p n -> p j n"), in_=o16),
                    ("st16", sl), reads=[("st16", sl, 0), ("st16", sl, 1)], writes=[("winb", l, s)])
            for j0 in range(0, NJ, 5):
                nj = min(5, NJ - j0)
                sl = pp_slot()
                i32 = st32[:, sl, 0:nj * D].rearrange("p (j n) -> p j n", n=D)
                i16 = st16[:, sl, 0:nj * D].rearrange("p (j n) -> p j n", n=D)
                P.dma(lambda e, j0=j0, nj=nj, i32=i32: e.dma_start(
                    out=i32, in_=ffn_w_out_d[l, s, j0 * 128:(j0 + nj) * 128, :].rearrange("(j p) n -> p j n", p=128)),
                    ("st32", sl), writes=[("st32", sl)])
                cast(i16, i32, reads=[("st32", sl)], writes=[("st16", sl, 0), ("st16", sl, 1)])
                for jj in range(nj):
                    P.dma(lambda e, j0=j0, jj=jj, i16=i16: e.dma_start(
                        out=woutb_d[l][s][:, :, j0 + jj, :].rearrange("c p n -> p c n"),
                        in_=i16[:, jj, :].rearrange("p (c n) -> p c n", n=128)),
                        ("st16", sl, jj), reads=[("st16", sl, 0), ("st16", sl, 1)], writes=[("woutb", l, s)])

        def prep_rows(src_rows, ncols, cast_fn, stores):
            sl = pp_slot()
            P.dma(lambda e: e.dma_start(out=st32[:, sl, 0:ncols], in_=src_rows), ("st32", sl), writes=[("st32", sl)])
            cast_fn(st32[:, sl, 0:ncols], st16[:, sl, 0:ncols], sl)
            for i, (dst, srcv, wkey) in enumerate(stores):
                P.dma(lambda e, dst=dst, srcv=srcv: e.dma_start(out=dst, in_=srcv(st16[:, sl, 0:ncols])),
                      ("st16", sl, i), reads=[("st16", sl, 0), ("st16", sl, 1)], writes=[wkey])

        def plain_cast(i32, i16, sl):
            cast(i16, i32, reads=[("st32", sl)], writes=[("st16", sl, 0), ("st16", sl, 1)])

        def prep_rows3(src_rows3, a, b, cast_fn, stores):
            sl = pp_slot()
            P.dma(lambda e: e.dma_start(out=st32[:, sl, 0:a * b].rearrange("p (a b) -> p a b", a=a), in_=src_rows3),
                  ("st32", sl), writes=[("st32", sl)])
            cast_fn(st32[:, sl, 0:a * b], st16[:, sl, 0:a * b], sl)
            for i, (dst, srcv, wkey) in enumerate(stores):
                P.dma(lambda e, dst=dst, srcv=srcv: e.dma_start(out=dst, in_=srcv(st16[:, sl, 0:a * b])),
                      ("st16", sl, i), reads=[("st16", sl, 0), ("st16", sl, 1)], writes=[wkey])

        def prep_hg():
            for kc in range(NCH):
                def cst(i32, i16, sl):
                    cast(i16.rearrange("p (h s n) -> p h s n", h=NHEAD, s=5),
                         i32.rearrange("p (s h n) -> p h s n", s=5, h=NHEAD),
                         reads=[("st32", sl)], writes=[("st16", sl, 0), ("st16", sl, 1)])
                prep_rows(hg_w_in_d[kc * 128:(kc + 1) * 128, :], 5 * D, cst,
                          [(whb_d[:, :, kc, :].rearrange("h p n -> p h n"),
                            lambda v: v.rearrange("p (h n) -> p h n", h=NHEAD), "whb")])
            for h0 in range(0, NHEAD, 4):
                prep_rows3(hg_w_out_d[h0 * 128:(h0 + 4) * 128, :].rearrange("(h p) n -> p h n", p=128), 4, D, plain_cast,
                           [(whob_d[h0:h0 + 4].rearrange("h p n -> p h n"),
                             lambda v: v.rearrange("p (h n) -> p h n", h=4), "whob")])

        def prep_cm():
            for kc in range(NCH):
                prep_rows(cm_w_in_d[kc * 128:(kc + 1) * 128, 0:CMI], CMI, plain_cast,
                          [(wub_d[:, :, kc, :].rearrange("f p n -> p f n"),
                            lambda v: v.rearrange("p (f n) -> p f n", n=128), "wub")])
                prep_rows(cm_w_in_d[kc * 128:(kc + 1) * 128, CMI:2 * CMI], CMI, plain_cast,
                          [(wvb_d[:, :, kc, :].rearrange("b p n -> p b n"),
                            lambda v: v.rearrange("p (b n) -> p b n", n=512), "wvb")])
            for j0 in range(0, NFC, 4):
                prep_rows3(cm_w_out_d[j0 * 128:(j0 + 4) * 128, :].rearrange("(j p) n -> p j n", p=128), 4, D, plain_cast,
                           [(wcob_d[:, :, j0 + jj, :].rearrange("c p n -> p c n"),
                             (lambda v, jj=jj: v[:, jj * D:(jj + 1) * D].rearrange("p (c n) -> p c n", n=128)), "wcob")
                            for jj in range(4)])

        zsm = sb("zsm", (128, 2048), BF16)

        def layer_norm_tile(xv, xkeys, n, gcol, bcol, eps_col, zbuf=None):
            for c in range(NCH):
                b = c % 2
                zb = zsm[:, b * 512:b * 512 + n]
                zq = zsm[:, 1024 + b * 512:1024 + b * 512 + n]
                P.op("act", lambda e, c=c, zb=zb: e.activation(out=zb, in_=xv(c), func=AF.Copy),
                     reads=[xkeys[c]], writes=[("zb", b)])
                P.op("pool", lambda e, c=c, zq=zq: e.tensor_tensor(out=zq, in0=xv(c), in1=xv(c), op=ALU.mult),
                     reads=[xkeys[c]], writes=[("zq", b)])
                P.op("pe", lambda e, c=c, zb=zb: e.matmul(pst[0][:, 0:n], lhsT=ones_b[:, :], rhs=zb, start=(c == 0), stop=(c == NCH - 1)),
                     reads=[("zb", b), "ones_b"], writes=["pst0"])
                P.op("pe", lambda e, c=c, zq=zq: e.matmul(pst[1][:, 0:n], lhsT=ones_b[:, :], rhs=zq, start=(c == 0), stop=(c == NCH - 1)),
                     reads=[("zq", b), "ones_b"], writes=["pst1"])
            var, rstd = (stat[:, i, 0:n] for i in range(2))
            m2 = tb[:, 0, 0:n]
            P.op("act", lambda e: e.activation(out=m2, in_=pst[0][:, 0:n], func=AF.Square), reads=["pst0"], writes=[("tb", 0)])
            P.op("dve", lambda e: e.tensor_tensor(out=var, in0=pst[1][:, 0:n], in1=m2, op=ALU.subtract),
                 reads=["pst1", ("tb", 0)], writes=["var"])
            P.op("act", lambda e: e.activation(out=var, in_=var, func=AF.Ln, bias=epsc[:, eps_col:eps_col + 1], scale=1.0),
                 reads=["var", "epsc"], writes=["var"])
            P.op("act", lambda e: e.activation(out=rstd, in_=var, func=AF.Exp, scale=-0.5), reads=["var"], writes=["rstd"])
            for c in range(NCH):
                b = c % 2
                t = tb[:, b, 0:n]
                P.op("dve", lambda e, c=c, t=t: e.tensor_tensor(out=t, in0=xv(c), in1=pst[0][:, 0:n], op=ALU.subtract),
                     reads=[xkeys[c], "pst0"], writes=[("tb", b)])
                P.op("pool", lambda e, t=t: e.tensor_tensor(out=t, in0=t, in1=rstd, op=ALU.mult),
                     reads=[("tb", b), "rstd"], writes=[("tb", b)])
                P.op("act", lambda e, c=c, t=t: e.activation(out=xv(c), in_=t, func=AF.Identity,
                                                            bias=lnbT[:, bcol + c:bcol + c + 1], scale=lngT[:, gcol + c:gcol + c + 1]),
                     reads=[("tb", b), "lngT", "lnbT"], writes=[xkeys[c]])

        wctr = {"win": 0, "wout": 0, "pgu": 0, "py": 0}

        hmod_ready = [False]
        pending_ln = []

        def flush_ln():
            while pending_ln:
                sts, lk = pending_ln.pop(0)
                for st in sts:
                    layer_norm_tile(st["xv"], st["keys"], st["n"], lk, lk, 0)

        def ffn_hmod(l, s, subtiles):
            want = {kk for st in subtiles for kk in st["keys"]}
            if any(kk in want for sts, _ in pending_ln for st in sts for kk in st["keys"]):
                flush_ln()
            k = 0 if s == 0 else 2
            o = 0
            for si, st in enumerate(subtiles):
                n, col = st["n"], st["col"]
                for c in range(NCH):
                    P.op("act", lambda e, st=st, c=c, o=o, n=n, col=col: e.activation(
                        out=hmod[:, c, o:o + n], in_=st["xv"](c), func=AF.Identity,
                        bias=modT[l][:, (3 * k) * 8 + c, col:col + 1], scale=modT[l][:, (3 * k + 1) * 8 + c, col:col + 1]),
                        reads=[st["keys"][c], ("modT", l)], writes=[("hmod", si)])
                o += n

        def ffn_group(l, s, subtiles, prefetch=None):
            k = 0 if s == 0 else 2
            lnk = (l * 3 + k) * 8
            offs = []
            o = 0
            for st in subtiles:
                offs.append(o)
                o += st["n"]
            assert o <= 1024
            if not hmod_ready[0]:
                ffn_hmod(l, s, subtiles)
            hmod_ready[0] = False
            for j in range(NJ):
                if j == 2:
                    flush_ln()
                    run_bg_job()
                slot = wctr["win"] % 3
                wctr["win"] += 1
                P.dma(lambda e, j=j, slot=slot: e.dma_start(out=win[:, slot, :, :], in_=winb_d[l][s][j]),
                      ("win", slot), reads=wkeys[("winb", l, s)], writes=[("win", slot)])
                for si, st in enumerate(subtiles):
                    n, o = st["n"], offs[si]
                    b = wctr["pgu"] % 2
                    wctr["pgu"] += 1
                    for c in range(NCH):
                        P.op("pe", lambda e, c=c, slot=slot, b=b, o=o, n=n: e.matmul(
                            pg[b][:, 0:n], lhsT=win[:, slot, c, 0:128], rhs=hmod[:, c, o:o + n],
                            start=(c == 0), stop=(c == NCH - 1)),
                            reads=[("win", slot), ("hmod", si)], writes=[("pg", b)], signal=(c == NCH - 1))
                    for c in range(NCH):
                        P.op("pe", lambda e, c=c, slot=slot, b=b, o=o, n=n: e.matmul(
                            pu[b][:, 0:n], lhsT=win[:, slot, c, 128:256], rhs=hmod[:, c, o:o + n],
                            start=(c == 0), stop=(c == NCH - 1)),
                            reads=[("win", slot), ("hmod", si)], writes=[("pu", b)], signal=(c == NCH - 1))
                    P.op("act", lambda e, b=b, n=n: e.activation(out=sgb[:, b, 0:n], in_=pg[b][:, 0:n], func=AF.Silu),
                         reads=[("pg", b)], writes=[("sg", b)])
                    P.op("dve", lambda e, b=b, n=n, o=o, j=j: e.tensor_tensor(
                        out=hid[:, j, o:o + n], in0=sgb[:, b, 0:n], in1=pu[b][:, 0:n], op=ALU.mult),
                        reads=[("sg", b), ("pu", b)], writes=[("hid", si)])
            if prefetch is not None:
                ffn_hmod(*prefetch)
                hmod_ready[0] = True
            for c in range(NCH):
                slot = wctr["wout"] % 2
                wctr["wout"] += 1
                P.dma(lambda e, c=c, slot=slot: e.dma_start(out=wout[:, slot, :, :], in_=woutb_d[l][s][c]),
                      ("wout", slot), reads=wkeys[("woutb", l, s)], writes=[("wout", slot)])
                for si, st in enumerate(subtiles):
                    n, o, col = st["n"], offs[si], st["col"]
                    b = wctr["py"] % 2
                    wctr["py"] += 1
                    for j in range(NJ):
                        P.op("pe", lambda e, j=j, slot=slot, b=b, o=o, n=n: e.matmul(
                            py[b][:, 0:n], lhsT=wout[:, slot, j, :], rhs=hid[:, j, o:o + n],
                            start=(j == 0), stop=(j == NJ - 1)),
                            reads=[("wout", slot), ("hid", si)], writes=[("py", b)], signal=(j == NJ - 1))
                    P.op("dve", lambda e, st=st, c=c, b=b, n=n, col=col: e.scalar_tensor_tensor(
                        out=st["xv"](c), in0=py[b][:, 0:n], scalar=modT[l][:, (3 * k + 2) * 8 + c, col:col + 1],
                        in1=st["xv"](c), op0=ALU.mult, op1=ALU.add),
                        reads=[("py", b), st["keys"][c], ("modT", l)], writes=[st["keys"][c]])
            pending_ln.append((subtiles, lnk))

        def lat_subtiles(seq, t0, t1):
            res = []
            for a in range(t0, t1, 512):
                n = min(512, t1 - a)
                res.append(dict(xv=(lambda c, a=a, n=n: x_sb[:, c, a:a + n]),
                                keys=[("x", c, a // 512) for c in range(NCH)], n=n, col=seq))
            return res

        def ctx_subtiles():
            return [dict(xv=(lambda c: ctx_sb[:, c, 0:CTX]), keys=[("ctx", c) for c in range(NCH)], n=CTX, col=NSEQ)]

        def ffn_sublayer(l, s, seq, with_ctx, nxt=None):
            groups = [lat_subtiles(seq, t0, min(T, t0 + 1024)) for t0 in range(0, T, 1024)]
            if with_ctx:
                groups.append(ctx_subtiles())
            for gi, g in enumerate(groups):
                if gi + 1 < len(groups):
                    pf = (l, s, groups[gi + 1])
                elif nxt is not None and len(groups) > 1:
                    pf = (nxt[0], nxt[1], groups[0])
                else:
                    pf = None
                ffn_group(l, s, g, prefetch=pf)

        def hgrn_sublayer(seq):
            l, k = 0, 1
            NT = TT // 128
            NCK = TT // 64
            CTXC = CTX // 64
            off = [0]

            def carve(shape, dtype):
                nb = int(np.prod(shape)) * (4 if dtype == F32 else 2)
                v = aview(off[0], shape, dtype)
                off[0] += (nb + 63) // 64 * 64
                return v
            hx = carve((NCH, TT), BF16)
            wh = carve((NCH, 640), BF16)
            who = carve((D,), BF16)
            vtok = carve((NT, 128), BF16)
            qf = carve((T,), F32)
            gb = carve((T,), BF16)
            ob = carve((T,), BF16)
            A = carve((TT,), F32)
            kin = carve((TT,), BF16)
            Sb = carve((2, 128), BF16)
            PT = carve((2, 64), BF16)
            osq = carve((512,), BF16)
            onb = carve((512,), BF16)
            tot = carve((NCK,), F32)
            B2 = stat[:, :, :].rearrange("p a b -> p (a b)")[:, 0:TT]
            Qin = sgb[:, :, :].rearrange("p a b -> p (a b)").bitcast(BF16)[:, 0:T]
            kin_tok = ctx_sb[:, :, :].rearrange("p a b -> p (a b)").bitcast(BF16)[:, 0:NT * 128].rearrange("p (t n) -> p t n", n=128)
            pstb = pst[1][:, :].bitcast(BF16)
            xkeys_all = [("x", c, st) for c in range(NCH) for st in range((T + 511) // 512)]
            ctxkeys = [("ctx", c) for c in range(NCH)]
            rngs = [(0, CTX)] + [(CTX + a, min(512, T - a)) for a in range(0, T, 512)]
            for c in range(NCH):
                P.op("act", lambda e, c=c: e.activation(
                    out=hx[:, c, 0:CTX], in_=ctx_sb[:, c, :], func=AF.Identity,
                    bias=modT[l][:, 3 * 8 + c, NSEQ:NSEQ + 1], scale=modT[l][:, 4 * 8 + c, NSEQ:NSEQ + 1]),
                    reads=[("ctx", c), ("modT", l)], writes=["hx"])
                for a in range(0, T, 512):
                    n = min(512, T - a)
                    P.op("act", lambda e, c=c, a=a, n=n: e.activation(
                        out=hx[:, c, CTX + a:CTX + a + n], in_=x_sb[:, c, a:a + n], func=AF.Identity,
                        bias=modT[l][:, 3 * 8 + c, seq:seq + 1], scale=modT[l][:, 4 * 8 + c, seq:seq + 1]),
                        reads=[("x", c, a // 512), ("modT", l)], writes=["hx"])
            pb_ctr = [0]

            def proj(sel, a, n, bank):
                for kc in range(NCH):
                    P.op("pe", lambda e, kc=kc: e.matmul(bank[:, 0:n], lhsT=wh[:, kc, sel * 128:(sel + 1) * 128],
                                                         rhs=hx[:, kc, a:a + n], start=(kc == 0), stop=(kc == NCH - 1)),
                         reads=["wh", "hx"], writes=[("bank", id(bank))], signal=(kc == NCH - 1))

            def finalize_gen(po, s0, nst, st):
                o32 = tb[:, 0, 0:nst]
                r32 = tb[:, 1, 0:nst]
                P.op("dve", lambda e: e.tensor_tensor(out=o32, in0=po[:, 0:nst], in1=ob[:, s0:s0 + nst], op=ALU.add),
                     reads=[("bank", id(po)), "ob"], writes=[("tb", 0)])
                P.op("pool", lambda e: e.tensor_tensor(out=osq[:, 0:nst], in0=o32, in1=o32, op=ALU.mult),
                     reads=[("tb", 0)], writes=["osq"])
                yield
                P.op("pe", lambda e: e.matmul(pst[1][:, 0:nst], lhsT=ones_h[:, :], rhs=osq[:, 0:nst], start=True, stop=True),
                     reads=["osq", "ones_h"], writes=["pst1"])
                P.op("act", lambda e: e.activation(out=r32, in_=pst[1][:, 0:nst], func=AF.Ln, bias=epsc[:, 2:3], scale=1.0),
                     reads=["pst1", "epsc"], writes=[("tb", 1)])
                P.op("act", lambda e: e.activation(out=r32, in_=r32, func=AF.Exp, scale=-0.5), reads=[("tb", 1)], writes=[("tb", 1)])
                yield
                P.op("dve", lambda e: e.tensor_tensor(out=o32, in0=o32, in1=r32, op=ALU.mult),
                     reads=[("tb", 0), ("tb", 1)], writes=[("tb", 0)])
                P.op("dve", lambda e: e.scalar_tensor_tensor(
                    out=onb[:, 0:nst], in0=o32, scalar=nwc[:, 0:1], in1=gb[:, s0:s0 + nst], op0=ALU.mult, op1=ALU.mult),
                    reads=[("tb", 0), "nwc", "gb"], writes=["onb"])
                yield
                for c in range(NCH):
                    bank = py[c % 2]
                    P.op("pe", lambda e, c=c, bank=bank: e.matmul(bank[:, 0:nst], lhsT=who[:, c * 128:(c + 1) * 128], rhs=onb[:, 0:nst], start=True, stop=True),
                         reads=["who", "onb"], writes=[("bank", id(bank))])
                    P.op("dve", lambda e, c=c, bank=bank: e.scalar_tensor_tensor(
                        out=x_sb[:, c, s0:s0 + nst], in0=bank[:, 0:nst], scalar=modT[l][:, 5 * 8 + c, seq:seq + 1],
                        in1=x_sb[:, c, s0:s0 + nst], op0=ALU.mult, op1=ALU.add),
                        reads=[("bank", id(bank)), ("x", c, st), ("modT", l)], writes=[("x", c, st)])
                    if c % 2 == 1:
                        yield

            deferred = []
            for h in range(NHEAD):
                P.dma(lambda e, h=h: e.dma_start(out=wh, in_=whb_d[h].rearrange("p k n -> p k n")), "wh",
                      reads=wkeys["whb"], writes=["wh"])
                P.dma(lambda e, h=h: e.dma_start(out=who, in_=whob_d[h]), "who", reads=wkeys["whob"], writes=["who"])
                for t0 in range(0, NT, 4):
                    nt = min(4, NT - t0)
                    bank = py[(t0 // 4) % 2]
                    for ti in range(nt):
                        tl = t0 + ti
                        for kc in range(NCH):
                            P.op("pe", lambda e, kc=kc, ti=ti, tl=tl, bank=bank: e.matmul(
                                bank[:, ti * 128:(ti + 1) * 128], lhsT=hx[:, kc, tl * 128:(tl + 1) * 128],
                                rhs=wh[:, kc, 128:256], start=(kc == 0), stop=(kc == NCH - 1)),
                                reads=["wh", "hx"], writes=[("bank", id(bank))], signal=(kc == NCH - 1))
                    P.op("dve", lambda e, t0=t0, nt=nt, bank=bank: e.tensor_copy(
                        vtok[:, t0:t0 + nt, :], bank[:, 0:nt * 128].rearrange("p (t n) -> p t n", n=128)),
                        reads=[("bank", id(bank))], writes=["vtok"])
                for (a, n) in rngs[1:]:
                    bank = pg[pb_ctr[0] % 2]
                    pb_ctr[0] += 1
                    proj(0, a, n, bank)
                    P.op("act", lambda e, a=a, n=n, bank=bank: e.activation(out=qf[:, a - CTX:a - CTX + n], in_=bank[:, 0:n], func=AF.Silu),
                         reads=[("bank", id(bank))], writes=["qf"])
                    bank = pg[pb_ctr[0] % 2]
                    pb_ctr[0] += 1
                    proj(4, a, n, bank)
                    P.op("act", lambda e, a=a, n=n, bank=bank: e.activation(out=gb[:, a - CTX:a - CTX + n], in_=bank[:, 0:n], func=AF.Silu),
                         reads=[("bank", id(bank))], writes=["gb"])
                for dr in (1, 0):
                    def piece_of(a):
                        return 0 if a < CTX else 1 + (a - CTX) // 512
                    cp_ctr = [0]
                    for pi, (a, n) in enumerate(rngs):
                        bank = pu[pb_ctr[0] % 2]
                        pb_ctr[0] += 1
                        proj(2 + dr, a, n, bank)
                        kA, kB, kK = ("A", pi), ("B2", pi), ("kin", pi)
                        c0, nck = a // 64, n // 64
                        Ap, Bp, Kp = A[:, a:a + n], B2[:, a:a + n], kin[:, a:a + n]
                        P.op("act", lambda e, Ap=Ap, n=n, bank=bank: e.activation(out=Ap, in_=bank[:, 0:n], func=AF.Sigmoid),
                             reads=[("bank", id(bank))], writes=[kA])
                        P.op("dve", lambda e, Ap=Ap, dr=dr, h=h: e.tensor_scalar(out=Ap, in0=Ap, scalar1=omlb[:, dr, h:h + 1], scalar2=lbt[:, dr, h:h + 1],
                                                                               op0=ALU.mult, op1=ALU.add), reads=[kA, "omlb", "lbt"], writes=[kA])
                        P.op("pool", lambda e, Ap=Ap, Kp=Kp: e.tensor_scalar(out=Kp, in0=Ap, scalar1=-1.0, scalar2=1.0, op0=ALU.mult, op1=ALU.add),
                             reads=[kA], writes=[kK])
                    for pi, (a, n) in enumerate(rngs):
                        kA, kB, kK = ("A", pi), ("B2", pi), ("kin", pi)
                        c0, nck = a // 64, n // 64
                        Ap, Bp, Kp = A[:, a:a + n], B2[:, a:a + n], kin[:, a:a + n]
                        P.op("act", lambda e, Ap=Ap: e.activation(out=Ap, in_=Ap, func=AF.Ln), reads=[kA], writes=[kA])
                        P.op("dve", lambda e, Ap=Ap, Bp=Bp, a=a, n=n: e.tensor_tensor_scan(out=Bp, data0=rmask[:, a:a + n], data1=Ap, initial=0.0,
                                                                                         op0=ALU.mult, op1=ALU.add),
                             reads=[kA, "rmask"], writes=[kB])
                        P.op("act", lambda e, Bp=Bp, c0=c0, nck=nck: e.activation(
                            out=tot[:, c0:c0 + nck].unsqueeze(2), in_=Bp.rearrange("p (c t) -> p c t", t=64)[:, :, 63:64], func=AF.Exp),
                            reads=[kB], writes=[("dt", pi)])
                        if dr == 1 and not DBG_SKIP_BWD_SUB:
                            P.op("dve", lambda e, Ap=Ap, Bp=Bp: e.tensor_tensor(out=Bp, in0=Ap, in1=Bp, op=ALU.subtract), reads=[kA, kB], writes=[kB])
                        P.op("act", lambda e, Ap=Ap, Bp=Bp: e.activation(out=Ap, in_=Bp, func=AF.Exp, scale=-1.0), reads=[kB, kA], writes=[kA])
                        P.op("pool", lambda e, Ap=Ap, Kp=Kp: e.tensor_tensor(out=Kp, in0=Kp, in1=Ap, op=ALU.mult), reads=[kK, kA], writes=[kK])
                        P.op("act", lambda e, Bp=Bp: e.activation(out=Bp, in_=Bp, func=AF.Exp), reads=[kB], writes=[kB])
                        if a >= CTX:
                            P.op("dve", lambda e, Bp=Bp, a=a, n=n: e.tensor_tensor(out=Qin[:, a - CTX:a - CTX + n], in0=qf[:, a - CTX:a - CTX + n], in1=Bp, op=ALU.mult),
                                 reads=["qf", kB], writes=[("Qin", pi)])
                        t_lo, t_hi = a // 128, (a + n) // 128
                        for t0 in range(t_lo, t_hi, 4):
                            nt = min(4, t_hi - t0)
                            half = cp_ctr[0] % 2
                            cp_ctr[0] += 1
                            pk = ("pstb", half)
                            for ti in range(nt):
                                tl = t0 + ti
                                P.op("pe", lambda e, ti=ti, tl=tl, half=half: e.transpose(
                                    pstb[:, half * 512 + ti * 128:half * 512 + (ti + 1) * 128], kin[:, tl * 128:(tl + 1) * 128], ident_b[:, :]),
                                    reads=[kK, "ident_b"], writes=[pk], signal=(ti == nt - 1))
                            src = pstb[:, half * 512:half * 512 + nt * 128].rearrange("p (t n) -> p t n", n=128)
                            if half == 0:
                                P.op("dve", lambda e, t0=t0, nt=nt, src=src: e.tensor_copy(kin_tok[:, t0:t0 + nt, :], src),
                                     reads=[pk], writes=ctxkeys + [("ktok", t0 + q_) for q_ in range(nt)])
                            else:
                                P.op("act", lambda e, t0=t0, nt=nt, src=src: e.copy(kin_tok[:, t0:t0 + nt, :], src),
                                     reads=[pk], writes=ctxkeys + [("ktok", t0 + q_) for q_ in range(nt)])
                    if dbg == 77 and h == 0 and dr == 1:
                        P.fence()
                        P.dma(lambda e: e.dma_start(out=dbg_d[0], in_=B2), "dbg0", reads=[], writes=["dbg0"])
                        P.dma(lambda e: e.dma_start(out=dbg_d[1], in_=A), "dbg1", reads=[], writes=["dbg1"])
                        P.dma(lambda e: e.dma_start(out=dbg_d[2][:, 0:NCK], in_=tot), "dbg2", reads=[], writes=["dbg2"])
                        P.fence()
                        P.dead = True
                    order = list(range(NCK)) if dr == 0 else (list(range(CTXC - 1, -1, -1)) + list(range(NCK - 1, CTXC - 1, -1)))
                    mask = maskf if dr == 0 else maskb
                    P.op("pool", lambda e: e.memset(Sb[:, 0, :], 0.0), writes=[("S", 0)])
                    done_in_st = {}

                    def issue_sT(idx2):
                        ck2 = order[idx2]
                        if ck2 < CTXC:
                            return
                        a2 = ck2 * 64
                        tl2 = a2 - CTX
                        pb2 = (ck2 % 2) * 64
                        r = idx2 % 8
                        r2 = idx2 % 2
                        pi2 = piece_of(a2)
                        P.op("pe", lambda e: e.matmul(
                            pst[0][pb2:pb2 + 64, r * 64:(r + 1) * 64], lhsT=kin[:, a2:a2 + 64], rhs=Qin[:, tl2:tl2 + 64], start=True, stop=True),
                            reads=[("kin", pi2), ("Qin", pi2)], writes=[("sT", r)])
                        P.op("dve", lambda e, mask=mask: e.tensor_tensor(
                            out=PT[pb2:pb2 + 64, r2, :], in0=pst[0][pb2:pb2 + 64, r * 64:(r + 1) * 64], in1=mask[pb2:pb2 + 64, :], op=ALU.mult),
                            reads=[("sT", r), "maskf", "maskb"], writes=[("PT", r2)])
                    issue_sT(0)
                    for idx, ck in enumerate(order):
                        if deferred:
                            if next(deferred[0], "END") == "END":
                                deferred.pop(0)
                        if idx + 1 < len(order):
                            issue_sT(idx + 1)
                        a = ck * 64
                        tile = ck // 2
                        pb = (ck % 2) * 64
                        is_lat = ck >= CTXC
                        tl = a - CTX
                        sc, sn = idx % 2, (idx + 1) % 2
                        pS = pu[idx % 2]
                        if is_lat:
                            st = tl // 512
                            po = pg[st % 2]
                            r2 = idx % 2
                            o0 = tl % 512
                            P.op("pe", lambda e, pb=pb, r2=r2, tile=tile, po=po, o0=o0: e.matmul(
                                po[:, o0:o0 + 64], lhsT=vtok[pb:pb + 64, tile, :], rhs=PT[pb:pb + 64, r2, :], start=True, stop=False),
                                reads=["vtok", ("PT", r2)], writes=[("bank", id(po))], signal=False)
                        P.op("pe", lambda e, pb=pb, tile=tile, pS=pS: e.matmul(
                            pS[:, 0:128], lhsT=kin_tok[pb:pb + 64, tile, :], rhs=vtok[pb:pb + 64, tile, :], start=True, stop=False),
                            reads=[("ktok", tile), "vtok"], writes=[("bank", id(pS))], signal=False)
                        if is_lat:
                            P.op("pe", lambda e, sc=sc, tl=tl, po=po, o0=o0: e.matmul(
                                po[:, o0:o0 + 64], lhsT=Sb[:, sc, :], rhs=Qin[:, tl:tl + 64], start=False, stop=True),
                                reads=[("S", sc), ("Qin", piece_of(a))], writes=[("bank", id(po))])
                        P.op("pe", lambda e, sc=sc, pS=pS: e.matmul(pS[:, 0:128], lhsT=ident_b[:, :], rhs=Sb[:, sc, :], start=False, stop=True),
                             reads=[("S", sc), "ident_b"], writes=[("bank", id(pS))])
                        if dr == 0:
                            dck = ck
                        else:
                            dck = order[idx + 1] if idx + 1 < len(order) else ck
                        P.op("dve", lambda e, sn=sn, pS=pS, dck=dck: e.tensor_scalar_mul(Sb[:, sn, :], pS[:, 0:128], tot[:, dck:dck + 1]),
                             reads=[("bank", id(pS)), ("dt", piece_of(dck * 64))], writes=[("S", sn)])
                        if dbg == 78 and h == 0 and dr == 1 and idx == len(order) - 1:
                            P.op("act", lambda e, po=po: e.copy(ob[:, 0:512], po[:, 0:512]), reads=[("bank", id(po))], writes=["ob"])
                            P.fence()
                            P.dma(lambda e: e.dma_start(out=dbgb_d[0][:, 0:T], in_=ob), "dbg0", reads=[], writes=["dbg0"])
                            P.dma(lambda e: e.dma_start(out=dbgb_d[1][:, 0:T], in_=Qin), "dbg1", reads=[], writes=["dbg1"])
                            P.dma(lambda e: e.dma_start(out=dbgb_d[2], in_=kin), "dbg2", reads=[], writes=["dbg2"])
                            P.fence()
                            P.dead = True
                        if is_lat:
                            done_in_st[st] = done_in_st.get(st, 0) + 1
                            nst = min(512, T - st * 512)
                            if done_in_st[st] == nst // 64:
                                s0 = st * 512
                                if dr == 1:
                                    P.op("act", lambda e, po=po, s0=s0, nst=nst: e.copy(ob[:, s0:s0 + nst], po[:, 0:nst]),
                                         reads=[("bank", id(po))], writes=["ob"])
                                else:
                                    while deferred:
                                        if next(deferred[0], "END") == "END":
                                            deferred.pop(0)
                                    deferred.append(finalize_gen(po, s0, nst, st))
                    while deferred:
                        if next(deferred[0], "END") == "END":
                            deferred.pop(0)
            P.fence()
            zbuf = aview(0, (NCH, 1024), BF16)
            lnk = (l * 3 + k) * 8
            for stl in lat_subtiles(seq, 0, T):
                layer_norm_tile(stl["xv"], stl["keys"], stl["n"], lnk, lnk, 0, zbuf=zbuf)

        def cm_sublayer(seq):
            l, k = 1, 1
            hmc = aview(0, (NCH, 512), BF16)
            wvs2 = [aview(8192, (NCH, 512), BF16),
                    ctx_sb[:, :, :].rearrange("p a b -> p (a b)").bitcast(BF16)[:, 0:NCH * 512].rearrange("p (k n) -> p k n", k=NCH)]
            vt = aview(16384, (4, CMI), BF16)
            ub = aview(40960, (NFC, 512), BF16)
            wus = aview(65536, (2, NCH, 128), BF16)
            wos2 = [aview(69632, (NFC, 128), BF16),
                    stat[:, 2:5, :].rearrange("p a b -> p (a b)").bitcast(BF16)[:, 0:NFC * 128].rearrange("p (f n) -> p f n", f=NFC)]
            dctr = [0, 0]
            rt = aview(75776, (NFC, 128), F32)
            zbuf = aview(16384, (NCH, 1024), BF16)
            bst = cmst[:, 0:144].rearrange("p (c s) -> p c s", c=4)
            mv = cmst[:, 144:152].rearrange("p (c s) -> p c s", c=4)
            sd = cmst[:, 152:156]
            lnk = (l * 3 + k) * 8
            P.dma(lambda e: e.dma_start(out=rt.rearrange("p a b -> p (a b)"), in_=r_d[:, :]), "rt", reads=["r_d"], writes=["rt"])
            bctr = [0]

            def gelu_from(bank, out, okeys):
                bk = ("bank", id(bank))
                P.op("act", lambda e: e.activation(out=out, in_=bank[:, :], func=AF.Gelu_apprx_tanh), reads=[bk], writes=okeys)

            def cm_hmc(st):
                s0 = st * 512
                for c in range(NCH):
                    P.op("act", lambda e, c=c, s0=s0: e.activation(
                        out=hmc[:, c, :], in_=x_sb[:, c, s0:s0 + 512], func=AF.Identity,
                        bias=modT[l][:, 3 * 8 + c, seq:seq + 1], scale=modT[l][:, 4 * 8 + c, seq:seq + 1]),
                        reads=[("x", c, st), ("modT", l)], writes=["hmc"])

            wvs_issued = {}

            def issue_wvs(st_, cb_):
                if (st_, cb_) in wvs_issued:
                    return wvs_issued[(st_, cb_)]
                wsl_ = dctr[0] % 2
                dctr[0] += 1
                wv_ = wvs2[wsl_]
                P.dma(lambda e: e.dma_start(out=wv_, in_=wvb_d[cb_]), ("wvs", wsl_), reads=wkeys["wvb"],
                      writes=[("wvs", wsl_)] + ([("ctx", c_) for c_ in range(NCH)] if wsl_ == 1 else []))
                wvs_issued[(st_, cb_)] = wsl_
                return wsl_

            nsub = (T + 511) // 512
            cm_hmc(0)
            for st in range(nsub):
                s0 = st * 512
                for cb in range(6):
                    wsl = issue_wvs(st, cb)
                    wvs = wvs2[wsl]
                    for ch in range(4):
                        bank = pg[bctr[0] % 2]
                        bctr[0] += 1
                        for kc in range(NCH):
                            P.op("pe", lambda e, kc=kc, ch=ch, bank=bank, wvs=wvs: e.matmul(
                                bank[:, :], lhsT=hmc[:, kc, ch * 128:(ch + 1) * 128], rhs=wvs[:, kc, :], start=(kc == 0), stop=(kc == NCH - 1)),
                                reads=["hmc", ("wvs", wsl)], writes=[("bank", id(bank))], signal=(kc == NCH - 1))
                        gelu_from(bank, vt[:, ch, cb * 512:(cb + 1) * 512], [("vt", ch)])
                        P.op("dve", lambda e, ch=ch, cb=cb: e.bn_stats(bst[:, ch, cb * 6:(cb + 1) * 6], vt[:, ch, cb * 512:(cb + 1) * 512]),
                             reads=[("vt", ch)], writes=[("bst", ch)])
                for ch in range(4):
                    P.op("dve", lambda e, ch=ch: e.bn_aggr(mv[:, ch, :], bst[:, ch, :]), reads=[("bst", ch)], writes=[("mv", ch)])
                    P.op("act", lambda e, ch=ch: e.activation(out=sd[:, ch:ch + 1], in_=mv[:, ch, 1:2], func=AF.Sqrt, bias=epsc[:, 1:2], scale=1.0),
                         reads=[("mv", ch), "epsc"], writes=[("sd", ch)])
                    P.op("dve", lambda e, ch=ch: e.reciprocal(sd[:, ch:ch + 1], sd[:, ch:ch + 1]), reads=[("sd", ch)], writes=[("sd", ch)])
                    P.op("dve", lambda e, ch=ch: e.tensor_scalar(out=vt[:, ch, :], in0=vt[:, ch, :], scalar1=mv[:, ch, 0:1], scalar2=sd[:, ch:ch + 1],
                                                                op0=ALU.subtract, op1=ALU.mult),
                         reads=[("vt", ch), ("mv", ch), ("sd", ch)], writes=[("vt", ch)])
                for fc in range(NFC):
                    slot = fc % 4
                    P.dma(lambda e, fc=fc, slot=slot: e.dma_start(out=wus4[:, slot, :, :], in_=wub_d[fc]), ("wus", slot),
                          reads=wkeys["wub"], writes=[("wus", slot)])
                    bank = pu[fc % 2]
                    for kc in range(NCH):
                        P.op("pe", lambda e, kc=kc, slot=slot, bank=bank: e.matmul(
                            bank[:, :], lhsT=wus4[:, slot, kc, :], rhs=hmc[:, kc, :], start=(kc == 0), stop=(kc == NCH - 1)),
                            reads=["hmc", ("wus", slot)], writes=[("bank", id(bank))], signal=(kc == NCH - 1))
                    gelu_from(bank, ub[:, fc, :], [("ub", fc)])
                if st + 1 < nsub:
                    cm_hmc(st + 1)
                for fc in range(NFC):
                    bank = py[fc % 2]
                    for ch in range(4):
                        P.op("pe", lambda e, fc=fc, ch=ch, bank=bank: e.matmul(
                            bank[:, ch * 128:(ch + 1) * 128], lhsT=vt[:, ch, fc * 128:(fc + 1) * 128], rhs=wsT[:, fc // 3, :], start=True, stop=True),
                            reads=[("vt", ch), "wsT"], writes=[("bank", id(bank))], signal=(ch == 3))
                    tmp = tb[:, fc % 2, :]
                    P.op("dve", lambda e, fc=fc, bank=bank, tmp=tmp: e.scalar_tensor_tensor(
                        out=tmp.rearrange("p (c t) -> p c t", c=4), in0=bank[:, :].rearrange("p (c t) -> p c t", c=4), scalar=vgT[:, fc:fc + 1],
                        in1=rt[:, fc, :].unsqueeze(1).to_broadcast([128, 4, 128]), op0=ALU.mult, op1=ALU.add),
                        reads=[("bank", id(bank)), "vgT", "rt"], writes=[("tb", fc % 2)])
                    P.op("pool" if fc % 2 == 0 else "dve", lambda e, fc=fc, tmp=tmp: e.tensor_tensor(out=ub[:, fc, :], in0=tmp, in1=ub[:, fc, :], op=ALU.mult),
                         reads=[("tb", fc % 2), ("ub", fc)], writes=[("ub", fc)])
                if st + 1 < nsub:
                    issue_wvs(st + 1, 0)
                    issue_wvs(st + 1, 1)
                for c in range(NCH):
                    osl = dctr[1] % 2
                    dctr[1] += 1
                    wos = wos2[osl]
                    P.dma(lambda e, c=c, wos=wos: e.dma_start(out=wos, in_=wcob_d[c]), ("wos", osl), reads=wkeys["wcob"],
                          writes=[("wos", osl)])
                    bank = pg[c % 2]
                    for fc in range(NFC):
                        P.op("pe", lambda e, fc=fc, bank=bank, wos=wos: e.matmul(bank[:, :], lhsT=wos[:, fc, :], rhs=ub[:, fc, :], start=(fc == 0), stop=(fc == NFC - 1)),
                             reads=[("wos", osl), ("ub", fc)], writes=[("bank", id(bank))], signal=(fc == NFC - 1))
                    P.op("dve", lambda e, c=c, bank=bank, s0=s0: e.scalar_tensor_tensor(
                        out=x_sb[:, c, s0:s0 + 512], in0=bank[:, :], scalar=modT[l][:, 5 * 8 + c, seq:seq + 1],
                        in1=x_sb[:, c, s0:s0 + 512], op0=ALU.mult, op1=ALU.add),
                        reads=[("bank", id(bank)), ("x", c, st), ("modT", l)], writes=[("x", c, st)])
                stl = lat_subtiles(seq, s0, s0 + 512)[0]
                layer_norm_tile(stl["xv"], stl["keys"], 512, lnk, lnk, 0)

        def load_tokens(src_rows_ap, ntok, dstv, keys_fn):
            for i, t0 in enumerate(range(0, ntok, 128)):
                sl = i % 4
                P.dma(lambda e, t0=t0, sl=sl: e.dma_start(out=iost[:, sl, :], in_=src_rows_ap[t0:t0 + 128, :]),
                      ("iost", sl), writes=[("iost", sl)])
                for h in range(2):
                    b = (2 * i + h) % 2
                    for cc in range(4):
                        c = h * 4 + cc
                        P.op("pe", lambda e, c=c, cc=cc, sl=sl, b=b: e.transpose(
                            pg[b][:, cc * 128:(cc + 1) * 128], iost[:, sl, c * 128:(c + 1) * 128], ident_f[:, :]),
                            reads=[("iost", sl), "ident_f"], writes=[("pg", b)], signal=(cc == 3))
                    P.op("dve" if h == 0 else "act",
                         (lambda e, h=h, b=b, t0=t0: e.tensor_copy(dstv(h, t0), pg[b][:, :].rearrange("p (c n) -> p c n", n=128)))
                         if h == 0 else
                         (lambda e, h=h, b=b, t0=t0: e.copy(dstv(h, t0), pg[b][:, :].rearrange("p (c n) -> p c n", n=128))),
                         reads=[("pg", b)], writes=keys_fn(h, t0))

        def store_tokens(dst_rows_ap, ntok):
            for i, t0 in enumerate(range(0, ntok, 128)):
                sl = i % 4
                for h in range(2):
                    b = (2 * i + h) % 2
                    for cc in range(4):
                        c = h * 4 + cc
                        P.op("pe", lambda e, c=c, cc=cc, b=b, t0=t0: e.transpose(
                            pu[b][:, cc * 128:(cc + 1) * 128], x_sb[:, c, t0:t0 + 128], ident_f[:, :]),
                            reads=[("x", c, t0 // 512), "ident_f"], writes=[("pu", b)], signal=(cc == 3))
                    if h == 0:
                        P.op("dve", lambda e, sl=sl, b=b: e.tensor_copy(iost[:, sl, 0:512], pu[b][:, :]),
                             reads=[("pu", b)], writes=[("iost", sl)])
                    else:
                        P.op("act", lambda e, sl=sl, b=b: e.copy(iost[:, sl, 512:1024], pu[b][:, :]),
                             reads=[("pu", b)], writes=[("iost", sl)])
                P.dma(lambda e, t0=t0, sl=sl: e.dma_start(out=dst_rows_ap[t0:t0 + 128, :], in_=iost[:, sl, :]),
                      ("iost", sl), reads=[("iost", sl)], writes=[("outd", t0)])


        if dbg == 3:
            P.finish(); P.emit(); return nc
        conv_rest()
        for seq in range(NSEQ):
            P.scope = f"load{seq}"
            load_tokens(x_d[seq], T,
                        lambda h, t0: x_sb[:, h * 4:(h + 1) * 4, t0:t0 + 128],
                        lambda h, t0: [("x", h * 4 + cc, t0 // 512) for cc in range(4)])
            load_tokens(ctx_d[seq], CTX,
                        lambda h, t0: ctx_sb[:, h * 4:(h + 1) * 4, t0:t0 + 128],
                        lambda h, t0: [("ctx", h * 4 + cc) for cc in range(4)])
            P.fence()
            P.scope = f"ffn00_{seq}"
            if stages >= 1 and dbg != 4:
                ffn_sublayer(0, 0, seq, True)
            flush_ln()
            while bg_jobs:
                run_bg_job()
            if stages >= 2:
                P.fence()
                P.scope = f"hgrn_{seq}"
                hgrn_sublayer(seq)
                P.fence()
            P.scope = f"ffn01_{seq}"
            if stages >= 3:
                ffn_sublayer(0, 1, seq, False, nxt=(1, 0) if stages >= 4 else None)
            P.scope = f"ffn10_{seq}"
            if stages >= 4:
                ffn_sublayer(1, 0, seq, False)
            if stages >= 5:
                flush_ln()
                P.fence()
                P.scope = f"cm_{seq}"
                cm_sublayer(seq)
                P.fence()
            P.scope = f"ffn11_{seq}"
            if stages >= 6:
                ffn_sublayer(1, 1, seq, False)
            flush_ln()
            P.fence()
            P.scope = f"store{seq}"
            store_tokens(out_d[seq], T)
            P.fence()
        P.finish()
        P.emit()
    return nc


def make_in_maps(inputs, n_cores, NSEQ):
    ident = np.eye(128, dtype=np.float32)
    TT = inputs["x"].shape[1] + inputs["ctx"].shape[1]
    si = np.arange(128)[:, None] % 64
    ti = np.arange(64)[None, :]
    maskf = (si <= ti).astype(np.float32)
    maskb = (si >= ti).astype(np.float32)
    rmask = np.broadcast_to((np.arange(TT) % 64 != 0).astype(np.float32)[None, :], (128, TT)).copy()
    maps = []
    for i in range(n_cores):
        b0 = i * NSEQ
        cc = np.concatenate([inputs["c"][b0:b0 + NSEQ], inputs["c_ctx"][None, :]], axis=0)
        maps.append({
            "x": np.ascontiguousarray(inputs["x"][b0:b0 + NSEQ]),
            "ctx": np.ascontiguousarray(inputs["ctx"][b0:b0 + NSEQ]),
            "cc": np.ascontiguousarray(cc.reshape((NSEQ + 1) * NCH, 128)),
            "mod_w": inputs["mod_w"],
            "mod_b": np.ascontiguousarray(inputs["mod_b"].reshape(DEPTH * 72, 128)),
            "ln_g": np.ascontiguousarray(inputs["ln_g"].reshape(48, 128)),
            "ln_b": np.ascontiguousarray(inputs["ln_b"].reshape(48, 128)),
            "ffn_w_in": inputs["ffn_w_in"],
            "ffn_w_out": inputs["ffn_w_out"],
            "ident": ident,
            "hg_w_in": inputs["hg_w_in"][0],
            "hg_w_out": inputs["hg_w_out"][0],
            "hg_lb": np.ascontiguousarray(inputs["hg_lower_bounds"].reshape(48, 128)),
            "hg_nw": np.ascontiguousarray(inputs["hg_norm_w"][0].reshape(128, 1)),
            "cm_w_in": inputs["cm_w_in"][0],
            "cm_w_out": inputs["cm_w_out"][0],
            "cm_vg": np.ascontiguousarray(inputs["cm_v_g"][0].reshape(NFC, 128)),
            "cm_vb": np.ascontiguousarray(inputs["cm_v_b"][0].reshape(NFC, 128)),
            "cm_ws": inputs["cm_w_s"][0],
            "cm_bs": inputs["cm_b_s"][0],
            "maskf": maskf, "maskb": maskb, "rmask": rmask,
        })
    return maps


def kernel(**inputs):
    inputs = {k: np.asarray(v) for k, v in inputs.items()}
    n_cores = 8
    B, T, _ = inputs["x"].shape
    CTX = inputs["ctx"].shape[1]
    NSEQ = B // n_cores
    nc = build_nc(NSEQ, T, CTX, stages=6)
    in_maps = make_in_maps(inputs, n_cores, NSEQ)
    res = run_bass_kernel_spmd(nc, in_maps, core_ids=list(range(n_cores)))
    return np.concatenate([r["out"] for r in res.results], axis=0)
```

```python
import numpy as np
from contextlib import ExitStack
import concourse.bass as bass
import concourse.mybir as mybir
from concourse.bass_utils import run_bass_kernel_spmd

F32 = mybir.dt.float32
BF16 = mybir.dt.bfloat16
AF = mybir.ActivationFunctionType
ALU = mybir.AluOpType

D = 1024
NCH = 8
FH = 2816
NJ = 22
NHEAD = 8
CMI = 3072
NFC = 24
DEPTH = 2
DN_ALPHA = (2 * DEPTH) ** 0.25
LN_EPS = 1e-5
RMS_EPS = 1e-6
EPOCH = 30000
import os
DBG_SKIP_BWD_SUB = bool(int(os.environ.get('DBG_SKIP_BWD_SUB', '0')))


class Tok:
    __slots__ = ("grp", "ep", "val", "sem", "own", "inc")

    def __init__(self, grp):
        self.grp = grp
        self.ep = 0
        self.val = None
        self.sem = None
        self.own = False
        self.inc = 1


class Rec:
    __slots__ = ("fn", "deps", "tok", "scope")


class Prog:
    ENG = ("pe", "act", "dve", "pool", "sp")

    def __init__(self, nc, es):
        self.nc = nc
        self.es = es
        self.streams = {e: [] for e in self.ENG}
        self.cnt = {}
        self.cur = {}
        self.pending = {e: [] for e in self.ENG}
        self.lastw = {}
        self.readers = {}
        self.fence_deps = []
        self.latest = {}
        self.nsem = 0
        self.scope = None
        self.use_scopes = False
        self.nofence = set()

    def _newsem(self):
        self.nsem += 1
        return self.es.enter_context(self.nc.semaphore(f"s{self.nsem}"))

    def _signal(self, grp, inc):
        if grp not in self.cur or self.cnt[grp] + inc > EPOCH:
            ep = self.cur[grp][0] + 1 if grp in self.cur else 0
            self.cur[grp] = (ep, self._newsem())
            self.cnt[grp] = 0
        self.cnt[grp] += inc
        t = Tok(grp)
        t.ep, t.sem = self.cur[grp]
        t.val = self.cnt[grp]
        t.own = True
        t.inc = inc
        return t

    def _record(self, eng, fn, reads, writes, tok_grp, inc, signal):
        if getattr(self, "dead", False):
            return None
        deps = list(self.fence_deps)
        for k in reads:
            t = self.lastw.get(k)
            if t is not None:
                deps.append(t)
        for k in writes:
            t = self.lastw.get(k)
            if t is not None:
                deps.append(t)
            deps.extend(self.readers.get(k, {}).values())
        for t in deps:
            if t.sem is None and not (eng == "pe" and t.grp == "pe"):
                raise RuntimeError(f"dependency on unsignaled op ({t.grp}) from {eng}")
        r = Rec()
        r.fn = fn
        r.deps = deps
        r.scope = self.scope
        if signal:
            tok = self._signal(tok_grp, inc)
            if tok_grp == eng:
                for p in self.pending[eng]:
                    p.ep, p.sem, p.val = tok.ep, tok.sem, tok.val
                self.pending[eng] = []
        else:
            tok = Tok(tok_grp)
            self.pending[eng].append(tok)
        r.tok = tok
        for k in reads:
            self.readers.setdefault(k, {})[tok.grp] = tok
        for k in writes:
            self.lastw[k] = tok
            self.readers[k] = {}
        self.latest[tok.grp] = tok
        self.streams[eng].append(r)
        return tok

    def op(self, eng, fn, reads=(), writes=(), signal=True):
        return self._record(eng, fn, reads, writes, eng, 1, signal)

    def dma(self, fn, slot, reads=(), writes=(), queue="sp"):
        return self._record(queue, fn, reads, writes, ("dma", slot), 16, True)

    def fence(self, all_groups=False):
        for e in self.ENG:
            if self.pending[e]:
                raise RuntimeError(f"fence with pending unsignaled ops on {e}")
        self.fence_deps = [t for g, t in self.latest.items() if all_groups or g not in self.nofence]

    def finish(self):
        self.fence(all_groups=True)
        r = Rec()
        r.fn = None
        r.deps = list(self.fence_deps)
        r.tok = None
        r.scope = None
        self.streams["sp"].append(r)

    def emit(self):
        nc = self.nc
        with nc.Block() as block:
            decos = {"pe": block.tensor, "act": block.scalar, "dve": block.vector,
                     "pool": block.gpsimd, "sp": block.sync}
            for name in self.ENG:
                def body(eng, name=name):
                    waited = {}
                    for r in self.streams[name]:
                        best = {}
                        for t in r.deps:
                            if name == "pe" and t.grp == "pe":
                                continue
                            b = best.get(t.grp)
                            if b is None or (t.ep, t.val) > (b.ep, b.val):
                                best[t.grp] = t
                        for t in best.values():
                            cur = waited.get(t.grp, (-1, -1))
                            if (t.ep, t.val) <= cur:
                                continue
                            eng.wait_ge(t.sem, t.val)
                            waited[t.grp] = (t.ep, t.val)
                        if r.fn is not None:
                            if self.use_scopes and r.scope is not None:
                                with nc.named_scope(r.scope):
                                    inst = r.fn(eng)
                            else:
                                inst = r.fn(eng)
                            if r.tok.own:
                                inst.then_inc(r.tok.sem, r.tok.inc)
                decos[name](body)


def build_nc(NSEQ, T, CTX, stages, dbg=False, scopes=False):
    nc = bass.Bass("TRN2", target_bir_lowering=False)
    NB = NSEQ + 1
    dt = nc.dram_tensor

    def din(name, shape, dtype=F32):
        return dt(name, list(shape), dtype, kind="ExternalInput").ap()

    x_d = din("x", (NSEQ, T, D))
    ctx_d = din("ctx", (NSEQ, CTX, D))
    cc_d = din("cc", (NB * NCH, 128))
    mod_w_d = din("mod_w", (DEPTH, D, 9 * D))
    mod_b_d = din("mod_b", (DEPTH * 72, 128))
    ln_g_d = din("ln_g", (48, 128))
    ln_b_d = din("ln_b", (48, 128))
    ffn_w_in_d = din("ffn_w_in", (DEPTH, 2, D, 2 * FH))
    ffn_w_out_d = din("ffn_w_out", (DEPTH, 2, FH, D))
    ident_d = din("ident", (128, 128))
    TT = T + CTX
    hg_w_in_d = din("hg_w_in", (D, 5 * D))
    hg_w_out_d = din("hg_w_out", (D, D))
    hg_lb_d = din("hg_lb", (48, 128))
    hg_nw_d = din("hg_nw", (128, 1))
    cm_w_in_d = din("cm_w_in", (D, 2 * CMI))
    cm_w_out_d = din("cm_w_out", (CMI, D))
    cm_vg_d = din("cm_vg", (NFC, 128))
    cm_vb_d = din("cm_vb", (NFC, 128))
    cm_ws_d = din("cm_ws", (8, 128, 128))
    cm_bs_d = din("cm_bs", (8, 128))
    maskf_d = din("maskf", (128, 64))
    maskb_d = din("maskb", (128, 64))
    rmask_d = din("rmask", (128, TT))
    whb_d = dt("whb", [NHEAD, 128, NCH, 640], BF16, kind="Internal").ap()
    whob_d = dt("whob", [NHEAD, 128, D], BF16, kind="Internal").ap()
    wub_d = dt("wub", [NFC, 128, NCH, 128], BF16, kind="Internal").ap()
    wvb_d = dt("wvb", [6, 128, NCH, 512], BF16, kind="Internal").ap()
    wcob_d = dt("wcob", [NCH, 128, NFC, 128], BF16, kind="Internal").ap()
    r_d = dt("r_d", [128, NFC * 128], F32, kind="Internal").ap()
    out_d = dt("out", [NSEQ, T, D], F32, kind="ExternalOutput").ap()
    dbg_d = dt("dbgout", [3, 128, TT], F32, kind="ExternalOutput").ap() if dbg == 77 else None
    dbgb_d = dt("dbgb", [3, 128, TT], BF16, kind="ExternalOutput").ap() if dbg == 78 else None

    winb_d = [[dt(f"winb{l}{s}", [NJ, 128, NCH, 256], BF16, kind="Internal").ap() for s in range(2)] for l in range(DEPTH)]
    woutb_d = [[dt(f"woutb{l}{s}", [NCH, 128, NJ, 128], BF16, kind="Internal").ap() for s in range(2)] for l in range(DEPTH)]

    es = ExitStack()
    with es:
        P = Prog(nc, es)
        P.use_scopes = scopes
        P.scope = "setup"
        sb = lambda name, shape, dtype: es.enter_context(nc.sbuf_tensor(name, list(shape), dtype))
        ps = lambda name: es.enter_context(nc.psum_tensor(name, [128, 512], F32))

        x_sb = sb("x_sb", (128, NCH, T), F32)
        ctx_sb = sb("ctx_sb", (128, NCH, CTX), F32)
        ARENA = 88064
        arena = sb("arena", (128, ARENA // 2), BF16)
        stat = sb("stat", (128, 5, 512), F32)
        sgb = sb("sgb", (128, 2, 512), F32)
        tb = sb("tb", (128, 2, 512), F32)
        ident_f = sb("ident_f", (128, 128), F32)
        ident_b = sb("ident_b", (128, 128), BF16)
        ones_b = sb("ones_b", (128, 128), BF16)
        small_in = sb("small_in", (128, 128), F32)
        scT = sb("scT", (128, NCH, 8), F32)
        modT = [sb(f"modT{l}", (128, 72, 8), F32) for l in range(DEPTH)]
        modbT = sb("modbT", (128, DEPTH * 72), F32)
        lngT = sb("lngT", (128, 48), F32)
        lnbT = sb("lnbT", (128, 48), F32)
        epsc = sb("epsc", (128, 4), F32)
        ones_h = sb("ones_h", (128, 128), BF16)
        maskf = sb("maskf_s", (128, 64), F32)
        maskb = sb("maskb_s", (128, 64), F32)
        rmask = sb("rmask_s", (128, TT), BF16)
        lbT = sb("lbT", (128, 48), F32)
        lbt = sb("lbt", (128, 2, 8), F32)
        omlb = sb("omlb", (128, 2, 8), F32)
        nwc = sb("nwc", (128, 1), F32)
        vgT = sb("vgT", (128, NFC), F32)
        vbT = sb("vbT", (128, NFC), F32)
        wsT = sb("wsT", (128, 8, 128), BF16)
        cmst = sb("cmst", (128, 160), F32)
        wus4 = sb("wus4", (128, 4, NCH, 128), BF16)

        def aview(off, shape, dtype):
            nbytes = int(np.prod(shape)) * (4 if dtype == F32 else 2)
            assert off % 4 == 0 and off + nbytes <= ARENA, (off, nbytes)
            v = arena[:, off // 2:(off + nbytes) // 2]
            if dtype == F32:
                v = v.bitcast(F32)
            if len(shape) == 1:
                return v
            if len(shape) == 2:
                return v.rearrange("p (a b) -> p a b", a=shape[0])
            if len(shape) == 3:
                return v.rearrange("p (a b c) -> p a b c", a=shape[0], b=shape[1])
            return v

        hmod = aview(0, (NCH, 1024), BF16)
        hid = aview(16384, (NJ, 1024), BF16)
        win = aview(61440, (3, NCH, 256), BF16)
        wout = aview(73728, (2, NJ, 128), BF16)
        st32 = aview(16384, (2, 5632), F32)
        st16 = aview(61440, (2, 5632), BF16)
        iost = aview(16384, (4, D), F32)

        pg = [ps("pg0"), ps("pg1")]
        pu = [ps("pu0"), ps("pu1")]
        py = [ps("py0"), ps("py1")]
        pst = [ps("pst0"), ps("pst1")]

        P.dma(lambda e: e.dma_start(out=ident_f[:, :], in_=ident_d[:, :]), "ident", writes=["ident_f"])
        P.op("dve", lambda e: e.tensor_copy(ident_b[:, :], ident_f[:, :]), reads=["ident_f"], writes=["ident_b"])
        P.op("pool", lambda e: e.memset(ones_b[:, :], 1.0 / D), writes=["ones_b"])
        P.op("pool", lambda e: e.memset(epsc[:, 0:1], LN_EPS / (DN_ALPHA ** 2)), writes=["epsc"])
        P.op("pool", lambda e: e.memset(epsc[:, 1:2], LN_EPS), writes=["epsc"])
        P.op("pool", lambda e: e.memset(epsc[:, 2:3], RMS_EPS), writes=["epsc"])
        P.op("pool", lambda e: e.memset(scT[:, :, :], 0.0), writes=["scT"])

        wkeys = {}

        def cast_dma(out_ap, in_ap, slot, key):
            P.nofence.add(("dma", slot))
            P.dma(lambda e: e.dma_start(out=out_ap, in_=in_ap), slot, writes=[key], queue="pool")

        def conv_ffn(l, s):
            ki, ko = [], []
            for kc in range(NCH):
                for half in range(2):
                    key = ("winb", l, s, kc, half)
                    ki.append(key)
                    cast_dma(winb_d[l][s][:, :, kc, half * 128:(half + 1) * 128].rearrange("j p n -> p j n"),
                             ffn_w_in_d[l, s, kc * 128:(kc + 1) * 128, half * FH:(half + 1) * FH].rearrange("p (j n) -> p j n", n=128),
                             ("cw_in", l, s), key)
            for j in range(NJ):
                key = ("woutb", l, s, j)
                ko.append(key)
                cast_dma(woutb_d[l][s][:, :, j, :].rearrange("c p n -> p c n"),
                         ffn_w_out_d[l, s, j * 128:(j + 1) * 128, :].rearrange("p (c n) -> p c n", n=128),
                         ("cw_out", l, s), key)
            wkeys[("winb", l, s)] = ki
            wkeys[("woutb", l, s)] = ko

        def conv_hg():
            ki = []
            for kc in range(NCH):
                for sel in range(5):
                    key = ("whb", kc, sel)
                    ki.append(key)
                    cast_dma(whb_d[:, :, kc, sel * 128:(sel + 1) * 128].rearrange("h p n -> p h n"),
                             hg_w_in_d[kc * 128:(kc + 1) * 128, sel * D:(sel + 1) * D].rearrange("p (h n) -> p h n", n=128),
                             "cw_hgi", key)
            wkeys["whb"] = ki
            cast_dma(whob_d.rearrange("h p n -> p h n"), hg_w_out_d.rearrange("(h p) n -> p h n", p=128), "cw_hgo", "whob")
            wkeys["whob"] = ["whob"]

        def conv_cm():
            ku, kv, ko = [], [], []
            for kc in range(NCH):
                key = ("wub", kc)
                ku.append(key)
                cast_dma(wub_d[:, :, kc, :].rearrange("f p n -> p f n"),
                         cm_w_in_d[kc * 128:(kc + 1) * 128, 0:CMI].rearrange("p (f n) -> p f n", n=128), "cw_cmu", key)
                key = ("wvb", kc)
                kv.append(key)
                cast_dma(wvb_d[:, :, kc, :].rearrange("b p n -> p b n"),
                         cm_w_in_d[kc * 128:(kc + 1) * 128, CMI:2 * CMI].rearrange("p (b n) -> p b n", n=512), "cw_cmv", key)
            for fc in range(NFC):
                key = ("wcob", fc)
                ko.append(key)
                cast_dma(wcob_d[:, :, fc, :].rearrange("c p n -> p c n"),
                         cm_w_out_d[fc * 128:(fc + 1) * 128, :].rearrange("p (c n) -> p c n", n=128), "cw_cmo", key)
            wkeys["wub"], wkeys["wvb"], wkeys["wcob"] = ku, kv, ko

        conv_ffn(0, 0)

        bg_jobs = []

        def conv_rest():
            if stages >= 2:
                conv_hg()
            if stages >= 3:
                bg_jobs.append(lambda: conv_ffn(0, 1))
            if stages >= 4:
                bg_jobs.append(lambda: conv_ffn(1, 0))
            if stages >= 5:
                bg_jobs.append(conv_cm)
            if stages >= 6:
                bg_jobs.append(lambda: conv_ffn(1, 1))

        def run_bg_job():
            if bg_jobs:
                bg_jobs.pop(0)()

        def load_T(dst, src_rows, nrows, func=None, key=None, srcview=None):
            P.dma(lambda e: e.dma_start(out=small_in[0:nrows, :], in_=src_rows), "small_in", writes=["small_in"])
            if func is not None:
                P.op("act", lambda e: e.activation(out=small_in[0:nrows, :], in_=small_in[0:nrows, :], func=func),
                     reads=["small_in"], writes=["small_in"])
            P.op("pe", lambda e: e.transpose(pst[0][:, 0:nrows], small_in[0:nrows, :], ident_f[0:nrows, 0:nrows]),
                 reads=["small_in", "ident_f"], writes=["pst0"])
            src = pst[0][:, 0:nrows] if srcview is None else srcview(pst[0][:, 0:nrows])
            P.op("dve", lambda e: e.tensor_copy(dst, src), reads=["pst0"], writes=[key])

        load_T(scT[:, :, 0:NB].rearrange("p k n -> p n k"), cc_d[:, :], NB * NCH, func=AF.Silu, key="scT",
               srcview=lambda a: a.rearrange("p (n k) -> p n k", k=NCH))
        load_T(modbT[:, 0:72], mod_b_d[0:72, :], 72, key="modbT")
        load_T(modbT[:, 72:144], mod_b_d[72:144, :], 72, key="modbT")
        load_T(lngT[:, :], ln_g_d[:, :], 48, key="lngT")
        load_T(lnbT[:, :], ln_b_d[:, :], 48, key="lnbT")

        P.op("pool", lambda e: e.memset(ones_h[:, :], 1.0 / 128.0), writes=["ones_h"])
        P.dma(lambda e: e.dma_start(out=maskf[:, :], in_=maskf_d[:, :]), "maskf", writes=["maskf"])
        P.dma(lambda e: e.dma_start(out=maskb[:, :], in_=maskb_d[:, :]), "maskb", writes=["maskb"])
        P.dma(lambda e: e.dma_start(out=nwc[:, :], in_=hg_nw_d[:, :]), "nwc", writes=["nwc"])
        rm32 = aview(0, (TT,), F32)
        P.dma(lambda e: e.dma_start(out=rm32, in_=rmask_d[:, :]), "rm32", writes=["rm32"])
        P.op("dve", lambda e: e.tensor_copy(rmask[:, :], rm32), reads=["rm32"], writes=["rmask"])
        if dbg == 10:
            P.finish(); P.emit(); return nc
        load_T(lbT[:, :], hg_lb_d[:, :], 48, key="lbT")
        load_T(vgT[:, :], cm_vg_d[:, :], NFC, key="vgT")
        load_T(vbT[:, :], cm_vb_d[:, :], NFC, key="vbT")
        P.op("act", lambda e: e.activation(out=lbT[:, :], in_=lbT[:, :], func=AF.Exp), reads=["lbT"], writes=["lbT"])
        for dr in range(2):
            e0, e1, e2 = (lbT[:, (dr * 3 + i) * 8:(dr * 3 + i + 1) * 8] for i in range(3))
            P.op("dve", lambda e, dr=dr, e0=e0, e1=e1: e.tensor_tensor(out=omlb[:, dr, :], in0=e0, in1=e1, op=ALU.add),
                 reads=["lbT"], writes=["omlb"])
            P.op("dve", lambda e, dr=dr, e2=e2: e.tensor_tensor(out=omlb[:, dr, :], in0=omlb[:, dr, :], in1=e2, op=ALU.add),
                 reads=["lbT", "omlb"], writes=["omlb"])
            P.op("dve", lambda e, dr=dr: e.reciprocal(omlb[:, dr, :], omlb[:, dr, :]), reads=["omlb"], writes=["omlb"])
            P.op("dve", lambda e, dr=dr, e0=e0: e.tensor_tensor(out=lbt[:, dr, :], in0=omlb[:, dr, :], in1=e0, op=ALU.mult),
                 reads=["lbT", "omlb"], writes=["lbt"])
            P.op("dve", lambda e, dr=dr: e.tensor_scalar(out=omlb[:, dr, :], in0=lbt[:, dr, :], scalar1=-1.0, scalar2=1.0,
                                                        op0=ALU.mult, op1=ALU.add), reads=["lbt"], writes=["omlb"])
        if dbg == 11:
            P.finish(); P.emit(); return nc
        bsrep = aview(16384, (8, 128), F32)
        rsrep = aview(20480, (8, 128), F32)
        rtab = aview(24576, (NFC, 128), F32)
        onesf = aview(40960, (128,), F32)
        P.op("pool", lambda e: e.memset(onesf, 1.0), writes=["onesf"])
        P.dma(lambda e: e.dma_start(out=bsrep, in_=cm_bs_d.partition_broadcast(128)), "bsrep", writes=["bsrep"])
        if dbg == 12:
            P.finish(); P.emit(); return nc
        for g in range(8):
            P.dma(lambda e, g=g: e.dma_start(out=small_in[:, :], in_=cm_ws_d[g]), "small_in", writes=["small_in"])
            P.op("pe", lambda e: e.transpose(pst[0][:, 0:128], small_in[:, :], ident_f[:, :]),
                 reads=["small_in", "ident_f"], writes=["pst0"])
            P.op("dve", lambda e, g=g: e.tensor_copy(wsT[:, g, :], pst[0][:, 0:128]), reads=["pst0"], writes=["wsT"])
            P.op("pe", lambda e, g=g: e.matmul(pst[1][:, 0:128], lhsT=ones_h[:, :], rhs=wsT[:, g, :], start=True, stop=True),
                 reads=["wsT", "ones_h"], writes=["pst1"])
            P.op("dve", lambda e, g=g: e.tensor_scalar_mul(rsrep[:, g, :], pst[1][:, 0:128], 128.0), reads=["pst1"], writes=["rsrep"])
        if dbg == 13:
            P.finish(); P.emit(); return nc
        for fc in range(NFC):
            P.op("dve", lambda e, fc=fc: e.scalar_tensor_tensor(
                out=rtab[:, fc, :], in0=rsrep[:, fc // 3, :], scalar=vbT[:, fc:fc + 1], in1=bsrep[:, fc // 3, :],
                op0=ALU.mult, op1=ALU.add), reads=["rsrep", "bsrep", "vbT"], writes=["rtab"])
        P.dma(lambda e: e.dma_start(out=r_d[:, :], in_=rtab.rearrange("p a b -> p (a b)")), "rtab", reads=["rtab"], writes=["r_d"])
        P.fence()
        if dbg == 1:
            P.finish(); P.emit(); return nc
        mw32 = aview(16384, (3, 4608), F32)
        mps = [pg[0], pg[1]]
        cnt = 0
        for l in range(DEPTH):
            for kc in range(NCH):
                for hf in range(2):
                    slot = cnt % 3
                    cnt += 1
                    P.dma(lambda e, l=l, kc=kc, hf=hf, slot=slot: e.dma_start(
                        out=mw32[:, slot, :], in_=mod_w_d[l, kc * 128:(kc + 1) * 128, hf * 4608:(hf + 1) * 4608]),
                        ("mw", slot), writes=[("mw", slot)])
                    for f in range(36):
                        P.op("pe", lambda e, kc=kc, hf=hf, slot=slot, f=f: e.matmul(
                            mps[hf][:, f * 8:(f + 1) * 8], lhsT=mw32[:, slot, f * 128:(f + 1) * 128],
                            rhs=scT[:, kc, :], start=(kc == 0 and f == 0), stop=(kc == NCH - 1),
                            skip_group_check=True),
                            reads=[("mw", slot), "scT"], writes=[("mps", hf)], signal=(f == 35))
            for hf in range(2):
                P.op("dve", lambda e, l=l, hf=hf: e.tensor_tensor(
                    out=modT[l][:, hf * 36:(hf + 1) * 36, :],
                    in0=mps[hf][:, 0:288].rearrange("p (f n) -> p f n", n=8),
                    in1=modbT[:, l * 72 + hf * 36: l * 72 + (hf + 1) * 36].unsqueeze(2).to_broadcast([128, 36, 8]),
                    op=ALU.add), reads=[("mps", hf), "modbT"], writes=[("modT", l)])
            for k in range(3):
                gf = (0.5 if k != 1 else 1.0) / DN_ALPHA
                P.op("dve", lambda e, l=l, k=k: e.tensor_scalar_add(
                    modT[l][:, (3 * k + 1) * 8:(3 * k + 2) * 8, :], modT[l][:, (3 * k + 1) * 8:(3 * k + 2) * 8, :], 1.0),
                    reads=[("modT", l)], writes=[("modT", l)])
                P.op("dve", lambda e, l=l, k=k, gf=gf: e.tensor_scalar_mul(
                    modT[l][:, (3 * k + 2) * 8:(3 * k + 3) * 8, :], modT[l][:, (3 * k + 2) * 8:(3 * k + 3) * 8, :], gf),
                    reads=[("modT", l)], writes=[("modT", l)])
        P.fence()

        if dbg == 2:
            P.finish(); P.emit(); return nc
        cast_rr = [0]

        def cast(out, in_, reads, writes):
            i = cast_rr[0] % 3
            cast_rr[0] += 1
            if i == 0:
                P.op("act", lambda e: e.copy(out, in_), reads=reads, writes=writes)
            elif i == 1:
                P.op("dve", lambda e: e.tensor_copy(out, in_), reads=reads, writes=writes)
            else:
                P.op("pool", lambda e: e.tensor_copy(out, in_), reads=reads, writes=writes)

        pp = [0]

        def pp_slot():
            s = pp[0] % 2
            pp[0] += 1
            return s

        def prep_ffn(l, s):
            for kc in range(NCH):
                sl = pp_slot()
                P.dma(lambda e, kc=kc, sl=sl: e.dma_start(out=st32[:, sl, :], in_=ffn_w_in_d[l, s, kc * 128:(kc + 1) * 128, :]),
                      ("st32", sl), writes=[("st32", sl)])
                o16 = st16[:, sl, :].rearrange("p (j n) -> p j n", n=256)
                for half in range(2):
                    cast(o16[:, :, half * 128:(half + 1) * 128],
                         st32[:, sl, half * FH:(half + 1) * FH].rearrange("p (j n) -> p j n", n=128),
                         reads=[("st32", sl)], writes=[("st16", sl, half)])
                P.dma(lambda e, kc=kc, sl=sl, o16=o16: e.dma_start(
                    out=winb_d[l][s][:, :, kc, :].rearrange("j p n -> p j n"), in_=o16),
                    ("st16", sl), reads=[("st16", sl, 0), ("st16", sl, 1)], writes=[("winb", l, s)])
            for j0 in range(0, NJ, 5):
                nj = min(5, NJ - j0)
                sl = pp_slot()
                i32 = st32[:, sl, 0:nj * D].rearrange("p (j n) -> p j n", n=D)
                i16 = st16[:, sl, 0:nj * D].rearrange("p (j n) -> p j n", n=D)
                P.dma(lambda e, j0=j0, nj=nj, i32=i32: e.dma_start(
                    out=i32, in_=ffn_w_out_d[l, s, j0 * 128:(j0 + nj) * 128, :].rearrange("(j p) n -> p j n", p=128)),
                    ("st32", sl), writes=[("st32", sl)])
                cast(i16, i32, reads=[("st32", sl)], writes=[("st16", sl, 0), ("st16", sl, 1)])
                for jj in range(nj):
                    P.dma(lambda e, j0=j0, jj=jj, i16=i16: e.dma_start(
                        out=woutb_d[l][s][:, :, j0 + jj, :].rearrange("c p n -> p c n"),
                        in_=i16[:, jj, :].rearrange("p (c n) -> p c n", n=128)),
                        ("st16", sl, jj), reads=[("st16", sl, 0), ("st16", sl, 1)], writes=[("woutb", l, s)])

        def prep_rows(src_rows, ncols, cast_fn, stores):
            sl = pp_slot()
            P.dma(lambda e: e.dma_start(out=st32[:, sl, 0:ncols], in_=src_rows), ("st32", sl), writes=[("st32", sl)])
            cast_fn(st32[:, sl, 0:ncols], st16[:, sl, 0:ncols], sl)
            for i, (dst, srcv, wkey) in enumerate(stores):
                P.dma(lambda e, dst=dst, srcv=srcv: e.dma_start(out=dst, in_=srcv(st16[:, sl, 0:ncols])),
                      ("st16", sl, i), reads=[("st16", sl, 0), ("st16", sl, 1)], writes=[wkey])

        def plain_cast(i32, i16, sl):
            cast(i16, i32, reads=[("st32", sl)], writes=[("st16", sl, 0), ("st16", sl, 1)])

        def prep_rows3(src_rows3, a, b, cast_fn, stores):
            sl = pp_slot()
            P.dma(lambda e: e.dma_start(out=st32[:, sl, 0:a * b].rearrange("p (a b) -> p a b", a=a), in_=src_rows3),
                  ("st32", sl), writes=[("st32", sl)])
            cast_fn(st32[:, sl, 0:a * b], st16[:, sl, 0:a * b], sl)
            for i, (dst, srcv, wkey) in enumerate(stores):
                P.dma(lambda e, dst=dst, srcv=srcv: e.dma_start(out=dst, in_=srcv(st16[:, sl, 0:a * b])),
                      ("st16", sl, i), reads=[("st16", sl, 0), ("st16", sl, 1)], writes=[wkey])

        def prep_hg():
            for kc in range(NCH):
                def cst(i32, i16, sl):
                    cast(i16.rearrange("p (h s n) -> p h s n", h=NHEAD, s=5),
                         i32.rearrange("p (s h n) -> p h s n", s=5, h=NHEAD),
                         reads=[("st32", sl)], writes=[("st16", sl, 0), ("st16", sl, 1)])
                prep_rows(hg_w_in_d[kc * 128:(kc + 1) * 128, :], 5 * D, cst,
                          [(whb_d[:, :, kc, :].rearrange("h p n -> p h n"),
                            lambda v: v.rearrange("p (h n) -> p h n", h=NHEAD), "whb")])
            for h0 in range(0, NHEAD, 4):
                prep_rows3(hg_w_out_d[h0 * 128:(h0 + 4) * 128, :].rearrange("(h p) n -> p h n", p=128), 4, D, plain_cast,
                           [(whob_d[h0:h0 + 4].rearrange("h p n -> p h n"),
                             lambda v: v.rearrange("p (h n) -> p h n", h=4), "whob")])

        def prep_cm():
            for kc in range(NCH):
                prep_rows(cm_w_in_d[kc * 128:(kc + 1) * 128, 0:CMI], CMI, plain_cast,
                          [(wub_d[:, :, kc, :].rearrange("f p n -> p f n"),
                            lambda v: v.rearrange("p (f n) -> p f n", n=128), "wub")])
                prep_rows(cm_w_in_d[kc * 128:(kc + 1) * 128, CMI:2 * CMI], CMI, plain_cast,
                          [(wvb_d[:, :, kc, :].rearrange("b p n -> p b n"),
                            lambda v: v.rearrange("p (b n) -> p b n", n=512), "wvb")])
            for j0 in range(0, NFC, 4):
                prep_rows3(cm_w_out_d[j0 * 128:(j0 + 4) * 128, :].rearrange("(j p) n -> p j n", p=128), 4, D, plain_cast,
                           [(wcob_d[:, :, j0 + jj, :].rearrange("c p n -> p c n"),
                             (lambda v, jj=jj: v[:, jj * D:(jj + 1) * D].rearrange("p (c n) -> p c n", n=128)), "wcob")
                            for jj in range(4)])

        zsm = sb("zsm", (128, 2048), BF16)

        def layer_norm_tile(xv, xkeys, n, gcol, bcol, eps_col, zbuf=None):
            for c in range(NCH):
                b = c % 2
                zb = zsm[:, b * 512:b * 512 + n]
                zq = zsm[:, 1024 + b * 512:1024 + b * 512 + n]
                P.op("act", lambda e, c=c, zb=zb: e.activation(out=zb, in_=xv(c), func=AF.Copy),
                     reads=[xkeys[c]], writes=[("zb", b)])
                P.op("pool", lambda e, c=c, zq=zq: e.tensor_tensor(out=zq, in0=xv(c), in1=xv(c), op=ALU.mult),
                     reads=[xkeys[c]], writes=[("zq", b)])
                P.op("pe", lambda e, c=c, zb=zb: e.matmul(pst[0][:, 0:n], lhsT=ones_b[:, :], rhs=zb, start=(c == 0), stop=(c == NCH - 1)),
                     reads=[("zb", b), "ones_b"], writes=["pst0"])
                P.op("pe", lambda e, c=c, zq=zq: e.matmul(pst[1][:, 0:n], lhsT=ones_b[:, :], rhs=zq, start=(c == 0), stop=(c == NCH - 1)),
                     reads=[("zq", b), "ones_b"], writes=["pst1"])
            var, rstd = (stat[:, i, 0:n] for i in range(2))
            m2 = tb[:, 0, 0:n]
            P.op("act", lambda e: e.activation(out=m2, in_=pst[0][:, 0:n], func=AF.Square), reads=["pst0"], writes=[("tb", 0)])
            P.op("dve", lambda e: e.tensor_tensor(out=var, in0=pst[1][:, 0:n], in1=m2, op=ALU.subtract),
                 reads=["pst1", ("tb", 0)], writes=["var"])
            P.op("act", lambda e: e.activation(out=var, in_=var, func=AF.Ln, bias=epsc[:, eps_col:eps_col + 1], scale=1.0),
                 reads=["var", "epsc"], writes=["var"])
            P.op("act", lambda e: e.activation(out=rstd, in_=var, func=AF.Exp, scale=-0.5), reads=["var"], writes=["rstd"])
            for c in range(NCH):
                b = c % 2
                t = tb[:, b, 0:n]
                P.op("dve", lambda e, c=c, t=t: e.tensor_tensor(out=t, in0=xv(c), in1=pst[0][:, 0:n], op=ALU.subtract),
                     reads=[xkeys[c], "pst0"], writes=[("tb", b)])
                P.op("pool", lambda e, t=t: e.tensor_tensor(out=t, in0=t, in1=rstd, op=ALU.mult),
                     reads=[("tb", b), "rstd"], writes=[("tb", b)])
                P.op("act", lambda e, c=c, t=t: e.activation(out=xv(c), in_=t, func=AF.Identity,
                                                            bias=lnbT[:, bcol + c:bcol + c + 1], scale=lngT[:, gcol + c:gcol + c + 1]),
                     reads=[("tb", b), "lngT", "lnbT"], writes=[xkeys[c]])

        wctr = {"win": 0, "wout": 0, "pgu": 0, "py": 0}

        hmod_ready = [False]
        pending_ln = []

        def flush_ln():
            while pending_ln:
                sts, lk = pending_ln.pop(0)
                for st in sts:
                    layer_norm_tile(st["xv"], st["keys"], st["n"], lk, lk, 0)

        def ffn_hmod(l, s, subtiles):
            want = {kk for st in subtiles for kk in st["keys"]}
            if any(kk in want for sts, _ in pending_ln for st in sts for kk in st["keys"]):
                flush_ln()
            k = 0 if s == 0 else 2
            o = 0
            for si, st in enumerate(subtiles):
                n, col = st["n"], st["col"]
                for c in range(NCH):
                    P.op("act", lambda e, st=st, c=c, o=o, n=n, col=col: e.activation(
                        out=hmod[:, c, o:o + n], in_=st["xv"](c), func=AF.Identity,
                        bias=modT[l][:, (3 * k) * 8 + c, col:col + 1], scale=modT[l][:, (3 * k + 1) * 8 + c, col:col + 1]),
                        reads=[st["keys"][c], ("modT", l)], writes=[("hmod", si)])
                o += n

        def ffn_group(l, s, subtiles, prefetch=None):
            k = 0 if s == 0 else 2
            lnk = (l * 3 + k) * 8
            offs = []
            o = 0
            for st in subtiles:
                offs.append(o)
                o += st["n"]
            assert o <= 1024
            if not hmod_ready[0]:
                ffn_hmod(l, s, subtiles)
            hmod_ready[0] = False
            for j in range(NJ):
                if j == 2:
                    flush_ln()
                    run_bg_job()
                slot = wctr["win"] % 3
                wctr["win"] += 1
                P.dma(lambda e, j=j, slot=slot: e.dma_start(out=win[:, slot, :, :], in_=winb_d[l][s][j]),
                      ("win", slot), reads=wkeys[("winb", l, s)], writes=[("win", slot)])
                for si, st in enumerate(subtiles):
                    n, o = st["n"], offs[si]
                    b = wctr["pgu"] % 2
                    wctr["pgu"] += 1
                    for c in range(NCH):
                        P.op("pe", lambda e, c=c, slot=slot, b=b, o=o, n=n: e.matmul(
                            pg[b][:, 0:n], lhsT=win[:, slot, c, 0:128], rhs=hmod[:, c, o:o + n],
                            start=(c == 0), stop=(c == NCH - 1)),
                            reads=[("win", slot), ("hmod", si)], writes=[("pg", b)], signal=(c == NCH - 1))
                    for c in range(NCH):
                        P.op("pe", lambda e, c=c, slot=slot, b=b, o=o, n=n: e.matmul(
                            pu[b][:, 0:n], lhsT=win[:, slot, c, 128:256], rhs=hmod[:, c, o:o + n],
                            start=(c == 0), stop=(c == NCH - 1)),
                            reads=[("win", slot), ("hmod", si)], writes=[("pu", b)], signal=(c == NCH - 1))
                    P.op("act", lambda e, b=b, n=n: e.activation(out=sgb[:, b, 0:n], in_=pg[b][:, 0:n], func=AF.Silu),
                         reads=[("pg", b)], writes=[("sg", b)])
                    P.op("dve", lambda e, b=b, n=n, o=o, j=j: e.tensor_tensor(
                        out=hid[:, j, o:o + n], in0=sgb[:, b, 0:n], in1=pu[b][:, 0:n], op=ALU.mult),
                        reads=[("sg", b), ("pu", b)], writes=[("hid", si)])
            if prefetch is not None:
                ffn_hmod(*prefetch)
                hmod_ready[0] = True
            for c in range(NCH):
                slot = wctr["wout"] % 2
                wctr["wout"] += 1
                P.dma(lambda e, c=c, slot=slot: e.dma_start(out=wout[:, slot, :, :], in_=woutb_d[l][s][c]),
                      ("wout", slot), reads=wkeys[("woutb", l, s)], writes=[("wout", slot)])
                for si, st in enumerate(subtiles):
                    n, o, col = st["n"], offs[si], st["col"]
                    b = wctr["py"] % 2
                    wctr["py"] += 1
                    for j in range(NJ):
                        P.op("pe", lambda e, j=j, slot=slot, b=b, o=o, n=n: e.matmul(
                            py[b][:, 0:n], lhsT=wout[:, slot, j, :], rhs=hid[:, j, o:o + n],
                            start=(j == 0), stop=(j == NJ - 1)),
                            reads=[("wout", slot), ("hid", si)], writes=[("py", b)], signal=(j == NJ - 1))
                    P.op("dve", lambda e, st=st, c=c, b=b, n=n, col=col: e.scalar_tensor_tensor(
                        out=st["xv"](c), in0=py[b][:, 0:n], scalar=modT[l][:, (3 * k + 2) * 8 + c, col:col + 1],
                        in1=st["xv"](c), op0=ALU.mult, op1=ALU.add),
                        reads=[("py", b), st["keys"][c], ("modT", l)], writes=[st["keys"][c]])
            pending_ln.append((subtiles, lnk))

        def lat_subtiles(seq, t0, t1):
            res = []
            for a in range(t0, t1, 512):
                n = min(512, t1 - a)
                res.append(dict(xv=(lambda c, a=a, n=n: x_sb[:, c, a:a + n]),
                                keys=[("x", c, a // 512) for c in range(NCH)], n=n, col=seq))
            return res

        def ctx_subtiles():
            return [dict(xv=(lambda c: ctx_sb[:, c, 0:CTX]), keys=[("ctx", c) for c in range(NCH)], n=CTX, col=NSEQ)]

        def ffn_sublayer(l, s, seq, with_ctx, nxt=None):
            groups = [lat_subtiles(seq, t0, min(T, t0 + 1024)) for t0 in range(0, T, 1024)]
            if with_ctx:
                if T == 2048:
                    groups = [lat_subtiles(seq, 0, 1024), lat_subtiles(seq, 1024, 1536) + ctx_subtiles(), lat_subtiles(seq, 1536, 2048)]
                else:
                    groups.append(ctx_subtiles())
            for gi, g in enumerate(groups):
                if gi + 1 < len(groups):
                    pf = (l, s, groups[gi + 1])
                elif nxt is not None and len(groups) > 1:
                    pf = (nxt[0], nxt[1], groups[0])
                else:
                    pf = None
                ffn_group(l, s, g, prefetch=pf)

        def hgrn_sublayer(seq):
            l, k = 0, 1
            NT = TT // 128
            NCK = TT // 64
            CTXC = CTX // 64
            off = [0]

            def carve(shape, dtype):
                nb = int(np.prod(shape)) * (4 if dtype == F32 else 2)
                v = aview(off[0], shape, dtype)
                off[0] += (nb + 63) // 64 * 64
                return v
            hx = carve((NCH, TT), BF16)
            wh = carve((NCH, 640), BF16)
            who = carve((D,), BF16)
            vtok = carve((NT, 128), BF16)
            qf = carve((T,), F32)
            gb = carve((T,), BF16)
            ob = carve((T,), BF16)
            A = carve((TT,), F32)
            kin = carve((TT,), BF16)
            Sb = carve((2, 128), BF16)
            PT = carve((2, 64), BF16)
            osq = carve((512,), BF16)
            onb = carve((512,), BF16)
            tot = carve((NCK,), F32)
            B2 = stat[:, :, :].rearrange("p a b -> p (a b)")[:, 0:TT]
            Qin = sgb[:, :, :].rearrange("p a b -> p (a b)").bitcast(BF16)[:, 0:T]
            kin_tok = ctx_sb[:, :, :].rearrange("p a b -> p (a b)").bitcast(BF16)[:, 0:NT * 128].rearrange("p (t n) -> p t n", n=128)
            pstb = pst[1][:, :].bitcast(BF16)
            xkeys_all = [("x", c, st) for c in range(NCH) for st in range((T + 511) // 512)]
            ctxkeys = [("ctx", c) for c in range(NCH)]
            rngs = [(0, CTX)] + [(CTX + a, min(512, T - a)) for a in range(0, T, 512)]
            for c in range(NCH):
                P.op("act", lambda e, c=c: e.activation(
                    out=hx[:, c, 0:CTX], in_=ctx_sb[:, c, :], func=AF.Identity,
                    bias=modT[l][:, 3 * 8 + c, NSEQ:NSEQ + 1], scale=modT[l][:, 4 * 8 + c, NSEQ:NSEQ + 1]),
                    reads=[("ctx", c), ("modT", l)], writes=["hx"])
                for a in range(0, T, 512):
                    n = min(512, T - a)
                    P.op("act", lambda e, c=c, a=a, n=n: e.activation(
                        out=hx[:, c, CTX + a:CTX + a + n], in_=x_sb[:, c, a:a + n], func=AF.Identity,
                        bias=modT[l][:, 3 * 8 + c, seq:seq + 1], scale=modT[l][:, 4 * 8 + c, seq:seq + 1]),
                        reads=[("x", c, a // 512), ("modT", l)], writes=["hx"])
            pb_ctr = [0]

            def proj(sel, a, n, bank):
                for kc in range(NCH):
                    P.op("pe", lambda e, kc=kc: e.matmul(bank[:, 0:n], lhsT=wh[:, kc, sel * 128:(sel + 1) * 128],
                                                         rhs=hx[:, kc, a:a + n], start=(kc == 0), stop=(kc == NCH - 1)),
                         reads=["wh", "hx"], writes=[("bank", id(bank))], signal=(kc == NCH - 1))

            def finalize_gen(po, s0, nst, st):
                o32 = tb[:, 0, 0:nst]
                r32 = tb[:, 1, 0:nst]
                P.op("dve", lambda e: e.tensor_tensor(out=o32, in0=po[:, 0:nst], in1=ob[:, s0:s0 + nst], op=ALU.add),
                     reads=[("bank", id(po)), "ob"], writes=[("tb", 0)])
                P.op("pool", lambda e: e.tensor_tensor(out=osq[:, 0:nst], in0=o32, in1=o32, op=ALU.mult),
                     reads=[("tb", 0)], writes=["osq"])
                yield
                P.op("pe", lambda e: e.matmul(pst[1][:, 0:nst], lhsT=ones_h[:, :], rhs=osq[:, 0:nst], start=True, stop=True),
                     reads=["osq", "ones_h"], writes=["pst1"])
                P.op("act", lambda e: e.activation(out=r32, in_=pst[1][:, 0:nst], func=AF.Ln, bias=epsc[:, 2:3], scale=1.0),
                     reads=["pst1", "epsc"], writes=[("tb", 1)])
                P.op("act", lambda e: e.activation(out=r32, in_=r32, func=AF.Exp, scale=-0.5), reads=[("tb", 1)], writes=[("tb", 1)])
                yield
                P.op("dve", lambda e: e.tensor_tensor(out=o32, in0=o32, in1=r32, op=ALU.mult),
                     reads=[("tb", 0), ("tb", 1)], writes=[("tb", 0)])
                P.op("dve", lambda e: e.scalar_tensor_tensor(
                    out=onb[:, 0:nst], in0=o32, scalar=nwc[:, 0:1], in1=gb[:, s0:s0 + nst], op0=ALU.mult, op1=ALU.mult),
                    reads=[("tb", 0), "nwc", "gb"], writes=["onb"])
                yield
                for c in range(NCH):
                    bank = py[c % 2]
                    P.op("pe", lambda e, c=c, bank=bank: e.matmul(bank[:, 0:nst], lhsT=who[:, c * 128:(c + 1) * 128], rhs=onb[:, 0:nst], start=True, stop=True),
                         reads=["who", "onb"], writes=[("bank", id(bank))])
                    P.op("dve", lambda e, c=c, bank=bank: e.scalar_tensor_tensor(
                        out=x_sb[:, c, s0:s0 + nst], in0=bank[:, 0:nst], scalar=modT[l][:, 5 * 8 + c, seq:seq + 1],
                        in1=x_sb[:, c, s0:s0 + nst], op0=ALU.mult, op1=ALU.add),
                        reads=[("bank", id(bank)), ("x", c, st), ("modT", l)], writes=[("x", c, st)])
                    if c % 2 == 1:
                        yield

            deferred = []
            for h in range(NHEAD):
                P.dma(lambda e, h=h: e.dma_start(out=wh, in_=whb_d[h].rearrange("p k n -> p k n")), "wh",
                      reads=wkeys["whb"], writes=["wh"])
                P.dma(lambda e, h=h: e.dma_start(out=who, in_=whob_d[h]), "who", reads=wkeys["whob"], writes=["who"])
                for t0 in range(0, NT, 4):
                    nt = min(4, NT - t0)
                    bank = py[(t0 // 4) % 2]
                    for ti in range(nt):
                        tl = t0 + ti
                        for kc in range(NCH):
                            P.op("pe", lambda e, kc=kc, ti=ti, tl=tl, bank=bank: e.matmul(
                                bank[:, ti * 128:(ti + 1) * 128], lhsT=hx[:, kc, tl * 128:(tl + 1) * 128],
                                rhs=wh[:, kc, 128:256], start=(kc == 0), stop=(kc == NCH - 1)),
                                reads=["wh", "hx"], writes=[("bank", id(bank))], signal=(kc == NCH - 1))
                    P.op("dve", lambda e, t0=t0, nt=nt, bank=bank: e.tensor_copy(
                        vtok[:, t0:t0 + nt, :], bank[:, 0:nt * 128].rearrange("p (t n) -> p t n", n=128)),
                        reads=[("bank", id(bank))], writes=["vtok"])
                for (a, n) in rngs[1:]:
                    bank = pg[pb_ctr[0] % 2]
                    pb_ctr[0] += 1
                    proj(0, a, n, bank)
                    P.op("act", lambda e, a=a, n=n, bank=bank: e.activation(out=qf[:, a - CTX:a - CTX + n], in_=bank[:, 0:n], func=AF.Silu),
                         reads=[("bank", id(bank))], writes=["qf"])
                    bank = pg[pb_ctr[0] % 2]
                    pb_ctr[0] += 1
                    proj(4, a, n, bank)
                    P.op("act", lambda e, a=a, n=n, bank=bank: e.activation(out=gb[:, a - CTX:a - CTX + n], in_=bank[:, 0:n], func=AF.Silu),
                         reads=[("bank", id(bank))], writes=["gb"])
                for dr in (1, 0):
                    def piece_of(a):
                        return 0 if a < CTX else 1 + (a - CTX) // 512
                    cp_ctr = [0]
                    for pi, (a, n) in enumerate(rngs):
                        bank = pu[pb_ctr[0] % 2]
                        pb_ctr[0] += 1
                        proj(2 + dr, a, n, bank)
                        kA, kB, kK = ("A", pi), ("B2", pi), ("kin", pi)
                        c0, nck = a // 64, n // 64
                        Ap, Bp, Kp = A[:, a:a + n], B2[:, a:a + n], kin[:, a:a + n]
                        P.op("act", lambda e, Ap=Ap, n=n, bank=bank: e.activation(out=Ap, in_=bank[:, 0:n], func=AF.Sigmoid),
                             reads=[("bank", id(bank))], writes=[kA])
                        P.op("dve", lambda e, Ap=Ap, dr=dr, h=h: e.tensor_scalar(out=Ap, in0=Ap, scalar1=omlb[:, dr, h:h + 1], scalar2=lbt[:, dr, h:h + 1],
                                                                               op0=ALU.mult, op1=ALU.add), reads=[kA, "omlb", "lbt"], writes=[kA])
                        P.op("pool", lambda e, Ap=Ap, Kp=Kp: e.tensor_scalar(out=Kp, in0=Ap, scalar1=-1.0, scalar2=1.0, op0=ALU.mult, op1=ALU.add),
                             reads=[kA], writes=[kK])
                    for pi, (a, n) in enumerate(rngs):
                        kA, kB, kK = ("A", pi), ("B2", pi), ("kin", pi)
                        c0, nck = a // 64, n // 64
                        Ap, Bp, Kp = A[:, a:a + n], B2[:, a:a + n], kin[:, a:a + n]
                        P.op("act", lambda e, Ap=Ap: e.activation(out=Ap, in_=Ap, func=AF.Ln), reads=[kA], writes=[kA])
                        P.op("dve", lambda e, Ap=Ap, Bp=Bp, a=a, n=n: e.tensor_tensor_scan(out=Bp, data0=rmask[:, a:a + n], data1=Ap, initial=0.0,
                                                                                         op0=ALU.mult, op1=ALU.add),
                             reads=[kA, "rmask"], writes=[kB])
                        P.op("act", lambda e, Bp=Bp, c0=c0, nck=nck: e.activation(
                            out=tot[:, c0:c0 + nck].unsqueeze(2), in_=Bp.rearrange("p (c t) -> p c t", t=64)[:, :, 63:64], func=AF.Exp),
                            reads=[kB], writes=[("dt", pi)])
                        if dr == 1 and not DBG_SKIP_BWD_SUB:
                            P.op("dve", lambda e, Ap=Ap, Bp=Bp: e.tensor_tensor(out=Bp, in0=Ap, in1=Bp, op=ALU.subtract), reads=[kA, kB], writes=[kB])
                        P.op("act", lambda e, Ap=Ap, Bp=Bp: e.activation(out=Ap, in_=Bp, func=AF.Exp, scale=-1.0), reads=[kB, kA], writes=[kA])
                        P.op("pool", lambda e, Ap=Ap, Kp=Kp: e.tensor_tensor(out=Kp, in0=Kp, in1=Ap, op=ALU.mult), reads=[kK, kA], writes=[kK])
                        P.op("act", lambda e, Bp=Bp: e.activation(out=Bp, in_=Bp, func=AF.Exp), reads=[kB], writes=[kB])
                        if a >= CTX:
                            P.op("dve", lambda e, Bp=Bp, a=a, n=n: e.tensor_tensor(out=Qin[:, a - CTX:a - CTX + n], in0=qf[:, a - CTX:a - CTX + n], in1=Bp, op=ALU.mult),
                                 reads=["qf", kB], writes=[("Qin", pi)])
                        t_lo, t_hi = a // 128, (a + n) // 128
                        for t0 in range(t_lo, t_hi, 4):
                            nt = min(4, t_hi - t0)
                            half = cp_ctr[0] % 2
                            cp_ctr[0] += 1
                            pk = ("pstb", half)
                            for ti in range(nt):
                                tl = t0 + ti
                                P.op("pe", lambda e, ti=ti, tl=tl, half=half: e.transpose(
                                    pstb[:, half * 512 + ti * 128:half * 512 + (ti + 1) * 128], kin[:, tl * 128:(tl + 1) * 128], ident_b[:, :]),
                                    reads=[kK, "ident_b"], writes=[pk], signal=(ti == nt - 1))
                            src = pstb[:, half * 512:half * 512 + nt * 128].rearrange("p (t n) -> p t n", n=128)
                            if half == 0:
                                P.op("dve", lambda e, t0=t0, nt=nt, src=src: e.tensor_copy(kin_tok[:, t0:t0 + nt, :], src),
                                     reads=[pk], writes=ctxkeys + [("ktok", t0 + q_) for q_ in range(nt)])
                            else:
                                P.op("act", lambda e, t0=t0, nt=nt, src=src: e.copy(kin_tok[:, t0:t0 + nt, :], src),
                                     reads=[pk], writes=ctxkeys + [("ktok", t0 + q_) for q_ in range(nt)])
                    if dbg == 77 and h == 0 and dr == 1:
                        P.fence()
                        P.dma(lambda e: e.dma_start(out=dbg_d[0], in_=B2), "dbg0", reads=[], writes=["dbg0"])
                        P.dma(lambda e: e.dma_start(out=dbg_d[1], in_=A), "dbg1", reads=[], writes=["dbg1"])
                        P.dma(lambda e: e.dma_start(out=dbg_d[2][:, 0:NCK], in_=tot), "dbg2", reads=[], writes=["dbg2"])
                        P.fence()
                        P.dead = True
                    order = list(range(NCK)) if dr == 0 else (list(range(CTXC - 1, -1, -1)) + list(range(NCK - 1, CTXC - 1, -1)))
                    mask = maskf if dr == 0 else maskb
                    P.op("pool", lambda e: e.memset(Sb[:, 0, :], 0.0), writes=[("S", 0)])
                    done_in_st = {}

                    def issue_sT(idx2):
                        ck2 = order[idx2]
                        if ck2 < CTXC:
                            return
                        a2 = ck2 * 64
                        tl2 = a2 - CTX
                        pb2 = (ck2 % 2) * 64
                        r = idx2 % 8
                        r2 = idx2 % 2
                        pi2 = piece_of(a2)
                        P.op("pe", lambda e: e.matmul(
                            pst[0][pb2:pb2 + 64, r * 64:(r + 1) * 64], lhsT=kin[:, a2:a2 + 64], rhs=Qin[:, tl2:tl2 + 64], start=True, stop=True),
                            reads=[("kin", pi2), ("Qin", pi2)], writes=[("sT", r)])
                        P.op("dve", lambda e, mask=mask: e.tensor_tensor(
                            out=PT[pb2:pb2 + 64, r2, :], in0=pst[0][pb2:pb2 + 64, r * 64:(r + 1) * 64], in1=mask[pb2:pb2 + 64, :], op=ALU.mult),
                            reads=[("sT", r), "maskf", "maskb"], writes=[("PT", r2)])
                    issue_sT(0)
                    for idx, ck in enumerate(order):
                        if deferred:
                            if next(deferred[0], "END") == "END":
                                deferred.pop(0)
                        if idx + 1 < len(order):
                            issue_sT(idx + 1)
                        a = ck * 64
                        tile = ck // 2
                        pb = (ck % 2) * 64
                        is_lat = ck >= CTXC
                        tl = a - CTX
                        sc, sn = idx % 2, (idx + 1) % 2
                        pS = pu[idx % 2]
                        if is_lat:
                            st = tl // 512
                            po = pg[st % 2]
                            r2 = idx % 2
                            o0 = tl % 512
                            P.op("pe", lambda e, pb=pb, r2=r2, tile=tile, po=po, o0=o0: e.matmul(
                                po[:, o0:o0 + 64], lhsT=vtok[pb:pb + 64, tile, :], rhs=PT[pb:pb + 64, r2, :], start=True, stop=False),
                                reads=["vtok", ("PT", r2)], writes=[("bank", id(po))], signal=False)
                        P.op("pe", lambda e, pb=pb, tile=tile, pS=pS: e.matmul(
                            pS[:, 0:128], lhsT=kin_tok[pb:pb + 64, tile, :], rhs=vtok[pb:pb + 64, tile, :], start=True, stop=False),
                            reads=[("ktok", tile), "vtok"], writes=[("bank", id(pS))], signal=False)
                        if is_lat:
                            P.op("pe", lambda e, sc=sc, tl=tl, po=po, o0=o0: e.matmul(
                                po[:, o0:o0 + 64], lhsT=Sb[:, sc, :], rhs=Qin[:, tl:tl + 64], start=False, stop=True),
                                reads=[("S", sc), ("Qin", piece_of(a))], writes=[("bank", id(po))])
                        P.op("pe", lambda e, sc=sc, pS=pS: e.matmul(pS[:, 0:128], lhsT=ident_b[:, :], rhs=Sb[:, sc, :], start=False, stop=True),
                             reads=[("S", sc), "ident_b"], writes=[("bank", id(pS))])
                        if dr == 0:
                            dck = ck
                        else:
                            dck = order[idx + 1] if idx + 1 < len(order) else ck
                        P.op("dve", lambda e, sn=sn, pS=pS, dck=dck: e.tensor_scalar_mul(Sb[:, sn, :], pS[:, 0:128], tot[:, dck:dck + 1]),
                             reads=[("bank", id(pS)), ("dt", piece_of(dck * 64))], writes=[("S", sn)])
                        if dbg == 78 and h == 0 and dr == 1 and idx == len(order) - 1:
                            P.op("act", lambda e, po=po: e.copy(ob[:, 0:512], po[:, 0:512]), reads=[("bank", id(po))], writes=["ob"])
                            P.fence()
                            P.dma(lambda e: e.dma_start(out=dbgb_d[0][:, 0:T], in_=ob), "dbg0", reads=[], writes=["dbg0"])
                            P.dma(lambda e: e.dma_start(out=dbgb_d[1][:, 0:T], in_=Qin), "dbg1", reads=[], writes=["dbg1"])
                            P.dma(lambda e: e.dma_start(out=dbgb_d[2], in_=kin), "dbg2", reads=[], writes=["dbg2"])
                            P.fence()
                            P.dead = True
                        if is_lat:
                            done_in_st[st] = done_in_st.get(st, 0) + 1
                            nst = min(512, T - st * 512)
                            if done_in_st[st] == nst // 64:
                                s0 = st * 512
                                if dr == 1:
                                    P.op("act", lambda e, po=po, s0=s0, nst=nst: e.copy(ob[:, s0:s0 + nst], po[:, 0:nst]),
                                         reads=[("bank", id(po))], writes=["ob"])
                                else:
                                    while deferred:
                                        if next(deferred[0], "END") == "END":
                                            deferred.pop(0)
                                    deferred.append(finalize_gen(po, s0, nst, st))
                    while deferred:
                        if next(deferred[0], "END") == "END":
                            deferred.pop(0)
            P.fence()
            zbuf = aview(0, (NCH, 1024), BF16)
            lnk = (l * 3 + k) * 8
            for stl in lat_subtiles(seq, 0, T):
                layer_norm_tile(stl["xv"], stl["keys"], stl["n"], lnk, lnk, 0, zbuf=zbuf)

        def cm_sublayer(seq):
            l, k = 1, 1
            hmc = aview(0, (NCH, 512), BF16)
            wvs2 = [aview(8192, (NCH, 512), BF16),
                    ctx_sb[:, :, :].rearrange("p a b -> p (a b)").bitcast(BF16)[:, 0:NCH * 512].rearrange("p (k n) -> p k n", k=NCH)]
            vt = aview(16384, (4, CMI), BF16)
            ub = aview(40960, (NFC, 512), BF16)
            wus = aview(65536, (2, NCH, 128), BF16)
            wos2 = [aview(69632, (NFC, 128), BF16),
                    stat[:, 2:5, :].rearrange("p a b -> p (a b)").bitcast(BF16)[:, 0:NFC * 128].rearrange("p (f n) -> p f n", f=NFC)]
            dctr = [0, 0]
            rt = aview(75776, (NFC, 128), F32)
            zbuf = aview(16384, (NCH, 1024), BF16)
            bst = cmst[:, 0:144].rearrange("p (c s) -> p c s", c=4)
            mv = cmst[:, 144:152].rearrange("p (c s) -> p c s", c=4)
            sd = cmst[:, 152:156]
            lnk = (l * 3 + k) * 8
            P.dma(lambda e: e.dma_start(out=rt.rearrange("p a b -> p (a b)"), in_=r_d[:, :]), "rt", reads=["r_d"], writes=["rt"])
            bctr = [0]

            def gelu_from(bank, out, okeys):
                bk = ("bank", id(bank))
                P.op("act", lambda e: e.activation(out=out, in_=bank[:, :], func=AF.Gelu_apprx_tanh), reads=[bk], writes=okeys)

            def cm_hmc(st):
                s0 = st * 512
                for c in range(NCH):
                    P.op("act", lambda e, c=c, s0=s0: e.activation(
                        out=hmc[:, c, :], in_=x_sb[:, c, s0:s0 + 512], func=AF.Identity,
                        bias=modT[l][:, 3 * 8 + c, seq:seq + 1], scale=modT[l][:, 4 * 8 + c, seq:seq + 1]),
                        reads=[("x", c, st), ("modT", l)], writes=["hmc"])

            wvs_issued = {}

            def issue_wvs(st_, cb_):
                if (st_, cb_) in wvs_issued:
                    return wvs_issued[(st_, cb_)]
                wsl_ = dctr[0] % 2
                dctr[0] += 1
                wv_ = wvs2[wsl_]
                P.dma(lambda e: e.dma_start(out=wv_, in_=wvb_d[cb_]), ("wvs", wsl_), reads=wkeys["wvb"],
                      writes=[("wvs", wsl_)] + ([("ctx", c_) for c_ in range(NCH)] if wsl_ == 1 else []))
                wvs_issued[(st_, cb_)] = wsl_
                return wsl_

            nsub = (T + 511) // 512
            cm_hmc(0)
            for st in range(nsub):
                s0 = st * 512
                for cb in range(6):
                    wsl = issue_wvs(st, cb)
                    wvs = wvs2[wsl]
                    for ch in range(4):
                        bank = pg[bctr[0] % 2]
                        bctr[0] += 1
                        for kc in range(NCH):
                            P.op("pe", lambda e, kc=kc, ch=ch, bank=bank, wvs=wvs: e.matmul(
                                bank[:, :], lhsT=hmc[:, kc, ch * 128:(ch + 1) * 128], rhs=wvs[:, kc, :], start=(kc == 0), stop=(kc == NCH - 1)),
                                reads=["hmc", ("wvs", wsl)], writes=[("bank", id(bank))], signal=(kc == NCH - 1))
                        gelu_from(bank, vt[:, ch, cb * 512:(cb + 1) * 512], [("vt", ch)])
                        P.op("dve", lambda e, ch=ch, cb=cb: e.bn_stats(bst[:, ch, cb * 6:(cb + 1) * 6], vt[:, ch, cb * 512:(cb + 1) * 512]),
                             reads=[("vt", ch)], writes=[("bst", ch)])
                for ch in range(4):
                    P.op("dve", lambda e, ch=ch: e.bn_aggr(mv[:, ch, :], bst[:, ch, :]), reads=[("bst", ch)], writes=[("mv", ch)])
                    P.op("act", lambda e, ch=ch: e.activation(out=sd[:, ch:ch + 1], in_=mv[:, ch, 1:2], func=AF.Sqrt, bias=epsc[:, 1:2], scale=1.0),
                         reads=[("mv", ch), "epsc"], writes=[("sd", ch)])
                    P.op("dve", lambda e, ch=ch: e.reciprocal(sd[:, ch:ch + 1], sd[:, ch:ch + 1]), reads=[("sd", ch)], writes=[("sd", ch)])
                    P.op("dve", lambda e, ch=ch: e.tensor_scalar(out=vt[:, ch, :], in0=vt[:, ch, :], scalar1=mv[:, ch, 0:1], scalar2=sd[:, ch:ch + 1],
                                                                op0=ALU.subtract, op1=ALU.mult),
                         reads=[("vt", ch), ("mv", ch), ("sd", ch)], writes=[("vt", ch)])
                for fc in range(NFC):
                    slot = fc % 4
                    P.dma(lambda e, fc=fc, slot=slot: e.dma_start(out=wus4[:, slot, :, :], in_=wub_d[fc]), ("wus", slot),
                          reads=wkeys["wub"], writes=[("wus", slot)])
                    bank = pu[fc % 2]
                    for kc in range(NCH):
                        P.op("pe", lambda e, kc=kc, slot=slot, bank=bank: e.matmul(
                            bank[:, :], lhsT=wus4[:, slot, kc, :], rhs=hmc[:, kc, :], start=(kc == 0), stop=(kc == NCH - 1)),
                            reads=["hmc", ("wus", slot)], writes=[("bank", id(bank))], signal=(kc == NCH - 1))
                    gelu_from(bank, ub[:, fc, :], [("ub", fc)])
                if st + 1 < nsub:
                    cm_hmc(st + 1)
                for fc in range(NFC):
                    bank = py[fc % 2]
                    for ch in range(4):
                        P.op("pe", lambda e, fc=fc, ch=ch, bank=bank: e.matmul(
                            bank[:, ch * 128:(ch + 1) * 128], lhsT=vt[:, ch, fc * 128:(fc + 1) * 128], rhs=wsT[:, fc // 3, :], start=True, stop=True),
                            reads=[("vt", ch), "wsT"], writes=[("bank", id(bank))], signal=(ch == 3))
                    tmp = tb[:, fc % 2, :]
                    P.op("dve", lambda e, fc=fc, bank=bank, tmp=tmp: e.scalar_tensor_tensor(
                        out=tmp.rearrange("p (c t) -> p c t", c=4), in0=bank[:, :].rearrange("p (c t) -> p c t", c=4), scalar=vgT[:, fc:fc + 1],
                        in1=rt[:, fc, :].unsqueeze(1).to_broadcast([128, 4, 128]), op0=ALU.mult, op1=ALU.add),
                        reads=[("bank", id(bank)), "vgT", "rt"], writes=[("tb", fc % 2)])
                    P.op("pool" if fc % 2 == 0 else "dve", lambda e, fc=fc, tmp=tmp: e.tensor_tensor(out=ub[:, fc, :], in0=tmp, in1=ub[:, fc, :], op=ALU.mult),
                         reads=[("tb", fc % 2), ("ub", fc)], writes=[("ub", fc)])
                if st + 1 < nsub:
                    issue_wvs(st + 1, 0)
                    issue_wvs(st + 1, 1)
                for c in range(NCH):
                    osl = dctr[1] % 2
                    dctr[1] += 1
                    wos = wos2[osl]
                    P.dma(lambda e, c=c, wos=wos: e.dma_start(out=wos, in_=wcob_d[c]), ("wos", osl), reads=wkeys["wcob"],
                          writes=[("wos", osl)])
                    bank = pg[c % 2]
                    for fc in range(NFC):
                        P.op("pe", lambda e, fc=fc, bank=bank, wos=wos: e.matmul(bank[:, :], lhsT=wos[:, fc, :], rhs=ub[:, fc, :], start=(fc == 0), stop=(fc == NFC - 1)),
                             reads=[("wos", osl), ("ub", fc)], writes=[("bank", id(bank))], signal=(fc == NFC - 1))
                    P.op("dve", lambda e, c=c, bank=bank, s0=s0: e.scalar_tensor_tensor(
                        out=x_sb[:, c, s0:s0 + 512], in0=bank[:, :], scalar=modT[l][:, 5 * 8 + c, seq:seq + 1],
                        in1=x_sb[:, c, s0:s0 + 512], op0=ALU.mult, op1=ALU.add),
                        reads=[("bank", id(bank)), ("x", c, st), ("modT", l)], writes=[("x", c, st)])
                stl = lat_subtiles(seq, s0, s0 + 512)[0]
                layer_norm_tile(stl["xv"], stl["keys"], 512, lnk, lnk, 0)

        def load_tokens(src_rows_ap, ntok, dstv, keys_fn):
            for i, t0 in enumerate(range(0, ntok, 128)):
                sl = i % 4
                P.dma(lambda e, t0=t0, sl=sl: e.dma_start(out=iost[:, sl, :], in_=src_rows_ap[t0:t0 + 128, :]),
                      ("iost", sl), writes=[("iost", sl)])
                for h in range(2):
                    b = (2 * i + h) % 2
                    for cc in range(4):
                        c = h * 4 + cc
                        P.op("pe", lambda e, c=c, cc=cc, sl=sl, b=b: e.transpose(
                            pg[b][:, cc * 128:(cc + 1) * 128], iost[:, sl, c * 128:(c + 1) * 128], ident_f[:, :]),
                            reads=[("iost", sl), "ident_f"], writes=[("pg", b)], signal=(cc == 3))
                    P.op("dve" if h == 0 else "act",
                         (lambda e, h=h, b=b, t0=t0: e.tensor_copy(dstv(h, t0), pg[b][:, :].rearrange("p (c n) -> p c n", n=128)))
                         if h == 0 else
                         (lambda e, h=h, b=b, t0=t0: e.copy(dstv(h, t0), pg[b][:, :].rearrange("p (c n) -> p c n", n=128))),
                         reads=[("pg", b)], writes=keys_fn(h, t0))

        def store_tokens(dst_rows_ap, ntok):
            for i, t0 in enumerate(range(0, ntok, 128)):
                sl = i % 4
                for h in range(2):
                    b = (2 * i + h) % 2
                    for cc in range(4):
                        c = h * 4 + cc
                        P.op("pe", lambda e, c=c, cc=cc, b=b, t0=t0: e.transpose(
                            pu[b][:, cc * 128:(cc + 1) * 128], x_sb[:, c, t0:t0 + 128], ident_f[:, :]),
                            reads=[("x", c, t0 // 512), "ident_f"], writes=[("pu", b)], signal=(cc == 3))
                    if h == 0:
                        P.op("dve", lambda e, sl=sl, b=b: e.tensor_copy(iost[:, sl, 0:512], pu[b][:, :]),
                             reads=[("pu", b)], writes=[("iost", sl)])
                    else:
                        P.op("act", lambda e, sl=sl, b=b: e.copy(iost[:, sl, 512:1024], pu[b][:, :]),
                             reads=[("pu", b)], writes=[("iost", sl)])
                P.dma(lambda e, t0=t0, sl=sl: e.dma_start(out=dst_rows_ap[t0:t0 + 128, :], in_=iost[:, sl, :]),
                      ("iost", sl), reads=[("iost", sl)], writes=[("outd", t0)])


        if dbg == 3:
            P.finish(); P.emit(); return nc
        conv_rest()
        for seq in range(NSEQ):
            P.scope = f"load{seq}"
            load_tokens(x_d[seq], T,
                        lambda h, t0: x_sb[:, h * 4:(h + 1) * 4, t0:t0 + 128],
                        lambda h, t0: [("x", h * 4 + cc, t0 // 512) for cc in range(4)])
            load_tokens(ctx_d[seq], CTX,
                        lambda h, t0: ctx_sb[:, h * 4:(h + 1) * 4, t0:t0 + 128],
                        lambda h, t0: [("ctx", h * 4 + cc) for cc in range(4)])
            P.fence()
            P.scope = f"ffn00_{seq}"
            if stages >= 1 and dbg != 4:
                ffn_sublayer(0, 0, seq, True)
            flush_ln()
            while bg_jobs:
                run_bg_job()
            if stages >= 2:
                P.fence()
                P.scope = f"hgrn_{seq}"
                hgrn_sublayer(seq)
                P.fence()
            P.scope = f"ffn01_{seq}"
            if stages >= 3:
                ffn_sublayer(0, 1, seq, False, nxt=(1, 0) if stages >= 4 else None)
            P.scope = f"ffn10_{seq}"
            if stages >= 4:
                ffn_sublayer(1, 0, seq, False)
            if stages >= 5:
                flush_ln()
                P.fence()
                P.scope = f"cm_{seq}"
                cm_sublayer(seq)
                P.fence()
            P.scope = f"ffn11_{seq}"
            if stages >= 6:
                ffn_sublayer(1, 1, seq, False)
            flush_ln()
            P.fence()
            P.scope = f"store{seq}"
            store_tokens(out_d[seq], T)
            P.fence()
        P.finish()
        P.emit()
    return nc


def make_in_maps(inputs, n_cores, NSEQ):
    ident = np.eye(128, dtype=np.float32)
    TT = inputs["x"].shape[1] + inputs["ctx"].shape[1]
    si = np.arange(128)[:, None] % 64
    ti = np.arange(64)[None, :]
    maskf = (si <= ti).astype(np.float32)
    maskb = (si >= ti).astype(np.float32)
    rmask = np.broadcast_to((np.arange(TT) % 64 != 0).astype(np.float32)[None, :], (128, TT)).copy()
    maps = []
    for i in range(n_cores):
        b0 = i * NSEQ
        cc = np.concatenate([inputs["c"][b0:b0 + NSEQ], inputs["c_ctx"][None, :]], axis=0)
        maps.append({
            "x": np.ascontiguousarray(inputs["x"][b0:b0 + NSEQ]),
            "ctx": np.ascontiguousarray(inputs["ctx"][b0:b0 + NSEQ]),
            "cc": np.ascontiguousarray(cc.reshape((NSEQ + 1) * NCH, 128)),
            "mod_w": inputs["mod_w"],
            "mod_b": np.ascontiguousarray(inputs["mod_b"].reshape(DEPTH * 72, 128)),
            "ln_g": np.ascontiguousarray(inputs["ln_g"].reshape(48, 128)),
            "ln_b": np.ascontiguousarray(inputs["ln_b"].reshape(48, 128)),
            "ffn_w_in": inputs["ffn_w_in"],
            "ffn_w_out": inputs["ffn_w_out"],
            "ident": ident,
            "hg_w_in": inputs["hg_w_in"][0],
            "hg_w_out": inputs["hg_w_out"][0],
            "hg_lb": np.ascontiguousarray(inputs["hg_lower_bounds"].reshape(48, 128)),
            "hg_nw": np.ascontiguousarray(inputs["hg_norm_w"][0].reshape(128, 1)),
            "cm_w_in": inputs["cm_w_in"][0],
            "cm_w_out": inputs["cm_w_out"][0],
            "cm_vg": np.ascontiguousarray(inputs["cm_v_g"][0].reshape(NFC, 128)),
            "cm_vb": np.ascontiguousarray(inputs["cm_v_b"][0].reshape(NFC, 128)),
            "cm_ws": inputs["cm_w_s"][0],
            "cm_bs": inputs["cm_b_s"][0],
            "maskf": maskf, "maskb": maskb, "rmask": rmask,
        })
    return maps


def kernel(**inputs):
    inputs = {k: np.asarray(v) for k, v in inputs.items()}
    n_cores = 8
    B, T, _ = inputs["x"].shape
    CTX = inputs["ctx"].shape[1]
    NSEQ = B // n_cores
    nc = build_nc(NSEQ, T, CTX, stages=6)
    in_maps = make_in_maps(inputs, n_cores, NSEQ)
    res = run_bass_kernel_spmd(nc, in_maps, core_ids=list(range(n_cores)))
    return np.concatenate([r["out"] for r in res.results], axis=0)
```
